# Optimizing a Trainium2 kernel written in Bass

```python
import math
import jax, jax.numpy as jnp
from jax import lax
import numpy as np

D_MODEL = 4096
BATCH = 4
SEQ = 2048
DEPTH = 2

N_MIXERS = 2
N_A_LAYERS = (DEPTH + 1) // 2
N_B_LAYERS = DEPTH // 2
HEAD_DIM = 128
ATT_WIDTH = D_MODEL
N_HEADS = ATT_WIDTH // HEAD_DIM
Q_BLOCK = 128
SSM_WIDTH = D_MODEL
GROUP = 16
N_GROUPS = SSM_WIDTH // GROUP
STATE = 64
SCAN_CHUNK = 128
RMS_EPS = 1e-6
DT_MIN = 1e-3
DT_MAX = 1e-1

kernel_name = "hybrid_stickbreak_s5_decoder"


def _rmsnorm(x, g):
    xf = x.astype(jnp.float32)
    r = lax.rsqrt(jnp.mean(xf * xf, axis=-1, keepdims=True) + RMS_EPS)
    return (xf * r).astype(x.dtype) * g


def _stick_breaking_attention(q, k, v):
    S = q.shape[2]
    scale = 1.0 / math.sqrt(q.shape[-1])
    outs = []
    for blk in range(S // Q_BLOCK):
        q0 = blk * Q_BLOCK
        kl = q0 + Q_BLOCK
        qb = q[:, :, q0:kl]
        kb = k[:, :, :kl]
        vb = v[:, :, :kl]
        z = jnp.einsum('bhqd,bhkd->bhqk', qb, kb).astype(jnp.float32) * scale
        qpos = q0 + jnp.arange(Q_BLOCK)[:, None]
        kpos = jnp.arange(kl)[None, :]
        mask = kpos < qpos
        log_1m = jnp.where(mask, jax.nn.log_sigmoid(-z), 0.0)
        suffix = lax.cumsum(log_1m, axis=3, reverse=True) - log_1m
        w = jnp.where(mask, jnp.exp(jax.nn.log_sigmoid(z) + suffix), 0.0)
        outs.append(jnp.einsum('bhqk,bhkd->bhqd', w.astype(vb.dtype), vb))
    return jnp.concatenate(outs, axis=2)


def _attention_layer(x, norm_g, w_in, q_g, k_g, w_out):
    B, S, _ = x.shape
    h = _rmsnorm(x, norm_g)
    proj = h @ w_in
    q, k, v, gate = jnp.split(proj, 4, axis=-1)
    heads = lambda t: t.reshape(B, S, N_HEADS, HEAD_DIM)
    q = _rmsnorm(heads(q), q_g).transpose(0, 2, 1, 3)
    k = _rmsnorm(heads(k), k_g).transpose(0, 2, 1, 3)
    v = heads(v).transpose(0, 2, 1, 3)
    o = _stick_breaking_attention(q, k, v)
    o = o.transpose(0, 2, 1, 3).reshape(B, S, ATT_WIDTH).astype(x.dtype)
    return x + (o * jax.nn.silu(gate)) @ w_out


def _complex_affine_combine(e1, e2):
    a1r, a1i, b1r, b1i = e1
    a2r, a2i, b2r, b2i = e2
    ar = a2r * a1r - a2i * a1i
    ai = a2r * a1i + a2i * a1r
    br = a2r * b1r - a2i * b1i + b2r
    bi = a2r * b1i + a2i * b1r + b2i
    return ar, ai, br, bi


def _s5_scan(u, A_re, A_im, log_dt, B_re, B_im, C_re, C_im, D):
    Bsz, S, _ = u.shape
    f32 = jnp.float32
    A_re, A_im = A_re.astype(f32), A_im.astype(f32)
    B_re, B_im = B_re.astype(f32), B_im.astype(f32)
    C_re, C_im = C_re.astype(f32), C_im.astype(f32)
    dt = jnp.exp(log_dt.astype(f32))[:, None]
    mag = jnp.exp(A_re * dt)
    Ab_re = mag * jnp.cos(A_im * dt)
    Ab_im = mag * jnp.sin(A_im * dt)
    den = A_re * A_re + A_im * A_im
    f_re = ((Ab_re - 1.0) * A_re + Ab_im * A_im) / den
    f_im = (Ab_im * A_re - (Ab_re - 1.0) * A_im) / den
    Bb_re = f_re[..., None] * B_re - f_im[..., None] * B_im
    Bb_im = f_re[..., None] * B_im + f_im[..., None] * B_re
    Dg = D.astype(f32).reshape(N_GROUPS, GROUP)

    nc = S // SCAN_CHUNK
    uc = u.astype(f32).reshape(Bsz, nc, SCAN_CHUNK, N_GROUPS, GROUP).transpose(1, 2, 0, 3, 4)
    a_re_b = jnp.broadcast_to(Ab_re, (SCAN_CHUNK, Bsz, N_GROUPS, STATE))
    a_im_b = jnp.broadcast_to(Ab_im, (SCAN_CHUNK, Bsz, N_GROUPS, STATE))

    def chunk_step(carry, u_c):
        hp_re, hp_im = carry
        bu_re = jnp.einsum('cbgi,gpi->cbgp', u_c, Bb_re)
        bu_im = jnp.einsum('cbgi,gpi->cbgp', u_c, Bb_im)
        ac_re, ac_im, h_re, h_im = lax.associative_scan(
            _complex_affine_combine, (a_re_b, a_im_b, bu_re, bu_im), axis=0)
        h_re, h_im = (h_re + ac_re * hp_re[None] - ac_im * hp_im[None],
                      h_im + ac_re * hp_im[None] + ac_im * hp_re[None])
        y = (jnp.einsum('cbgp,gip->cbgi', h_re, C_re)
             - jnp.einsum('cbgp,gip->cbgi', h_im, C_im)
             + Dg * u_c)
        return (h_re[-1], h_im[-1]), y

    h0 = jnp.zeros((Bsz, N_GROUPS, STATE), f32)
    _, ys = lax.scan(chunk_step, (h0, h0), uc)
    return ys.transpose(2, 0, 1, 3, 4).reshape(Bsz, S, SSM_WIDTH)


def _ssm_layer(x, norm_g, w_in, A_re, A_im, log_dt, B_re, B_im, C_re, C_im, D, glu_w, glu_b, w_out):
    h = _rmsnorm(x, norm_g)
    proj = h @ w_in
    u, gate = jnp.split(proj, 2, axis=-1)
    y = _s5_scan(u, A_re, A_im, log_dt, B_re, B_im, C_re, C_im, D).astype(x.dtype)
    y = jax.nn.gelu(y)
    y = y * jax.nn.sigmoid(y @ glu_w + glu_b)
    return x + (y * jax.nn.silu(gate)) @ w_out


def setup_inputs(seed: int = 0) -> dict:
    key = jax.random.key(seed)
    ks = jax.random.split(key, 20)
    n = jax.random.normal
    f32 = jnp.float32
    x = n(ks[0], (BATCH, SEQ, D_MODEL), f32)
    norm_g = 1.0 + 0.02 * n(ks[1], (DEPTH, D_MODEL), f32)
    attn_w_in = n(ks[2], (N_A_LAYERS, D_MODEL, 4 * ATT_WIDTH), f32) * D_MODEL ** -0.5
    attn_q_g = 1.0 + 0.02 * n(ks[3], (N_A_LAYERS, HEAD_DIM), f32)
    attn_k_g = 1.0 + 0.02 * n(ks[4], (N_A_LAYERS, HEAD_DIM), f32)
    attn_w_out = n(ks[5], (N_A_LAYERS, ATT_WIDTH, D_MODEL), f32) * ATT_WIDTH ** -0.5
    ssm_w_in = n(ks[6], (N_B_LAYERS, D_MODEL, 2 * SSM_WIDTH), f32) * D_MODEL ** -0.5
    ssm_A_re = -0.5 + 0.01 * n(ks[7], (N_B_LAYERS, N_GROUPS, STATE), f32)
    ssm_A_im = (math.pi * jnp.arange(STATE, dtype=f32)
                + 0.01 * n(ks[8], (N_B_LAYERS, N_GROUPS, STATE), f32))
    ssm_log_dt = jax.random.uniform(ks[9], (N_B_LAYERS, N_GROUPS), f32,
                                    math.log(DT_MIN), math.log(DT_MAX))
    ssm_B_re = n(ks[10], (N_B_LAYERS, N_GROUPS, STATE, GROUP), f32) * (2 * GROUP) ** -0.5
    ssm_B_im = n(ks[11], (N_B_LAYERS, N_GROUPS, STATE, GROUP), f32) * (2 * GROUP) ** -0.5
    ssm_C_re = n(ks[12], (N_B_LAYERS, N_GROUPS, GROUP, STATE), f32) * STATE ** -0.5
    ssm_C_im = n(ks[13], (N_B_LAYERS, N_GROUPS, GROUP, STATE), f32) * STATE ** -0.5
    ssm_D = n(ks[14], (N_B_LAYERS, SSM_WIDTH), f32)
    ssm_glu_w = n(ks[15], (N_B_LAYERS, SSM_WIDTH, SSM_WIDTH), f32) * SSM_WIDTH ** -0.5
    ssm_glu_b = 0.01 * n(ks[16], (N_B_LAYERS, SSM_WIDTH), f32)
    ssm_w_out = n(ks[17], (N_B_LAYERS, SSM_WIDTH, D_MODEL), f32) * SSM_WIDTH ** -0.5
    return {"x": x, "norm_g": norm_g, "attn_w_in": attn_w_in, "attn_q_g": attn_q_g,
            "attn_k_g": attn_k_g, "attn_w_out": attn_w_out, "ssm_w_in": ssm_w_in,
            "ssm_A_re": ssm_A_re, "ssm_A_im": ssm_A_im, "ssm_log_dt": ssm_log_dt,
            "ssm_B_re": ssm_B_re, "ssm_B_im": ssm_B_im, "ssm_C_re": ssm_C_re,
            "ssm_C_im": ssm_C_im, "ssm_D": ssm_D, "ssm_glu_w": ssm_glu_w,
            "ssm_glu_b": ssm_glu_b, "ssm_w_out": ssm_w_out}


def reference(x, norm_g, attn_w_in, attn_q_g, attn_k_g, attn_w_out, ssm_w_in, ssm_A_re, ssm_A_im,
              ssm_log_dt, ssm_B_re, ssm_B_im, ssm_C_re, ssm_C_im, ssm_D, ssm_glu_w, ssm_glu_b,
              ssm_w_out):
    for i in range(DEPTH):
        j = i // N_MIXERS
        if i % N_MIXERS == 0:
            x = _attention_layer(x, norm_g[i], attn_w_in[j], attn_q_g[j], attn_k_g[j], attn_w_out[j])
        else:
            x = _ssm_layer(x, norm_g[i], ssm_w_in[j], ssm_A_re[j], ssm_A_im[j], ssm_log_dt[j],
                           ssm_B_re[j], ssm_B_im[j], ssm_C_re[j], ssm_C_im[j], ssm_D[j],
                           ssm_glu_w[j], ssm_glu_b[j], ssm_w_out[j])
    return x
```

```python
import numpy as np
import os
from contextlib import ExitStack
import ml_dtypes
import concourse.bass as bass
import concourse.mybir as mybir
from concourse.bass_utils import run_bass_kernel_spmd

F32 = mybir.dt.float32
BF16 = mybir.dt.bfloat16
ALU = mybir.AluOpType
AF = mybir.ActivationFunctionType
NPBF = ml_dtypes.bfloat16

D = 4096
NTOK = 1024
KC = 32
FW = 256
EPS = 1e-6
MAGIC = 12582912.0
TWO_PI = 6.283185307179586


class K:
    def __init__(self, nc):
        self.nc = nc
        self.eng = {'pe': nc.tensor, 'act': nc.scalar, 'dve': nc.vector, 'pool': nc.gpsimd, 'sp': nc.sync}
        self.prog = {}
        self.waited = {}
        self.lastw = {}
        self.readers = {}
        self.dsem = {}
        self.nsem = 0
        self.out_tokens = []
        self.old_tokens = []

    def _newsem(self, tag):
        self.nsem += 1
        return self.nc.alloc_semaphore(f"s_{tag}_{self.nsem}")

    def _wait(self, e, tok):
        sem, val, owner, sid = tok
        key = (e, sid)
        if self.waited.get(key, 0) >= val:
            return
        self.eng[e].wait_ge(sem, val)
        self.waited[key] = val

    def _deps(self, e, reads, writes):
        toks = []
        for b in list(reads) + list(writes):
            for t in self.lastw.get(b, {}).values():
                toks.append((t, True))
        for b in writes:
            for t in self.readers.get(b, []):
                toks.append((t, False))
        for t, is_w in toks:
            if t[2] == e:
                if e in ('pe', 'sp'):
                    continue
            self._wait(e, t)

    def _mark(self, e, ins):
        p = self.prog.get(e)
        if p is None or p[1] >= 6000:
            if p is not None:
                self.old_tokens.append((p[0], p[1], e, p[2]))
            p = [self._newsem(e), 0, self.nsem]
            self.prog[e] = p
        p[1] += 1
        ins.then_inc(p[0], 1)
        return (p[0], p[1], e, p[2])

    def _record(self, tok, reads, writes):
        for b in writes:
            self.lastw.setdefault(b, {})[tok[2] if tok[2] != 'dma' else ('dma', tok[3])] = tok
            self.readers[b] = []
        for b in reads:
            self.readers.setdefault(b, []).append(tok)

    def op(self, e, reads, writes, fn):
        self._deps(e, reads, writes)
        ins = fn(self.eng[e])
        tok = self._mark(e, ins)
        self._record(tok, reads, writes)
        return tok

    def pe(self, reads, writes, fns):
        self._deps('pe', reads, writes)
        ins = None
        for f in fns:
            ins = f(self.nc.tensor)
        tok = self._mark('pe', ins)
        self._record(tok, reads, writes)
        return tok

    def dma(self, q, reads, writes, out, in_, key, final=False):
        self._deps(q, reads, writes)
        d = self.dsem.get(key)
        if d is None or d[1] >= 1500:
            if d is not None:
                self.old_tokens.append((d[0], 16 * d[1], 'dma', d[2]))
            d = [self._newsem('d'), 0, self.nsem]
            self.dsem[key] = d
        ins = self.eng[q].dma_start(out=out, in_=in_)
        d[1] += 1
        ins.then_inc(d[0], 16)
        tok = (d[0], 16 * d[1], 'dma', d[2])
        self._record(tok, reads, writes)
        if final:
            self.out_tokens.append(tok)
        return tok

    def collective(self, kind, groups, in_ap, out_ap, in_key, out_key, scratch):
        self._deps('pool', [in_key], [out_key])
        if not hasattr(self, 'ccsem'):
            self.ccsem = [self._newsem('cc'), 0]
        ins = self.nc.gpsimd.collective_compute(kind, ALU.bypass, replica_groups=groups,
                                                ins=[in_ap.opt()], outs=[out_ap.opt()])
        self.ccsem[1] += 1
        ins.then_inc(self.ccsem[0])
        self.nc.gpsimd.wait_ge(self.ccsem[0], self.ccsem[1])
        return self.op('pool', [in_key], [out_key], lambda e: e.memset(scratch, 0.0))

    def barrier(self):
        toks = []
        for e, p in self.prog.items():
            toks.append((p[0], p[1], e, p[2]))
        for key, d in self.dsem.items():
            toks.append((d[0], 16 * d[1], 'dma', d[2]))
        toks += self.old_tokens
        for e in self.eng:
            for t in toks:
                if t[2] == e:
                    continue
                self._wait(e, t)

    def finish(self, keys=()):
        for tok in self.out_tokens:
            self._wait('sp', tok)
        for key in keys:
            for t in self.lastw.get(key, {}).values():
                self._wait('sp', t)


def sb(nc, es, name, shape, dt):
    return es.enter_context(nc.sbuf_tensor(name, shape, dt))


def ps(nc, es, name, shape, dt=F32):
    return es.enter_context(nc.psum_tensor(name, shape, dt))


def load_const(nc, k, es, name, src, shape, dt):
    t = sb(nc, es, name, shape, dt)
    k.dma('sp', [], [name], out=t[:], in_=src, key=name)
    return t


def dense_phase(nc, k, tag, *, W, units, act_src, gk=None, consts=None, extra_alloc=None):
    with ExitStack() as es:
        actT = sb(nc, es, f"{tag}_actT", [128, KC, NTOK], BF16)
        wst = [sb(nc, es, f"{tag}_wst{i}", [128, KC, FW], F32) for i in range(2)]
        wbf = [sb(nc, es, f"{tag}_wbf{i}", [128, KC, FW], BF16) for i in range(2)]
        psum = [ps(nc, es, f"{tag}_ps{i}", [128, 2048]) for i in range(2)]
        st = extra_alloc(nc, es) if extra_alloc else {}
        st.update(actT=actT, psum=psum, tag=tag)
        Wv = W.rearrange("(kc p) n -> p kc n", p=128)
        n_units = len(units)

        def load_w(u):
            s = u % 2
            c0 = units[u]['col0']
            for j in range(4):
                k.dma('sp', [], [f"{tag}_wst{s}"],
                      out=wst[s][:, j * 8:(j + 1) * 8, :], in_=Wv[:, j * 8:(j + 1) * 8, c0:c0 + FW],
                      key=f"{tag}_wst{s}")

        def conv_w(u):
            s = u % 2
            for j in range(4):
                sl = slice(j * 8, (j + 1) * 8)
                eng = ('pool', 'pool', 'dve', 'act')[j]
                if gk is not None:
                    if eng == 'act':
                        for kc in range(j * 8, (j + 1) * 8):
                            k.op('act', [f"{tag}_wst{s}", 'c_gk'], [f"{tag}_wbf{s}"],
                                 lambda e: e.activation(out=wbf[s][:, kc, :], in_=wst[s][:, kc, :], func=AF.Copy,
                                                        scale=gk[:, kc:kc + 1]))
                    else:
                        k.op(eng, [f"{tag}_wst{s}", 'c_gk'], [f"{tag}_wbf{s}"],
                             lambda e: e.tensor_tensor(out=wbf[s][:, sl, :], in0=wst[s][:, sl, :],
                                                       in1=gk[:, sl].unsqueeze(2).to_broadcast([128, 8, FW]),
                                                       op=ALU.mult))
                elif eng == 'act':
                    k.op('act', [f"{tag}_wst{s}"], [f"{tag}_wbf{s}"],
                         lambda e: e.activation(out=wbf[s][:, sl, :], in_=wst[s][:, sl, :], func=AF.Copy))
                else:
                    k.op(eng, [f"{tag}_wst{s}"], [f"{tag}_wbf{s}"],
                         lambda e: e.tensor_copy(out=wbf[s][:, sl, :], in_=wst[s][:, sl, :]))

        def mm(u):
            s = u % 2
            fns = []
            if units[u]['orient'] == 'F':
                for ft in range(2):
                    for hf in range(2):
                        bank = ft * 2 + hf
                        for kc in range(KC):
                            fns.append(lambda e, ft=ft, hf=hf, kc=kc, bank=bank: e.matmul(
                                psum[s][:, bank * 512:(bank + 1) * 512],
                                lhsT=wbf[s][:, kc, ft * 128:(ft + 1) * 128],
                                rhs=actT[:, kc, hf * 512:(hf + 1) * 512],
                                start=(kc == 0), stop=(kc == KC - 1)))
            else:
                for tt in range(8):
                    for kc in range(KC):
                        fns.append(lambda e, tt=tt, kc=kc: e.matmul(
                            psum[s][:, tt * FW:(tt + 1) * FW],
                            lhsT=actT[:, kc, tt * 128:(tt + 1) * 128],
                            rhs=wbf[s][:, kc, :],
                            start=(kc == 0), stop=(kc == KC - 1)))
            k.pe([f"{tag}_wbf{s}", f"{tag}_actT"], [f"{tag}_ps{s}"], fns)

        load_w(0)
        if act_src[0] == 'load':
            aT = act_src[1].rearrange("(kc p) t -> p kc t", p=128)
            for j in range(4):
                k.dma('sp', list(act_src[2]), [f"{tag}_actT"],
                      out=actT[:, j * 8:(j + 1) * 8, :], in_=aT[:, j * 8:(j + 1) * 8, :],
                      key=f"{tag}_actT")
        elif act_src[0] == 'load2':
            cands, keys, sel = act_src[1], list(act_src[2]), act_src[3]
            tmpb = st['tmpb']
            for j in range(4):
                sl = slice(j * 8, (j + 1) * 8)
                k.dma('sp', keys, [f"{tag}_actT"], out=actT[:, sl, :], in_=cands[0](j), key=f"{tag}_actT")
                k.dma('sp', keys, [f"{tag}_tmpb"], out=tmpb[:], in_=cands[1](j), key=f"{tag}_tmpb")
                k.op('pool', [f"{tag}_actT", 'c_sel'], [f"{tag}_actT"],
                     lambda e: e.tensor_scalar(out=actT[:, sl, :], in0=actT[:, sl, :], scalar1=sel[:, 0:1], scalar2=None,
                                               op0=ALU.mult))
                k.op('dve', [f"{tag}_actT", f"{tag}_tmpb", 'c_sel'], [f"{tag}_actT"],
                     lambda e: e.scalar_tensor_tensor(out=actT[:, sl, :], in0=tmpb[:], scalar=sel[:, 1:2],
                                                      in1=actT[:, sl, :], op0=ALU.mult, op1=ALU.add))
        else:
            x_ap = act_src[1]
            xkeys = list(act_src[2])
            ident = consts['ident']
            epst = consts['eps']
            xin = [wst[1][:, 0:16, :], wst[1][:, 16:32, :]]
            hn = sb(nc, es, f"{tag}_hn", [128, D], BF16)
            ss = sb(nc, es, f"{tag}_ss", [128, 8], F32)
            rstd = sb(nc, es, f"{tag}_rstd", [128, 8], F32)
            k.op('dve', [], [f"{tag}_ss{t}" for t in range(8)], lambda e: e.memset(ss[:], 0.0))
            for tt in range(8):
                j = tt % 2
                xv = xin[j].rearrange("p a b -> p (a b)")
                k.dma('sp', xkeys, [f"{tag}_xin{j}"], out=xv, in_=x_ap[tt * 128:(tt + 1) * 128, :],
                      key=f"{tag}_xin{j}")
                k.op('act', [f"{tag}_xin{j}"], [f"{tag}_hn", f"{tag}_ss{tt}"],
                     lambda e: e.activation(out=hn[:], in_=xv, func=AF.Square,
                                            accum_out=ss[:, tt:tt + 1]))
                k.op('act', [f"{tag}_ss{tt}", 'c_eps'], [f"{tag}_ss{tt}"],
                     lambda e: e.activation(out=ss[:, tt:tt + 1], in_=ss[:, tt:tt + 1], func=AF.Sqrt,
                                            bias=epst[:, 0:1], scale=1.0 / D))
                k.op('dve', [f"{tag}_ss{tt}"], [f"{tag}_rstd{tt}"],
                     lambda e: e.reciprocal(out=rstd[:, tt:tt + 1], in_=ss[:, tt:tt + 1]))
                k.op('act', [f"{tag}_xin{j}", f"{tag}_rstd{tt}"], [f"{tag}_hn"],
                     lambda e: e.activation(out=hn[:], in_=xv, func=AF.Copy,
                                            scale=rstd[:, tt:tt + 1]))
                for g in range(4):
                    s = g % 2
                    fns = []
                    for i in range(8):
                        kc = g * 8 + i
                        fns.append(lambda e, i=i, kc=kc, s=s: e.matmul(
                            psum[s][:, i * 128:(i + 1) * 128],
                            lhsT=hn[:, kc * 128:(kc + 1) * 128], rhs=ident[:],
                            start=True, stop=True))
                    k.pe([f"{tag}_hn", 'c_ident'], [f"{tag}_ps{s}"], fns)
                    src = psum[s][:, 0:1024].rearrange("p (a b) -> p a b", a=8)
                    dst = actT[:, g * 8:(g + 1) * 8, tt * 128:(tt + 1) * 128]
                    if g % 2 == 0:
                        k.op('dve', [f"{tag}_ps{s}"], [f"{tag}_actT"],
                             lambda e: e.tensor_copy(out=dst, in_=src))
                    else:
                        k.op('act', [f"{tag}_ps{s}"], [f"{tag}_actT"],
                             lambda e: e.activation(out=dst, in_=src, func=AF.Copy))
            k.readers.setdefault(f"{tag}_wst1", [])
            for j in range(2):
                k.readers[f"{tag}_wst1"] += k.readers.get(f"{tag}_xin{j}", []) + list(k.lastw.get(f"{tag}_xin{j}", {}).values())

        conv_w(0)
        if n_units > 1:
            load_w(1)
        prefetch(nc, k, st, 0, units[0])
        for u in range(n_units):
            if u + 1 < n_units:
                prefetch(nc, k, st, u + 1, units[u + 1])
            mm(u)
            if u >= 1 and units[u - 1]['kind'] == 'qkF':
                evac_unit(nc, k, st, u - 1, units[u - 1], part='late')
            if u + 1 < n_units:
                conv_w(u + 1)
            if u + 2 < n_units:
                load_w(u + 2)
            evac_unit(nc, k, st, u, units[u], part='early' if units[u]['kind'] == 'qkF' else 'all')
        if units[-1]['kind'] == 'qkF':
            evac_unit(nc, k, st, n_units - 1, units[-1], part='late')


def prefetch(nc, k, st, u, unit):
    tag = st['tag']
    s = u % 2
    kind = unit['kind']
    if kind == 'residT':
        c0 = unit['col0']
        src = unit['res'][:, c0:c0 + FW].rearrange("(tt p) c -> p tt c", p=128)
        k.dma('sp', list(unit.get('res_keys', [])), [f"{tag}_outR{s}"], out=st['outR'][s][:], in_=src,
              key=f"{tag}_xres{s}")
    elif kind == 'gluF':
        c0 = unit['col0']
        src = unit['sg'][c0:c0 + FW, :].rearrange("(ft p) t -> p ft t", p=128)
        k.dma('sp', list(unit.get('sg_keys', [])), [f"{tag}_sgs{s}"], out=st['sgs'][s][:], in_=src,
              key=f"{tag}_sgs{s}")


def evac_unit(nc, k, st, u, unit, part='all'):
    tag = st['tag']
    s = u % 2
    kind = unit['kind']
    P = st['psum'][s]
    pk = f"{tag}_ps{s}"
    if kind in ('copyF', 'siluF', 'qkF', 'gluF', 'vT'):
        oF = st['outF'][s]
        ok = f"{tag}_outF{s}"
    if kind == 'copyF' or kind == 'siluF':
        for ft in range(2):
            for hf in range(2):
                b = ft * 2 + hf
                src = P[:, b * 512:(b + 1) * 512]
                dst = oF[:, ft, hf * 512:(hf + 1) * 512]
                if kind == 'siluF':
                    k.op('act', [pk], [ok], lambda e: e.activation(out=dst, in_=src, func=AF.Silu))
                elif b % 2 == 0:
                    k.op('act', [pk], [ok], lambda e: e.activation(out=dst, in_=src, func=AF.Copy))
                else:
                    k.op('dve', [pk], [ok], lambda e: e.tensor_copy(out=dst, in_=src))
    elif kind == 'qkF':
        qraw, sq, rt = st['qraw'], st['sq'], st['rt']
        ones, epst, gcol = st['ones'], st['eps'], unit['g']
        if part in ('all', 'early'):
            for b in range(4):
                src = P[:, b * 512:(b + 1) * 512]
                k.op('act', [pk], [f"{tag}_qraw{b}"],
                     lambda e: e.activation(out=qraw[:, b, :], in_=src, func=AF.Copy))
                k.op('act', [pk], [f"{tag}_sq{b}"],
                     lambda e: e.activation(out=sq[:, b, :], in_=src, func=AF.Square))
        if part == 'early':
            return
        for ft in range(2):
            for hf in range(2):
                b = ft * 2 + hf
                k.pe([f"{tag}_sq{b}", 'c_ones'], [pk],
                     [lambda e, b=b: e.matmul(P[:, b * 512:(b + 1) * 512], lhsT=ones[:], rhs=sq[:, b, :],
                                              start=True, stop=True)])
            for hf in range(2):
                b = ft * 2 + hf
                src = P[:, b * 512:(b + 1) * 512]
                k.op('act', [pk, 'c_eps'], [f"{tag}_rt{hf}"],
                     lambda e: e.activation(out=rt[:, hf, :], in_=src, func=AF.Sqrt, bias=epst[:, 0:1],
                                            scale=1.0 / 128))
                k.op('dve', [f"{tag}_rt{hf}"], [f"{tag}_rt{hf}"],
                     lambda e: e.reciprocal(out=rt[:, hf, :], in_=rt[:, hf, :]))
                k.op('dve', [f"{tag}_qraw{b}", f"{tag}_rt{hf}", 'c_qg', 'c_kg'], [ok],
                     lambda e: e.scalar_tensor_tensor(out=oF[:, ft, hf * 512:(hf + 1) * 512], in0=qraw[:, b, :],
                                                      scalar=gcol, in1=rt[:, hf, :], op0=ALU.mult, op1=ALU.mult))
    elif kind == 'gluF':
        sig, sgs, actT, bias = st['sig'], st['sgs'][s], st['actT'], st['glub']
        for b in range(4):
            ft, hf = b // 2, b % 2
            fidx = unit['col0'] // 128 + ft
            src = P[:, b * 512:(b + 1) * 512]
            k.op('act', [pk, 'c_glub'], [f"{tag}_sig{b}"],
                 lambda e: e.activation(out=sig[:, b, :], in_=src, func=AF.Sigmoid, bias=bias[:, fidx:fidx + 1]))
            k.op('dve', [f"{tag}_sig{b}", f"{tag}_actT"], [f"{tag}_sig{b}"],
                 lambda e: e.tensor_tensor(out=sig[:, b, :], in0=sig[:, b, :],
                                           in1=actT[:, fidx, hf * 512:(hf + 1) * 512], op=ALU.mult))
            k.op('dve', [f"{tag}_sig{b}", f"{tag}_sgs{s}"], [ok],
                 lambda e: e.tensor_tensor(out=oF[:, ft, hf * 512:(hf + 1) * 512], in0=sig[:, b, :],
                                           in1=sgs[:, ft, hf * 512:(hf + 1) * 512], op=ALU.mult))
    if kind in ('copyF', 'siluF', 'qkF', 'gluF'):
        r0 = unit['row0']
        dst = unit['dst'][r0:r0 + FW, :].rearrange("(ft p) t -> p ft t", p=128)
        k.dma('act', [ok], [unit['dst_key']], out=dst, in_=oF[:], key=ok)
    elif kind == 'vT':
        oT = oF[:].rearrange("p a (b c) -> p (a b) c", b=4)
        for h in range(2):
            src = P[:, h * 1024:(h + 1) * 1024].rearrange("p (a b) -> p a b", a=4)
            dst = oT[:, h * 4:(h + 1) * 4, :]
            if h == 0:
                k.op('act', [pk], [ok], lambda e: e.activation(out=dst, in_=src, func=AF.Copy))
            else:
                k.op('dve', [pk], [ok], lambda e: e.tensor_copy(out=dst, in_=src))
        r0 = unit['row0']
        dst = unit['dst'][:, r0:r0 + FW].rearrange("(tt p) c -> p tt c", p=128)
        k.dma('act', [ok], [unit['dst_key']], out=dst, in_=oT, key=ok)
    elif kind == 'residT':
        oR = st['outR'][s]
        ok = f"{tag}_outR{s}"
        xr = oR
        for h in range(2):
            src = P[:, h * 1024:(h + 1) * 1024].rearrange("p (a b) -> p a b", a=4)
            k.op('dve', [pk], [ok],
                 lambda e: e.tensor_tensor(out=oR[:, h * 4:(h + 1) * 4, :], in0=src,
                                           in1=xr[:, h * 4:(h + 1) * 4, :], op=ALU.add))
        c0 = unit['col0']
        dst = unit['dst'][:, c0:c0 + FW].rearrange("(tt p) c -> p tt c", p=128)
        k.dma('act', [ok], [unit['dst_key']], out=dst, in_=oR[:], key=ok, final=unit.get('final', False))


def phase_A(nc, k, io, cst):
    def alloc(nc_, es):
        st = {}
        st['outF'] = [sb(nc, es, f"A_outF{i}", [128, 2, NTOK], BF16) for i in range(2)]
        st['qraw'] = sb(nc, es, "A_qraw", [128, 4, 512], F32)
        st['sq'] = sb(nc, es, "A_sq", [128, 4, 512], BF16)
        st['rt'] = sb(nc, es, "A_rt", [128, 2, 512], F32)
        st['ones'] = cst['ones']
        st['eps'] = cst['eps']
        return st
    units = []
    for c0 in range(0, 4096, FW):
        units.append(dict(col0=c0, kind='qkF', orient='F', g=cst['qg'][:, 0:1], dst=io['qT'], row0=c0, dst_key='qT'))
    for c0 in range(0, 4096, FW):
        units.append(dict(col0=4096 + c0, kind='qkF', orient='F', g=cst['kg'][:, 0:1], dst=io['kT'], row0=c0, dst_key='kT'))
    for c0 in range(0, 4096, FW):
        units.append(dict(col0=8192 + c0, kind='vT', orient='T', dst=io['v'], row0=c0, dst_key='v'))
    for c0 in range(0, 4096, FW):
        units.append(dict(col0=12288 + c0, kind='siluF', orient='F', dst=io['sgT'], row0=c0, dst_key='sgT'))
    if io.get('limit_units'):
        units = [units[i] for i in io['limit_units']]
    dense_phase(nc, k, 'A', W=io['w_in'], units=units, act_src=('rms', io['x'], io.get('x_keys', [])),
                gk=cst['gk0'], consts=cst, extra_alloc=alloc)


def phase_B(nc, k, io, cst, heads=range(32), stage=99):
    SC = 1.0 / float(np.sqrt(128.0))
    ident = cst['ident']
    negm = cst['negm']
    zcol = cst['zcol']
    with ExitStack() as es:
        kTh = [sb(nc, es, f"B_k{i}", [128, 8, 2, 128], BF16) for i in range(2)]
        vh = [sb(nc, es, f"B_v{i}", [128, 8, 2, 128], BF16) for i in range(2)]
        qTh = [sb(nc, es, f"B_q{i}", [128, NTOK], BF16) for i in range(2)]
        sgh = [sb(nc, es, f"B_sg{i}", [128, NTOK], BF16) for i in range(2)]
        goh = [sb(nc, es, f"B_go{i}", [128, NTOK], BF16) for i in range(2)]
        bt = [sb(nc, es, f"B_bt{i}", [128, 2048], F32) for i in range(3)]
        ob = [sb(nc, es, f"B_ob{i}", [128, 2048], F32) for i in range(3)]
        inc = [sb(nc, es, f"B_inc{i}", [128, 2056], F32) for i in range(2)]
        wq = [sb(nc, es, f"B_wq{i}", [128, 2048], BF16) for i in range(2)]
        wT = [sb(nc, es, f"B_wT{i}", [128, 16, 128], BF16) for i in range(2)]
        zm = [sb(nc, es, f"B_zm{i}", [128, 256], F32) for i in range(3)]
        psz = ps(nc, es, "B_psz", [128, 2048])
        pst = [ps(nc, es, f"B_pst{i}", [128, 512]) for i in range(2)]
        pso = ps(nc, es, "B_pso", [128, 1024])
        kT_rows = io.get('kT_rows') or (lambda r, h: io['kT_all'][r, h * 128:(h + 1) * 128, :])
        v_parts = io.get('v_parts') or (lambda r, h: [(0, 8, io['v_all'][r, :, h * 128:(h + 1) * 128])])

        def load_head(h, hb):
            for r in range(2):
                k.dma('sp', ['kT_all'], [f"B_k{hb}"], out=kTh[hb][:, :, r, :],
                      in_=kT_rows(r, h).rearrange("d (m i) -> d m i", i=128), key=f"B_k{hb}")
                for m0, nm, vap in v_parts(r, h):
                    k.dma('sp', ['v_all'], [f"B_v{hb}"], out=vh[hb][:, m0:m0 + nm, r, :],
                          in_=vap.rearrange("(m p) d -> p m d", p=128), key=f"B_v{hb}")
            k.dma('sp', ['qT'], [f"B_q{hb}"], out=qTh[hb][:], in_=io['qT'][h * 128:(h + 1) * 128, :], key=f"B_q{hb}")
            k.dma('sp', ['sgT'], [f"B_sg{hb}"], out=sgh[hb][:], in_=io['sgT'][h * 128:(h + 1) * 128, :], key=f"B_sg{hb}")

        heads = list(heads)
        blocks = [(hi, m) for hi in range(len(heads)) for m in range(8)]
        tcnt = [0]

        def stage1(bi):
            hi, m = blocks[bi]
            hb, mb = hi % 2, bi % 3
            kl = 256 * (m + 1)
            kflat = kTh[hb][:].rearrange("p m r i -> p (m r i)")
            fns = []
            for c0 in range(0, kl, 512):
                c1 = min(kl, c0 + 512)
                fns.append(lambda e, c0=c0, c1=c1: e.matmul(psz[:, c0:c1], lhsT=qTh[hb][:, m * 128:(m + 1) * 128],
                                                            rhs=kflat[:, c0:c1], start=True, stop=True))
            k.pe([f"B_q{hb}", f"B_k{hb}"], ["B_psz"], fns)
            k.op('dve', ["B_psz", 'c_negm'], [f"B_zm{mb}"],
                 lambda e: e.tensor_tensor(out=zm[mb][:], in0=psz[:, kl - 256:kl], in1=negm[:], op=ALU.add))
            for c0 in range(0, kl - 256, 512):
                c1 = min(kl - 256, c0 + 512)
                k.op('act', ["B_psz", f"B_zm{mb}"], [f"B_bt{mb}"],
                     lambda e: e.activation(out=bt[mb][:, c0:c1], in_=psz[:, c0:c1], func=AF.Sigmoid, scale=SC))
                k.op('act', ["B_psz", f"B_zm{mb}"], [f"B_ob{mb}"],
                     lambda e: e.activation(out=ob[mb][:, c0:c1], in_=psz[:, c0:c1], func=AF.Sigmoid, scale=-SC))
            k.op('act', [f"B_zm{mb}"], [f"B_bt{mb}"],
                 lambda e: e.activation(out=bt[mb][:, kl - 256:kl], in_=zm[mb][:], func=AF.Sigmoid, scale=SC))
            k.op('act', [f"B_zm{mb}"], [f"B_ob{mb}"],
                 lambda e: e.activation(out=ob[mb][:, kl - 256:kl], in_=zm[mb][:], func=AF.Sigmoid, scale=-SC))

        def stage2(bi):
            hi, m = blocks[bi]
            mb, m3 = bi % 2, bi % 3
            kl = 256 * (m + 1)
            k.op('dve', [], [f"B_inc{mb}"], lambda e: e.memset(inc[mb][:, kl:kl + 1], 1.0))
            ob_rev = bass.AP(ob[m3][:].tensor, ob[m3][:, kl - 1:kl].offset, [list(ob[m3][:].ap[0]), [-1, kl]])
            inc_rev = bass.AP(inc[mb][:].tensor, inc[mb][:, kl - 1:kl].offset, [list(inc[mb][:].ap[0]), [-1, kl]])
            k.op('dve', [f"B_ob{m3}", 'c_zcol', f"B_inc{mb}"], [f"B_inc{mb}"],
                 lambda e: e.tensor_tensor_scan(out=inc_rev, data0=zcol[:, 0:1].to_broadcast([128, kl]),
                                                data1=ob_rev, initial=1.0, op0=ALU.add, op1=ALU.mult))
            for c0 in range(0, kl, 1024):
                c1 = min(kl, c0 + 1024)
                k.op('dve', [f"B_bt{m3}", f"B_inc{mb}"], [f"B_wq{mb}"],
                     lambda e: e.tensor_tensor(out=wq[mb][:, c0:c1], in0=bt[m3][:, c0:c1], in1=inc[mb][:, c0 + 1:c1 + 1], op=ALU.mult))

        def stage3(bi):
            hi, m = blocks[bi]
            hb, mb = hi % 2, bi % 2
            nkb = 2 * (m + 1)
            vflat = vh[hb][:].rearrange("p m r d -> p (m r) d")
            for g0 in range(0, nkb, 4):
                g1 = min(nkb, g0 + 4)
                tb = tcnt[0] % 2
                tcnt[0] += 1
                fns = [lambda e, kb=kb, tb=tb, g0=g0: e.matmul(pst[tb][:, (kb - g0) * 128:(kb - g0 + 1) * 128],
                                                               lhsT=wq[mb][:, kb * 128:(kb + 1) * 128], rhs=ident[:],
                                                               start=True, stop=True) for kb in range(g0, g1)]
                k.pe([f"B_wq{mb}", 'c_ident'], [f"B_pst{tb}"], fns)
                src = pst[tb][:, 0:(g1 - g0) * 128].rearrange("p (a b) -> p a b", b=128)
                dst = wT[mb][:, g0:g1, :]
                k.op('act', [f"B_pst{tb}"], [f"B_wT{mb}"], lambda e: e.activation(out=dst, in_=src, func=AF.Copy))
            ob_ = (bi % 2) * 512
            fns = [lambda e, kb=kb: e.matmul(pso[:, ob_:ob_ + 128], lhsT=vflat[:, kb, :], rhs=wT[mb][:, kb, :],
                                             start=(kb == 0), stop=(kb == nkb - 1)) for kb in range(nkb)]
            k.pe([f"B_v{hb}", f"B_wT{mb}"], [f"B_pso{bi % 2}"], fns)
            k.op('dve', [f"B_pso{bi % 2}", f"B_sg{hb}"], [f"B_go{hb}"],
                 lambda e: e.tensor_tensor(out=goh[hb][:, m * 128:(m + 1) * 128], in0=pso[:, ob_:ob_ + 128],
                                           in1=sgh[hb][:, m * 128:(m + 1) * 128], op=ALU.mult))
            if m == 7:
                h = heads[hi]
                k.dma('act', [f"B_go{hb}"], ['goT'], out=io['goT'][h * 128:(h + 1) * 128, :], in_=goh[hb][:], key=f"B_go{hb}")

        load_head(heads[0], 0)
        if len(heads) > 1:
            load_head(heads[1], 1)
        nb = len(blocks)
        stage1(0)
        if nb > 1:
            stage1(1)
        stage2(0)
        for bi in range(nb):
            if bi + 2 < nb:
                stage1(bi + 2)
            if bi + 1 < nb:
                stage2(bi + 1)
            stage3(bi)
            hi0, m0 = blocks[bi]
            if m0 == 7 and hi0 + 2 < len(heads):
                load_head(heads[hi0 + 2], hi0 % 2)


def phase_C(nc, k, io, cst):
    def alloc(nc_, es):
        return dict(outR=[sb(nc, es, f"C_outR{i}", [128, 8, FW], F32) for i in range(2)])
    units = [dict(col0=c0, kind='residT', orient='T', res=io['x'], dst=io['x1'], dst_key='x1', final=True)
             for c0 in range(0, 4096, FW)]
    dense_phase(nc, k, 'C', W=io['w_out'], units=units, act_src=('load', io['goT'], ['goT']),
                consts=cst, extra_alloc=alloc)


def _consts(nc, k, es, names, cin):
    return {n: load_const(nc, k, es, "c_" + {"gk0": "gk"}.get(n, n), cin[n], list(cin[n].shape), cin[n].dtype)
            for n in names}


def build_launch1():
    nc = bass.Bass("TRN2", target_bir_lowering=False)
    k = K(nc)
    dt = lambda n, s, d, kind: nc.dram_tensor(n, s, d, kind=kind).ap()
    io = dict(x=dt("x", [NTOK, D], F32, "ExternalInput"), w_in=dt("w_in", [D, 4 * D], F32, "ExternalInput"),
              qT=dt("qT", [D, NTOK], BF16, "ExternalOutput"), kT=dt("kT", [D, NTOK], BF16, "ExternalOutput"),
              v=dt("v", [NTOK, D], BF16, "ExternalOutput"), sgT=dt("sgT", [D, NTOK], BF16, "ExternalOutput"))
    cin = dict(gk0=dt("gk0", [128, 32], F32, "ExternalInput"), qg=dt("qg", [128, 1], F32, "ExternalInput"),
               kg=dt("kg", [128, 1], F32, "ExternalInput"), ident=dt("ident", [128, 128], BF16, "ExternalInput"),
               ones=dt("ones", [128, 128], BF16, "ExternalInput"), eps=dt("eps", [128, 1], F32, "ExternalInput"))
    with ExitStack() as es:
        cst = _consts(nc, k, es, list(cin), cin)
        phase_A(nc, k, io, cst)
        k.finish(['qT', 'kT', 'v', 'sgT'])
    return nc


def phase_D(nc, k, io, cst):
    def alloc(nc_, es):
        return dict(outF=[sb(nc, es, f"D_outF{i}", [128, 2, NTOK], BF16) for i in range(2)])
    units = [dict(col0=c0, kind='copyF', orient='F', dst=io['uT'], row0=c0, dst_key='uT') for c0 in range(0, D, FW)]
    units += [dict(col0=D + c0, kind='siluF', orient='F', dst=io['sg1T'], row0=c0, dst_key='sg1T') for c0 in range(0, D, FW)]
    dense_phase(nc, k, 'D', W=io['ssm_w_in'], units=units, act_src=('rms', io['x1'], ['x1']),
                gk=cst['gk1'], consts=cst, extra_alloc=alloc)


def phase_F(nc, k, io, cst):
    def alloc(nc_, es):
        return dict(outF=[sb(nc, es, f"F_outF{i}", [128, 2, NTOK], BF16) for i in range(2)],
                    sgs=[sb(nc, es, f"F_sgs{i}", [128, 2, NTOK], BF16) for i in range(2)],
                    sig=sb(nc, es, "F_sig", [128, 4, 512], F32), glub=cst['glub'],
                    **({'tmpb': sb(nc, es, "F_tmpb", [128, 8, NTOK], BF16)} if 'yg_cands' in io else {}))
    units = [dict(col0=c0, kind='gluF', orient='F', sg=io['sg1T'], sg_keys=['sg1T'], dst=io['mT'], row0=c0, dst_key='mT')
             for c0 in range(0, D, FW)]
    src = ('load2', io['yg_cands'], ['yG'], cst['sel']) if 'yg_cands' in io else ('load', io['ygT'], [])
    dense_phase(nc, k, 'F', W=io['glu_w'], units=units, act_src=src, consts=cst, extra_alloc=alloc)


def phase_G(nc, k, io, cst):
    def alloc(nc_, es):
        return dict(outR=[sb(nc, es, f"G_outR{i}", [128, 8, FW], F32) for i in range(2)])
    units = [dict(col0=c0, kind='residT', orient='T', res=io['x1'], res_keys=['x1'], dst=io['out'], dst_key='out', final=True)
             for c0 in range(0, D, FW)]
    dense_phase(nc, k, 'G', W=io['ssm_w_out'], units=units, act_src=('load', io['mT'], ['mT']), consts=cst,
                extra_alloc=alloc)


def build_launch2():
    nc = bass.Bass("TRN2", target_bir_lowering=False)
    k = K(nc)
    dt = lambda n, s, d, kind: nc.dram_tensor(n, s, d, kind=kind).ap()
    io = dict(qT=dt("qT", [D, NTOK], BF16, "ExternalInput"), kT_all=dt("kT_all", [2, D, NTOK], BF16, "ExternalInput"),
              v_all=dt("v_all", [2, NTOK, D], BF16, "ExternalInput"), sgT=dt("sgT", [D, NTOK], BF16, "ExternalInput"),
              goT=dt("goT", [D, NTOK], BF16, "Internal"),
              x=dt("x", [NTOK, D], F32, "ExternalInput"), w_out=dt("w_out", [D, D], F32, "ExternalInput"),
              x1=dt("x1", [NTOK, D], F32, "ExternalOutput"),
              ssm_w_in=dt("ssm_w_in", [D, 2 * D], F32, "ExternalInput"),
              uT=dt("uT", [D, NTOK], BF16, "ExternalOutput"), sg1T=dt("sg1T", [D, NTOK], BF16, "ExternalOutput"))
    cin = dict(ident=dt("ident", [128, 128], BF16, "ExternalInput"), negm=dt("negm", [128, 256], F32, "ExternalInput"),
               zcol=dt("zcol", [128, 1], F32, "ExternalInput"), eps=dt("eps", [128, 1], F32, "ExternalInput"),
               gk1=dt("gk1", [128, 32], F32, "ExternalInput"))
    with ExitStack() as es:
        cst = _consts(nc, k, es, list(cin), cin)
        phase_B(nc, k, io, cst)
        k.barrier()
        phase_C(nc, k, io, cst)
        k.barrier()
        phase_D(nc, k, io, cst)
        k.finish(['x1', 'uT', 'sg1T'])
    return nc


def build_launch3():
    nc = bass.Bass("TRN2", target_bir_lowering=False)
    k = K(nc)
    dt = lambda n, s, d, kind: nc.dram_tensor(n, s, d, kind=kind).ap()
    io = dict(uT_all=dt("uT_all", [2, 2048, NTOK], BF16, "ExternalInput"), ygT=dt("ygT", [2, 2048, NTOK], BF16, "ExternalOutput"))
    for n in ('Bp_re', 'Bp_im', 'Cp_re', 'Cp_im'):
        io[n] = dt(n, [128, 64, 128], F32, "ExternalInput")
    io['Ddiag'] = dt('Ddiag', [128, 16, 128], F32, "ExternalInput")
    for n in ('A_re_T', 'A_im_T', 'logdt_T'):
        io[n] = dt(n, [128, 64], F32, "ExternalInput")
    cin = dict(iota=dt("iota", [128, 2048], F32, "ExternalInput"), halfpi=dt("halfpi", [128, 1], F32, "ExternalInput"),
               zcol=dt("zcol", [128, 1], F32, "ExternalInput"))
    with ExitStack() as es:
        cst = _consts(nc, k, es, list(cin), cin)
        phase_E(nc, k, io, cst)
        k.finish(['ygT'])
    return nc


def build_launch4():
    nc = bass.Bass("TRN2", target_bir_lowering=False)
    k = K(nc)
    dt = lambda n, s, d, kind: nc.dram_tensor(n, s, d, kind=kind).ap()
    io = dict(ygT=dt("ygT_own", [D, NTOK], BF16, "ExternalInput"), sg1T=dt("sg1T", [D, NTOK], BF16, "ExternalInput"),
              glu_w=dt("glu_w", [D, D], F32, "ExternalInput"), mT=dt("mT", [D, NTOK], BF16, "Internal"),
              ssm_w_out=dt("ssm_w_out", [D, D], F32, "ExternalInput"), x1=dt("x1", [NTOK, D], F32, "ExternalInput"),
              out=dt("out", [NTOK, D], F32, "ExternalOutput"))
    cin = dict(glub=dt("glub", [128, 32], F32, "ExternalInput"))
    with ExitStack() as es:
        cst = _consts(nc, k, es, list(cin), cin)
        phase_F(nc, k, io, cst)
        k.barrier()
        phase_G(nc, k, io, cst)
        k.finish(['out'])
    return nc


PAIRS = [[0, 1], [2, 3], [4, 5], [6, 7]]


def build_fused():
    nc = bass.Bass("TRN2", target_bir_lowering=False)
    k = K(nc)
    dt = lambda n, s, d, kind="Internal": nc.dram_tensor(n, s, d, kind=kind).ap()
    EI = "ExternalInput"
    io = dict(x=dt("x", [NTOK, D], F32, EI), w_in=dt("w_in", [D, 4 * D], F32, EI), w_out=dt("w_out", [D, D], F32, EI),
              ssm_w_in=dt("ssm_w_in", [D, 2 * D], F32, EI), glu_w=dt("glu_w", [D, D], F32, EI),
              ssm_w_out=dt("ssm_w_out", [D, D], F32, EI), out=dt("out", [NTOK, D], F32, "ExternalOutput"),
              qT=dt("qT", [D, NTOK], BF16), kT=dt("kT", [D, NTOK], BF16), v=dt("v", [NTOK, D], BF16),
              sgT=dt("sgT", [D, NTOK], BF16), goT=dt("goT", [D, NTOK], BF16), x1=dt("x1", [NTOK, D], F32),
              uT=dt("uT", [D, NTOK], BF16), sg1T=dt("sg1T", [D, NTOK], BF16), mT=dt("mT", [D, NTOK], BF16))
    kT_g = dt("kT_g", [2 * D, NTOK], BF16)
    v_g = dt("v_g", [2 * NTOK, D], BF16)
    u_g = dt("u_g", [2 * D, NTOK], BF16)
    yg_l = dt("yg_l", [D, NTOK], BF16)
    yg_g = dt("yg_g", [2 * D, NTOK], BF16)
    NCH = 4
    k4 = kT_g.rearrange("(c r f) t -> c r f t", c=NCH, r=2)
    v4 = v_g.rearrange("(c r t) f -> c r t f", c=NCH, r=2)
    u4 = u_g.rearrange("(c r f) t -> c r f t", c=NCH, r=2)
    y4 = yg_g.rearrange("(c r f) t -> c r f t", c=NCH, r=2)
    io['kT_rows'] = lambda r, h: k4[h // 8, r, (h % 8) * 128:(h % 8 + 1) * 128, :]
    io['v_parts'] = lambda r, h: [(2 * c, 2, v4[c, r, :, h * 128:(h + 1) * 128]) for c in range(NCH)]
    io['ygT'] = yg_l.rearrange("(d c) t -> d c t", d=2)
    io['u_cands'] = lambda h, r, ct: u4[(h * 2048 + ct * 128) // 1024, r,
                                        (h * 2048 + ct * 128) % 1024:(h * 2048 + ct * 128) % 1024 + 128, :]
    io['yg_cands'] = [(lambda j, d=d: y4[d * 2 + j % 2, j // 2, :, :].rearrange("(kc p) t -> p kc t", p=128))
                      for d in range(2)]

    def gather(src, dst, in_key, out_key, scr):
        rows = src.shape[0] // NCH
        for c in range(NCH):
            k.collective("AllGather", PAIRS, src[c * rows:(c + 1) * rows, :], dst[c * 2 * rows:(c + 1) * 2 * rows, :],
                         in_key, out_key, scr)
    for n in ('Bp_re', 'Bp_im', 'Cp_re', 'Cp_im'):
        io[n] = dt(n, [128, 64, 128], F32, EI)
    io['Ddiag'] = dt('Ddiag', [128, 16, 128], F32, EI)
    for n in ('A_re_T', 'A_im_T', 'logdt_T'):
        io[n] = dt(n, [128, 64], F32, EI)
    cin = dict(gk0=dt("gk0", [128, 32], F32, EI), gk1=dt("gk1", [128, 32], F32, EI), qg=dt("qg", [128, 1], F32, EI),
               kg=dt("kg", [128, 1], F32, EI), ident=dt("ident", [128, 128], BF16, EI), ones=dt("ones", [128, 128], BF16, EI),
               eps=dt("eps", [128, 1], F32, EI), negm=dt("negm", [128, 256], F32, EI), zcol=dt("zcol", [128, 1], F32, EI),
               halfpi=dt("halfpi", [128, 1], F32, EI), glub=dt("glub", [128, 32], F32, EI), sel=dt("sel", [128, 2], F32, EI))
    iota_d = dt("iota", [128, 2048], F32, EI)
    with ExitStack() as es:
        cst = {n: load_const(nc, k, es, "c_" + n, cin[n], list(cin[n].shape), cin[n].dtype) for n in cin}
        scr = sb(nc, es, "cc_scr", [128, 8], F32)
        phase_A(nc, k, io, cst)
        k.barrier()
        gather(io['kT'], kT_g, 'kT', 'kT_all', scr[:])
        gather(io['v'], v_g, 'v', 'v_all', scr[:])
        phase_B(nc, k, io, cst)
        k.barrier()
        phase_C(nc, k, io, cst)
        k.barrier()
        phase_D(nc, k, io, cst)
        k.barrier()
        gather(io['uT'], u_g, 'uT', 'uT_all', scr[:])
        with ExitStack() as es2:
            cst_e = dict(cst)
            cst_e['iota'] = load_const(nc, k, es2, "c_iota", iota_d, [128, 2048], F32)
            phase_E(nc, k, io, cst_e)
            k.barrier()
        gather(yg_l, yg_g, 'ygT', 'yG', scr[:])
        phase_F(nc, k, io, cst)
        k.barrier()
        phase_G(nc, k, io, cst)
        k.finish(['out'])
    return nc


def _negmask(r):
    t = np.arange(128)[:, None]
    j = np.arange(128)[None, :]
    diag = np.where(j < t, 0.0, -1000.0)
    full = np.zeros((128, 128))
    none = np.full((128, 128), -1000.0)
    return np.ascontiguousarray(np.concatenate([diag, none] if r == 0 else [full, diag], axis=1).astype(np.float32))


def kernel_unfused(x, norm_g, attn_w_in, attn_q_g, attn_k_g, attn_w_out, ssm_w_in, ssm_A_re, ssm_A_im, ssm_log_dt,
           ssm_B_re, ssm_B_im, ssm_C_re, ssm_C_im, ssm_D, ssm_glu_w, ssm_glu_b, ssm_w_out):
    f32 = lambda a: np.ascontiguousarray(np.asarray(a, np.float32))
    x = f32(x)
    ncores = 8
    cores = list(range(ncores))
    bf = lambda a: np.asarray(a, np.float32).astype(NPBF)
    ident = bf(np.eye(128))
    ones = bf(np.ones((128, 128)))
    eps = np.full((128, 1), EPS, np.float32)
    zcol = np.zeros((128, 1), np.float32)
    xo = [np.ascontiguousarray(x[c // 2].reshape(8, 2, 128, D)[:, c % 2].reshape(NTOK, D)) for c in cores]
    norm_g = f32(norm_g)
    gk = [np.ascontiguousarray(norm_g[i].reshape(32, 128).T) for i in range(2)]
    qg = f32(attn_q_g)[0].reshape(128, 1)
    kg = f32(attn_k_g)[0].reshape(128, 1)
    w_in = f32(attn_w_in)[0]
    in1 = [dict(x=xo[c], w_in=w_in, gk0=gk[0], qg=qg, kg=kg, ident=ident, ones=ones, eps=eps) for c in cores]
    r1 = run_bass_kernel_spmd(build_launch1(), in1, core_ids=cores).results
    del in1, w_in
    w_out = f32(attn_w_out)[0]
    sw_in = f32(ssm_w_in)[0]
    in2 = []
    for c in cores:
        p = (c // 2) * 2
        kT_all = np.ascontiguousarray(np.stack([np.asarray(r1[p]['kT']), np.asarray(r1[p + 1]['kT'])]))
        v_all = np.ascontiguousarray(np.stack([np.asarray(r1[p]['v']), np.asarray(r1[p + 1]['v'])]))
        in2.append(dict(qT=np.asarray(r1[c]['qT']), kT_all=kT_all, v_all=v_all, sgT=np.asarray(r1[c]['sgT']),
                        x=xo[c], w_out=w_out, ssm_w_in=sw_in, ident=ident, negm=_negmask(c % 2), zcol=zcol,
                        eps=eps, gk1=gk[1]))
    r2 = run_bass_kernel_spmd(build_launch2(), in2, core_ids=cores).results
    del in2, r1, w_out, sw_in
    if os.environ.get('KDEBUG'):
        np.save(os.environ['KDEBUG'] + '_x1.npy', np.stack([np.asarray(r2[c]['x1']) for c in range(2)]))
        np.save(os.environ['KDEBUG'] + '_uT.npy', np.stack([np.asarray(r2[c]['uT']).astype(np.float32) for c in range(2)]))
    lay = [ssm_layouts(r, f32(ssm_A_re)[0], f32(ssm_A_im)[0], f32(ssm_log_dt)[0], f32(ssm_B_re)[0], f32(ssm_B_im)[0],
                       f32(ssm_C_re)[0], f32(ssm_C_im)[0], f32(ssm_D)[0]) for r in range(2)]
    iota = np.ascontiguousarray(np.broadcast_to(np.arange(2048, dtype=np.float32), (128, 2048)))
    halfpi = np.full((128, 1), np.pi / 2, np.float32)
    in3 = []
    for c in cores:
        p, r = (c // 2) * 2, c % 2
        uT_all = np.ascontiguousarray(np.stack([np.asarray(r2[p + rr]['uT'])[2048 * r:2048 * (r + 1)] for rr in range(2)]))
        in3.append(dict(uT_all=uT_all, iota=iota, halfpi=halfpi, zcol=zcol, **lay[r]))
    r3 = run_bass_kernel_spmd(build_launch3(), in3, core_ids=cores).results
    del in3
    glu_w = f32(ssm_glu_w)[0]
    sw_out = f32(ssm_w_out)[0]
    glub = np.ascontiguousarray(f32(ssm_glu_b)[0].reshape(32, 128).T)
    in4 = []
    for c in cores:
        p, r = (c // 2) * 2, c % 2
        yg_own = np.ascontiguousarray(np.concatenate([np.asarray(r3[p + rr]['ygT'])[r] for rr in range(2)], axis=0))
        in4.append(dict(ygT_own=yg_own, sg1T=np.asarray(r2[c]['sg1T']), glu_w=glu_w, ssm_w_out=sw_out,
                        x1=np.asarray(r2[c]['x1']), glub=glub))
    r4 = run_bass_kernel_spmd(build_launch4(), in4, core_ids=cores).results
    out = np.zeros((4, 2048, D), np.float32)
    for c in cores:
        b, r = c // 2, c % 2
        out[b].reshape(8, 2, 128, D)[:, r] = np.asarray(r4[c]['out'], np.float32).reshape(8, 128, D)
    return out


def phase_E(nc, k, io, cst, pairs=range(64)):
    iota, halfpi, zc = cst['iota'], cst['halfpi'], cst['zcol']
    pairs = list(pairs)
    cts = sorted(set(q // 4 for q in pairs))
    with ExitStack() as es:
        T = lambda name, shape, dt=F32: sb(nc, es, "E_" + name, shape, dt)
        Are = T("Are", [128, 64]); Aim = T("Aim", [128, 64]); dtt = T("dt", [128, 64])
        xx = T("xx", [128, 64]); tt = T("tt", [128, 64]); rho = T("rho", [128, 64]); rm1 = T("rm1", [128, 64])
        th = T("th", [128, 64]); thf = T("thf", [128, 64]); fa1 = T("fa1", [128, 64])
        s1 = T("s1", [128, 64]); c1 = T("c1", [128, 64]); sh = T("sh", [128, 64])
        am1 = T("am1", [128, 64]); abi = T("abi", [128, 64]); den = T("den", [128, 64])
        fre = T("fre", [128, 64]); fim = T("fim", [128, 64]); t2 = T("t2", [128, 64])
        k.dma('sp', [], ['E_Are'], out=Are[:], in_=io['A_re_T'], key='E_Are')
        k.dma('sp', [], ['E_Aim'], out=Aim[:], in_=io['A_im_T'], key='E_Aim')
        k.dma('sp', [], ['E_dt'], out=dtt[:], in_=io['logdt_T'], key='E_dt')
        nm_ = lambda r: r if r.startswith('c_') else "E_" + r
        V = lambda reads, writes, fn: k.op('dve', [nm_(r) for r in reads], [nm_(w) for w in writes], fn)
        A = lambda reads, writes, fn: k.op('act', [nm_(r) for r in reads], [nm_(w) for w in writes], fn)
        Pl = lambda reads, writes, fn: k.op('pool', [nm_(r) for r in reads], [nm_(w) for w in writes], fn)
        A(['dt'], ['dt'], lambda e: e.activation(out=dtt[:], in_=dtt[:], func=AF.Exp))
        V(['Are', 'dt'], ['xx'], lambda e: e.tensor_tensor(out=xx[:], in0=Are[:], in1=dtt[:], op=ALU.mult))
        V(['xx'], ['tt'], lambda e: e.tensor_scalar(out=tt[:], in0=xx[:], scalar1=1.0 / 24, scalar2=1.0 / 6, op0=ALU.mult, op1=ALU.add))
        V(['tt', 'xx'], ['tt'], lambda e: e.tensor_tensor(out=tt[:], in0=tt[:], in1=xx[:], op=ALU.mult))
        V(['tt'], ['tt'], lambda e: e.tensor_scalar(out=tt[:], in0=tt[:], scalar1=0.5, scalar2=None, op0=ALU.add))
        V(['tt', 'xx'], ['tt'], lambda e: e.tensor_tensor(out=tt[:], in0=tt[:], in1=xx[:], op=ALU.mult))
        V(['tt'], ['tt'], lambda e: e.tensor_scalar(out=tt[:], in0=tt[:], scalar1=1.0, scalar2=None, op0=ALU.add))
        V(['tt', 'xx'], ['rm1'], lambda e: e.tensor_tensor(out=rm1[:], in0=tt[:], in1=xx[:], op=ALU.mult))
        V(['rm1'], ['rho'], lambda e: e.tensor_scalar(out=rho[:], in0=rm1[:], scalar1=1.0, scalar2=None, op0=ALU.add))
        V(['Aim', 'dt'], ['th'], lambda e: e.tensor_tensor(out=th[:], in0=Aim[:], in1=dtt[:], op=ALU.mult))
        V(['th'], ['t2'], lambda e: e.tensor_scalar(out=t2[:], in0=th[:], scalar1=1.0 / TWO_PI, scalar2=MAGIC, op0=ALU.mult, op1=ALU.add))
        V(['t2'], ['t2'], lambda e: e.tensor_scalar(out=t2[:], in0=t2[:], scalar1=MAGIC, scalar2=None, op0=ALU.subtract))
        V(['th', 't2'], ['thf'], lambda e: e.scalar_tensor_tensor(out=thf[:], in0=th[:], scalar=1.0 / TWO_PI, in1=t2[:], op0=ALU.mult, op1=ALU.subtract))
        V(['thf'], ['fa1'], lambda e: e.scalar_tensor_tensor(out=fa1[:], in0=thf[:], scalar=-1.0, in1=thf[:], op0=ALU.mult, op1=ALU.max))
        A(['thf'], ['s1'], lambda e: e.activation(out=s1[:], in_=thf[:], func=AF.Sin, scale=TWO_PI))
        A(['fa1', 'c_halfpi'], ['c1'], lambda e: e.activation(out=c1[:], in_=fa1[:], func=AF.Sin, scale=-TWO_PI, bias=halfpi[:, 0:1]))
        A(['thf'], ['sh'], lambda e: e.activation(out=sh[:], in_=thf[:], func=AF.Sin, scale=TWO_PI / 2))
        V(['sh'], ['sh'], lambda e: e.tensor_tensor(out=sh[:], in0=sh[:], in1=sh[:], op=ALU.mult))
        V(['rm1', 'c1'], ['am1'], lambda e: e.tensor_tensor(out=am1[:], in0=rm1[:], in1=c1[:], op=ALU.mult))
        V(['sh', 'am1'], ['am1'], lambda e: e.scalar_tensor_tensor(out=am1[:], in0=sh[:], scalar=-2.0, in1=am1[:], op0=ALU.mult, op1=ALU.add))
        V(['rho', 's1'], ['abi'], lambda e: e.tensor_tensor(out=abi[:], in0=rho[:], in1=s1[:], op=ALU.mult))
        V(['Are'], ['den'], lambda e: e.tensor_tensor(out=den[:], in0=Are[:], in1=Are[:], op=ALU.mult))
        V(['Aim'], ['t2'], lambda e: e.tensor_tensor(out=t2[:], in0=Aim[:], in1=Aim[:], op=ALU.mult))
        V(['den', 't2'], ['den'], lambda e: e.tensor_tensor(out=den[:], in0=den[:], in1=t2[:], op=ALU.add))
        V(['den'], ['den'], lambda e: e.reciprocal(out=den[:], in_=den[:]))
        V(['am1', 'Are'], ['fre'], lambda e: e.tensor_tensor(out=fre[:], in0=am1[:], in1=Are[:], op=ALU.mult))
        V(['abi', 'Aim'], ['t2'], lambda e: e.tensor_tensor(out=t2[:], in0=abi[:], in1=Aim[:], op=ALU.mult))
        V(['fre', 't2'], ['fre'], lambda e: e.tensor_tensor(out=fre[:], in0=fre[:], in1=t2[:], op=ALU.add))
        V(['fre', 'den'], ['fre'], lambda e: e.tensor_tensor(out=fre[:], in0=fre[:], in1=den[:], op=ALU.mult))
        V(['abi', 'Are'], ['fim'], lambda e: e.tensor_tensor(out=fim[:], in0=abi[:], in1=Are[:], op=ALU.mult))
        V(['am1', 'Aim'], ['t2'], lambda e: e.tensor_tensor(out=t2[:], in0=am1[:], in1=Aim[:], op=ALU.mult))
        V(['fim', 't2'], ['fim'], lambda e: e.tensor_tensor(out=fim[:], in0=fim[:], in1=t2[:], op=ALU.subtract))
        V(['fim', 'den'], ['fim'], lambda e: e.tensor_tensor(out=fim[:], in0=fim[:], in1=den[:], op=ALU.mult))

        ut = [T(f"ut{i}", [128, 8, 2, 128], BF16) for i in range(2)]
        if 'u_cands' in io:
            ucand = [[T(f"uc{h}", [128, 8, 2, 128], BF16)] * 2 for h in range(2)]
        wstg = [T("wstg", [128, 4, 4, 128])] * 2
        dstg = [T("dstg", [128, 128])] * 2
        TA = T("ta", [128, 2048]); TB = T("tb", [128, 2048])
        bw = [T(f"bw{i}", [128, 2, 4, 128], BF16) for i in range(2)]
        cw = [T(f"cw{i}", [128, 2, 4, 128], BF16) for i in range(2)]
        dw = [T(f"dw{i}", [128, 128], BF16) for i in range(2)]
        ctmp = T("ctmp", [128, 2, 4, 128])
        cosT = [T(f"cos{i}", [128, 2048]) for i in range(2)]
        sinT = [T(f"sin{i}", [128, 2048]) for i in range(2)]
        fa = T("fa", [128, 2048]); fr = T("fr", [128, 2048])
        G = [T("Gre", [128, 2048]), T("Gim", [128, 2048])]
        Rb = [[T(f"Rre{i}", [128, 2048]), T(f"Rim{i}", [128, 2048])] for i in range(2)]
        mg = T("mg", [128, 2])
        V([], ['mg'], lambda e: e.memset(mg[:, 0:1], MAGIC))
        V(['mg'], ['mg'], lambda e: e.memset(mg[:, 1:2], -MAGIC))
        H = [T(f"H{i}", [128, 4, 2048], BF16) for i in range(2)]
        yg = [T(f"yg{i}", [128, 8, 2, 128], BF16) for i in range(2)]
        psb = [ps(nc, es, f"E_psb{i}", [128, 1024]) for i in range(2)]
        yps = ps(nc, es, "E_yps", [128, 2048])

        def load_ct(ci):
            ct = cts[ci]
            j = ci % 2
            if 'u_cands' in io:
                for h in range(2):
                    for r in range(2):
                        k.dma('sp', ['uT_all'], [f"E_uc{h}"], out=ucand[h][j][:, :, r, :],
                              in_=io['u_cands'](h, r, ct).rearrange("c (m i) -> c m i", i=128), key=f"E_uc{h}")
                k.op('pool', ["E_uc0", 'c_sel'], [f"E_ut{j}"],
                     lambda e: e.tensor_scalar(out=ut[j][:], in0=ucand[0][j][:], scalar1=cst['sel'][:, 0:1], scalar2=None,
                                               op0=ALU.mult))
                k.op('pool', ["E_uc1", 'c_sel'], ["E_uc1"],
                     lambda e: e.tensor_scalar(out=ucand[1][j][:], in0=ucand[1][j][:], scalar1=cst['sel'][:, 1:2],
                                               scalar2=None, op0=ALU.mult))
                k.op('pool', ["E_uc1", f"E_ut{j}"], [f"E_ut{j}"],
                     lambda e: e.tensor_tensor(out=ut[j][:], in0=ut[j][:], in1=ucand[1][j][:], op=ALU.add))
            else:
                for r in range(2):
                    k.dma('sp', ['uT_all'], [f"E_ut{j}"], out=ut[j][:, :, r, :],
                          in_=io['uT_all'][r, ct * 128:(ct + 1) * 128, :].rearrange("c (m i) -> c m i", i=128), key=f"E_ut{j}")
            for w, nm in enumerate(('Bp_re', 'Bp_im', 'Cp_re', 'Cp_im')):
                k.dma('sp', [], ["E_wstg"], out=wstg[j][:, w, :, :], in_=io[nm][:, ct * 4:(ct + 1) * 4, :], key="E_wstg")
            k.dma('sp', [], ["E_dstg"], out=dstg[j][:], in_=io['Ddiag'][:, ct, :], key="E_dstg")
            prep_ct(ci)

        def prep_ct(ci):
            ct = cts[ci]
            j = ci % 2
            k.op('pool', ["E_wstg"], [f"E_bw{j}"], lambda e: e.tensor_copy(out=bw[j][:], in_=wstg[j][:, 0:2, :, :]))
            k.op('pool', ["E_dstg"], [f"E_dw{j}"], lambda e: e.tensor_copy(out=dw[j][:], in_=dstg[j][:]))
            fr_b = fre[:, ct * 4:(ct + 1) * 4].unsqueeze(2).to_broadcast([128, 4, 128])
            fi_b = fim[:, ct * 4:(ct + 1) * 4].unsqueeze(2).to_broadcast([128, 4, 128])
            Cre, Cim = wstg[j][:, 2, :, :], wstg[j][:, 3, :, :]
            k.op('pool', ["E_wstg", 'E_fre'], ['E_ctmp'], lambda e: e.tensor_tensor(out=ctmp[:, 0, :, :], in0=Cre, in1=fr_b, op=ALU.mult))
            k.op('pool', ["E_wstg", 'E_fim'], ['E_ctmp'], lambda e: e.tensor_tensor(out=ctmp[:, 1, :, :], in0=Cim, in1=fi_b, op=ALU.mult))
            k.op('pool', ['E_ctmp'], [f"E_cw{j}"], lambda e: e.tensor_tensor(out=cw[j][:, 0, :, :], in0=ctmp[:, 0, :, :], in1=ctmp[:, 1, :, :], op=ALU.subtract))
            k.op('pool', ["E_wstg", 'E_fim'], ['E_ctmp'], lambda e: e.tensor_tensor(out=ctmp[:, 0, :, :], in0=Cre, in1=fi_b, op=ALU.mult))
            k.op('pool', ["E_wstg", 'E_fre'], ['E_ctmp'], lambda e: e.tensor_tensor(out=ctmp[:, 1, :, :], in0=Cim, in1=fr_b, op=ALU.mult))
            k.op('pool', ['E_ctmp'], ['E_ctmp'], lambda e: e.tensor_tensor(out=ctmp[:, 0, :, :], in0=ctmp[:, 0, :, :], in1=ctmp[:, 1, :, :], op=ALU.add))
            k.op('pool', ['E_ctmp'], [f"E_cw{j}"], lambda e: e.tensor_scalar(out=cw[j][:, 1, :, :], in0=ctmp[:, 0, :, :], scalar1=-1.0, scalar2=None, op0=ALU.mult))

        ci_of = {ct: ci for ci, ct in enumerate(cts)}

        def tables_a(qi):
            q = pairs[qi]
            thq = thf[:, q:q + 1]
            A(['thf', 'c_iota', 'mg'], ['fa'], lambda e: e.activation(out=fa[:], in_=iota[:], func=AF.Identity, scale=thq, bias=mg[:, 0:1]))
            A(['fa', 'mg'], ['fa'], lambda e: e.activation(out=fa[:], in_=fa[:], func=AF.Identity, bias=mg[:, 1:2]))

        def tables_b(qi):
            q = pairs[qi]
            tb = qi % 2
            thq = thf[:, q:q + 1]
            V(['fa', 'thf', 'c_iota'], ['fr'], lambda e: e.scalar_tensor_tensor(out=fr[:], in0=iota[:], scalar=thq, in1=fa[:], op0=ALU.mult, op1=ALU.subtract))
            A(['fr'], [f"sin{tb}"], lambda e: e.activation(out=sinT[tb][:], in_=fr[:], func=AF.Sin, scale=TWO_PI))
            A(['fr'], ['fa'], lambda e: e.activation(out=fa[:], in_=fr[:], func=AF.Abs))
            A(['fa', 'c_halfpi'], [f"cos{tb}"], lambda e: e.activation(out=cosT[tb][:], in_=fa[:], func=AF.Sin, scale=-TWO_PI, bias=halfpi[:, 0:1]))

        def open_ct(ct):
            ci = ci_of[ct]
            j = ci % 2
            uflat = ut[j][:].rearrange("p m r i -> p (m r i)")
            k.pe([f"E_dw{j}", f"E_ut{j}"], ['E_yps'],
                 [lambda e, sg=sg: e.matmul(yps[:, sg * 512:(sg + 1) * 512], lhsT=dw[j][:], rhs=uflat[:, sg * 512:(sg + 1) * 512],
                                            start=True, stop=False) for sg in range(4)])

        def stage_in(qi):
            q = pairs[qi]
            ct, b, tb = q // 4, q % 4, qi % 2
            j = ci_of[ct] % 2
            uflat = ut[j][:].rearrange("p m r i -> p (m r i)")
            cT, sT = cosT[tb], sinT[tb]
            for sg in range(4):
                sl = slice(sg * 512, (sg + 1) * 512)
                s_ = sg % 2
                k.pe([f"E_bw{j}", f"E_ut{j}"], [f"E_psb{s_}"],
                     [lambda e: e.matmul(psb[s_][:, 0:512], lhsT=bw[j][:, 0, b, :], rhs=uflat[:, sl], start=True, stop=True),
                      lambda e: e.matmul(psb[s_][:, 512:1024], lhsT=bw[j][:, 1, b, :], rhs=uflat[:, sl], start=True, stop=True)])
                A([f"psb{s_}"], ['Gre'], lambda e: e.activation(out=G[0][:, sl], in_=psb[s_][:, 0:512], func=AF.Copy))
                A([f"psb{s_}"], ['Gim'], lambda e: e.activation(out=G[1][:, sl], in_=psb[s_][:, 512:1024], func=AF.Copy))

        def stage_rot(qi):
            q = pairs[qi]
            tb = qi % 2
            cT, sT = cosT[tb], sinT[tb]
            V([f"sin{tb}", 'Gim'], ['ta'], lambda e: e.tensor_tensor(out=TA[:], in0=G[1][:], in1=sT[:], op=ALU.mult))
            V([f"sin{tb}", 'Gre'], ['tb'], lambda e: e.tensor_tensor(out=TB[:], in0=G[0][:], in1=sT[:], op=ALU.mult))
            V([f"cos{tb}", 'Gre'], ['Gre'], lambda e: e.tensor_tensor(out=G[0][:], in0=G[0][:], in1=cT[:], op=ALU.mult))
            V([f"cos{tb}", 'Gim'], ['Gim'], lambda e: e.tensor_tensor(out=G[1][:], in0=G[1][:], in1=cT[:], op=ALU.mult))
            V(['Gre', 'ta'], ['Gre'], lambda e: e.tensor_tensor(out=G[0][:], in0=G[0][:], in1=TA[:], op=ALU.add))
            V(['Gim', 'tb'], ['Gim'], lambda e: e.tensor_tensor(out=G[1][:], in0=G[1][:], in1=TB[:], op=ALU.subtract))

        def stage_scan(qi):
            q = pairs[qi]
            rq = rho[:, q:q + 1].to_broadcast([128, 2048])
            R = Rb[qi % 2]
            for c in range(2):
                V(['rho', ('Gre', 'Gim')[c]], [(f"Rre{qi % 2}", f"Rim{qi % 2}")[c]],
                  lambda e: e.tensor_tensor_scan(out=R[c][:], data0=rq, data1=G[c][:], initial=0.0, op0=ALU.mult, op1=ALU.add))

        def stage_in_b(qi):
            tb, hb = qi % 2, qi % 2
            cT, sT = cosT[tb], sinT[tb]
            R = Rb[qi % 2]
            rre, rim = f"Rre{qi % 2}", f"Rim{qi % 2}"
            V([f"cos{tb}", rre], [f"H{hb}"], lambda e: e.tensor_tensor(out=H[hb][:, 0, :], in0=R[0][:], in1=cT[:], op=ALU.mult))
            V([f"sin{tb}", rim], [f"H{hb}"], lambda e: e.scalar_tensor_tensor(out=H[hb][:, 1, :], in0=R[1][:], scalar=-1.0, in1=sT[:], op0=ALU.mult, op1=ALU.mult))
            V([f"cos{tb}", rim], [f"H{hb}"], lambda e: e.tensor_tensor(out=H[hb][:, 2, :], in0=R[1][:], in1=cT[:], op=ALU.mult))
            V([f"sin{tb}", rre], [f"H{hb}"], lambda e: e.tensor_tensor(out=H[hb][:, 3, :], in0=R[0][:], in1=sT[:], op=ALU.mult))

        def stage_out(qi):
            q = pairs[qi]
            ct, b, hb = q // 4, q % 4, qi % 2
            ci = ci_of[ct]
            j = ci % 2
            myp = [qq for qq in pairs if qq // 4 == ct]
            if q == myp[0]:
                open_ct(ct)
            last = (q == myp[-1])
            fns = []
            for sg in range(4):
                sl = slice(sg * 512, (sg + 1) * 512)
                fns.append(lambda e, sl=sl: e.matmul(yps[:, sl], lhsT=cw[j][:, 0, b, :], rhs=H[hb][:, 0, sl], start=False, stop=False))
                fns.append(lambda e, sl=sl: e.matmul(yps[:, sl], lhsT=cw[j][:, 0, b, :], rhs=H[hb][:, 1, sl], start=False, stop=False))
                fns.append(lambda e, sl=sl: e.matmul(yps[:, sl], lhsT=cw[j][:, 1, b, :], rhs=H[hb][:, 2, sl], start=False, stop=False))
                fns.append(lambda e, sl=sl: e.matmul(yps[:, sl], lhsT=cw[j][:, 1, b, :], rhs=H[hb][:, 3, sl], start=False, stop=last))
            k.pe([f"E_cw{j}", f"E_H{hb}"], ['E_yps'], fns)
            if last:
                ygf = yg[j][:].rearrange("p m r i -> p (m r i)")
                for sg in range(4):
                    sl = slice(sg * 512, (sg + 1) * 512)
                    A(['yps'], ['ta'], lambda e: e.activation(out=TA[:, sl], in_=yps[:, sl], func=AF.Copy))
                    A(['yps'], ['tb'], lambda e: e.activation(out=TB[:, sl], in_=yps[:, sl], func=AF.Square))
                V(['tb'], ['tb'], lambda e: e.tensor_scalar(out=TB[:], in0=TB[:], scalar1=0.044715, scalar2=1.0, op0=ALU.mult, op1=ALU.add))
                V(['ta', 'tb'], ['tb'], lambda e: e.tensor_tensor(out=TB[:], in0=TB[:], in1=TA[:], op=ALU.mult))
                A(['tb'], ['tb'], lambda e: e.activation(out=TB[:], in_=TB[:], func=AF.Sigmoid, scale=1.5957691216057308))
                V(['ta', 'tb'], [f"yg{j}"], lambda e: e.tensor_tensor(out=ygf, in0=TA[:], in1=TB[:], op=ALU.mult))
                for r in range(2):
                    k.dma('act', [f"E_yg{j}"], ['ygT'], out=io['ygT'][r, ct * 128:(ct + 1) * 128, :].rearrange("c (m i) -> c m i", i=128),
                          in_=yg[j][:, :, r, :], key=f"E_yg{j}")

        load_ct(0)
        if len(cts) > 1:
            load_ct(1)
        tables_a(0)
        tables_b(0)
        npairs = len(pairs)
        for qi in range(npairs):
            q = pairs[qi]
            stage_in(qi)
            if qi + 1 < npairs:
                tables_a(qi + 1)
            stage_rot(qi)
            if qi + 1 < npairs:
                tables_b(qi + 1)
            stage_scan(qi)
            stage_in_b(qi)
            if qi >= 1:
                stage_out(qi - 1)
                qp = pairs[qi - 1]
                if qp // 4 != q // 4 and ci_of[qp // 4] + 2 < len(cts):
                    load_ct(ci_of[qp // 4] + 2)
        stage_out(npairs - 1)


def ssm_layouts(r, A_re, A_im, log_dt, B_re, B_im, C_re, C_im, Dv):
    g0 = 128 * r
    Bp = [np.zeros((128, 64, 128), np.float32) for _ in range(2)]
    Cp = [np.zeros((128, 64, 128), np.float32) for _ in range(2)]
    for q in range(64):
        b = q % 4
        for gg in range(2):
            g = g0 + 2 * q + gg
            rows = slice(32 * b + 16 * gg, 32 * b + 16 * gg + 16)
            cols = slice(64 * gg, 64 * gg + 64)
            Bp[0][rows, q, cols] = B_re[g].T
            Bp[1][rows, q, cols] = B_im[g].T
            Cp[0][cols, q, rows] = C_re[g].T
            Cp[1][cols, q, rows] = C_im[g].T
    Dd = np.zeros((128, 16, 128), np.float32)
    for ct in range(16):
        Dd[np.arange(128), ct, np.arange(128)] = Dv[2048 * r + 128 * ct: 2048 * r + 128 * (ct + 1)]
    tr = lambda a: np.ascontiguousarray(a[g0:g0 + 128].reshape(64, 2, 64).transpose(1, 2, 0).reshape(128, 64))
    ldt = np.ascontiguousarray(np.broadcast_to(log_dt[g0:g0 + 128].reshape(64, 2, 1), (64, 2, 64)).transpose(1, 2, 0).reshape(128, 64))
    return dict(Bp_re=Bp[0], Bp_im=Bp[1], Cp_re=Cp[0], Cp_im=Cp[1], Ddiag=Dd, A_re_T=tr(A_re), A_im_T=tr(A_im),
                logdt_T=ldt.astype(np.float32))


def kernel(x, norm_g, attn_w_in, attn_q_g, attn_k_g, attn_w_out, ssm_w_in, ssm_A_re, ssm_A_im, ssm_log_dt,
           ssm_B_re, ssm_B_im, ssm_C_re, ssm_C_im, ssm_D, ssm_glu_w, ssm_glu_b, ssm_w_out):
    f32 = lambda a: np.ascontiguousarray(np.asarray(a, np.float32))
    x = f32(x)
    cores = list(range(8))
    bf = lambda a: np.asarray(a, np.float32).astype(NPBF)
    norm_g = f32(norm_g)
    lay = [ssm_layouts(r, f32(ssm_A_re)[0], f32(ssm_A_im)[0], f32(ssm_log_dt)[0], f32(ssm_B_re)[0], f32(ssm_B_im)[0],
                       f32(ssm_C_re)[0], f32(ssm_C_im)[0], f32(ssm_D)[0]) for r in range(2)]
    shared = dict(
        w_in=f32(attn_w_in)[0], w_out=f32(attn_w_out)[0], ssm_w_in=f32(ssm_w_in)[0], glu_w=f32(ssm_glu_w)[0],
        ssm_w_out=f32(ssm_w_out)[0],
        gk0=np.ascontiguousarray(norm_g[0].reshape(32, 128).T), gk1=np.ascontiguousarray(norm_g[1].reshape(32, 128).T),
        qg=f32(attn_q_g)[0].reshape(128, 1), kg=f32(attn_k_g)[0].reshape(128, 1),
        ident=bf(np.eye(128)), ones=bf(np.ones((128, 128))), eps=np.full((128, 1), EPS, np.float32),
        zcol=np.zeros((128, 1), np.float32), halfpi=np.full((128, 1), np.pi / 2, np.float32),
        glub=np.ascontiguousarray(f32(ssm_glu_b)[0].reshape(32, 128).T),
        iota=np.ascontiguousarray(np.broadcast_to(np.arange(2048, dtype=np.float32), (128, 2048))))
    in_maps = []
    for c in cores:
        b, r = c // 2, c % 2
        sel = np.zeros((128, 2), np.float32)
        sel[:, r] = 1.0
        m = dict(shared)
        m.update(x=np.ascontiguousarray(x[b].reshape(8, 2, 128, D)[:, r].reshape(NTOK, D)), negm=_negmask(r), sel=sel, **lay[r])
        in_maps.append(m)
    res = run_bass_kernel_spmd(build_fused(), in_maps, core_ids=cores).results
    out = np.zeros((4, 2048, D), np.float32)
    for c in cores:
        b, r = c // 2, c % 2
        out[b].reshape(8, 2, 128, D)[:, r] = np.asarray(res[c]['out'], np.float32).reshape(8, 128, D)
    return out
```

```python
import numpy as np
import os
from contextlib import ExitStack
import ml_dtypes
import concourse.bass as bass
import concourse.mybir as mybir
from concourse.bass_utils import run_bass_kernel_spmd

F32 = mybir.dt.float32
BF16 = mybir.dt.bfloat16
ALU = mybir.AluOpType
AF = mybir.ActivationFunctionType
NPBF = ml_dtypes.bfloat16

D = 4096
NTOK = 1024
KC = 32
FW = 256
EPS = 1e-6
MAGIC = 12582912.0
TWO_PI = 6.283185307179586


class K:
    def __init__(self, nc):
        self.nc = nc
        self.eng = {'pe': nc.tensor, 'act': nc.scalar, 'dve': nc.vector, 'pool': nc.gpsimd, 'sp': nc.sync}
        self.prog = {}
        self.waited = {}
        self.lastw = {}
        self.readers = {}
        self.dsem = {}
        self.nsem = 0
        self.out_tokens = []
        self.old_tokens = []

    def _newsem(self, tag):
        self.nsem += 1
        return self.nc.alloc_semaphore(f"s_{tag}_{self.nsem}")

    def _wait(self, e, tok):
        sem, val, owner, sid = tok
        key = (e, sid)
        if self.waited.get(key, 0) >= val:
            return
        self.eng[e].wait_ge(sem, val)
        self.waited[key] = val

    def _deps(self, e, reads, writes):
        toks = []
        for b in list(reads) + list(writes):
            for t in self.lastw.get(b, {}).values():
                toks.append((t, True))
        for b in writes:
            for t in self.readers.get(b, []):
                toks.append((t, False))
        for t, is_w in toks:
            if t[2] == e:
                if e in ('pe', 'sp'):
                    continue
            self._wait(e, t)

    def _mark(self, e, ins):
        p = self.prog.get(e)
        if p is None or p[1] >= 6000:
            if p is not None:
                self.old_tokens.append((p[0], p[1], e, p[2]))
            p = [self._newsem(e), 0, self.nsem]
            self.prog[e] = p
        p[1] += 1
        ins.then_inc(p[0], 1)
        return (p[0], p[1], e, p[2])

    def _record(self, tok, reads, writes):
        for b in writes:
            self.lastw.setdefault(b, {})[tok[2] if tok[2] != 'dma' else ('dma', tok[3])] = tok
            self.readers[b] = []
        for b in reads:
            self.readers.setdefault(b, []).append(tok)

    def op(self, e, reads, writes, fn):
        self._deps(e, reads, writes)
        ins = fn(self.eng[e])
        tok = self._mark(e, ins)
        self._record(tok, reads, writes)
        return tok

    def pe(self, reads, writes, fns):
        self._deps('pe', reads, writes)
        ins = None
        for f in fns:
            ins = f(self.nc.tensor)
        tok = self._mark('pe', ins)
        self._record(tok, reads, writes)
        return tok

    def dma(self, q, reads, writes, out, in_, key, final=False):
        self._deps(q, reads, writes)
        d = self.dsem.get(key)
        if d is None or d[1] >= 1500:
            if d is not None:
                self.old_tokens.append((d[0], 16 * d[1], 'dma', d[2]))
            d = [self._newsem('d'), 0, self.nsem]
            self.dsem[key] = d
        ins = self.eng[q].dma_start(out=out, in_=in_)
        d[1] += 1
        ins.then_inc(d[0], 16)
        tok = (d[0], 16 * d[1], 'dma', d[2])
        self._record(tok, reads, writes)
        if final:
            self.out_tokens.append(tok)
        return tok

    def collective(self, kind, groups, in_ap, out_ap, in_key, out_key, scratch):
        self._deps('pool', [in_key], [out_key])
        if not hasattr(self, 'ccsem'):
            self.ccsem = [self._newsem('cc'), 0]
        ins = self.nc.gpsimd.collective_compute(kind, ALU.bypass, replica_groups=groups,
                                                ins=[in_ap.opt()], outs=[out_ap.opt()])
        self.ccsem[1] += 1
        ins.then_inc(self.ccsem[0])
        self.nc.gpsimd.wait_ge(self.ccsem[0], self.ccsem[1])
        return self.op('pool', [in_key], [out_key], lambda e: e.memset(scratch, 0.0))

    def barrier(self):
        toks = []
        for e, p in self.prog.items():
            toks.append((p[0], p[1], e, p[2]))
        for key, d in self.dsem.items():
            toks.append((d[0], 16 * d[1], 'dma', d[2]))
        toks += self.old_tokens
        for e in self.eng:
            for t in toks:
                if t[2] == e:
                    continue
                self._wait(e, t)

    def finish(self, keys=()):
        for tok in self.out_tokens:
            self._wait('sp', tok)
        for key in keys:
            for t in self.lastw.get(key, {}).values():
                self._wait('sp', t)


def sb(nc, es, name, shape, dt):
    return es.enter_context(nc.sbuf_tensor(name, shape, dt))


def ps(nc, es, name, shape, dt=F32):
    return es.enter_context(nc.psum_tensor(name, shape, dt))


def load_const(nc, k, es, name, src, shape, dt):
    t = sb(nc, es, name, shape, dt)
    k.dma('sp', [], [name], out=t[:], in_=src, key=name)
    return t


def dense_phase(nc, k, tag, *, W, units, act_src, gk=None, consts=None, extra_alloc=None):
    with ExitStack() as es:
        actT = sb(nc, es, f"{tag}_actT", [128, KC, NTOK], BF16)
        wst = [sb(nc, es, f"{tag}_wst{i}", [128, KC, FW], F32) for i in range(2)]
        wbf = [sb(nc, es, f"{tag}_wbf{i}", [128, KC, FW], BF16) for i in range(2)]
        psum = [ps(nc, es, f"{tag}_ps{i}", [128, 2048]) for i in range(2)]
        st = extra_alloc(nc, es) if extra_alloc else {}
        st.update(actT=actT, psum=psum, tag=tag)
        Wv = W.rearrange("(kc p) n -> p kc n", p=128)
        n_units = len(units)

        def load_w(u):
            s = u % 2
            c0 = units[u]['col0']
            for j in range(4):
                k.dma('sp', [], [f"{tag}_wst{s}"],
                      out=wst[s][:, j * 8:(j + 1) * 8, :], in_=Wv[:, j * 8:(j + 1) * 8, c0:c0 + FW],
                      key=f"{tag}_wst{s}")

        def conv_w(u):
            s = u % 2
            for j in range(4):
                sl = slice(j * 8, (j + 1) * 8)
                eng = ('pool', 'pool', 'dve', 'act')[j]
                if gk is not None:
                    if eng == 'act':
                        for kc in range(j * 8, (j + 1) * 8):
                            k.op('act', [f"{tag}_wst{s}", 'c_gk'], [f"{tag}_wbf{s}"],
                                 lambda e: e.activation(out=wbf[s][:, kc, :], in_=wst[s][:, kc, :], func=AF.Copy,
                                                        scale=gk[:, kc:kc + 1]))
                    else:
                        k.op(eng, [f"{tag}_wst{s}", 'c_gk'], [f"{tag}_wbf{s}"],
                             lambda e: e.tensor_tensor(out=wbf[s][:, sl, :], in0=wst[s][:, sl, :],
                                                       in1=gk[:, sl].unsqueeze(2).to_broadcast([128, 8, FW]),
                                                       op=ALU.mult))
                elif eng == 'act':
                    k.op('act', [f"{tag}_wst{s}"], [f"{tag}_wbf{s}"],
                         lambda e: e.activation(out=wbf[s][:, sl, :], in_=wst[s][:, sl, :], func=AF.Copy))
                else:
                    k.op(eng, [f"{tag}_wst{s}"], [f"{tag}_wbf{s}"],
                         lambda e: e.tensor_copy(out=wbf[s][:, sl, :], in_=wst[s][:, sl, :]))

        def mm(u):
            s = u % 2
            fns = []
            if units[u]['orient'] == 'F':
                for ft in range(2):
                    for hf in range(2):
                        bank = ft * 2 + hf
                        for kc in range(KC):
                            fns.append(lambda e, ft=ft, hf=hf, kc=kc, bank=bank: e.matmul(
                                psum[s][:, bank * 512:(bank + 1) * 512],
                                lhsT=wbf[s][:, kc, ft * 128:(ft + 1) * 128],
                                rhs=actT[:, kc, hf * 512:(hf + 1) * 512],
                                start=(kc == 0), stop=(kc == KC - 1)))
            else:
                for tt in range(8):
                    for kc in range(KC):
                        fns.append(lambda e, tt=tt, kc=kc: e.matmul(
                            psum[s][:, tt * FW:(tt + 1) * FW],
                            lhsT=actT[:, kc, tt * 128:(tt + 1) * 128],
                            rhs=wbf[s][:, kc, :],
                            start=(kc == 0), stop=(kc == KC - 1)))
            k.pe([f"{tag}_wbf{s}", f"{tag}_actT"], [f"{tag}_ps{s}"], fns)

        load_w(0)
        if act_src[0] == 'load':
            aT = act_src[1].rearrange("(kc p) t -> p kc t", p=128)
            for j in range(4):
                k.dma('sp', list(act_src[2]), [f"{tag}_actT"],
                      out=actT[:, j * 8:(j + 1) * 8, :], in_=aT[:, j * 8:(j + 1) * 8, :],
                      key=f"{tag}_actT")
        elif act_src[0] == 'load2':
            cands, keys, sel = act_src[1], list(act_src[2]), act_src[3]
            tmpb = st['tmpb']
            for j in range(4):
                sl = slice(j * 8, (j + 1) * 8)
                k.dma('sp', keys, [f"{tag}_actT"], out=actT[:, sl, :], in_=cands[0](j), key=f"{tag}_actT")
                k.dma('sp', keys, [f"{tag}_tmpb"], out=tmpb[:], in_=cands[1](j), key=f"{tag}_tmpb")
                k.op('pool', [f"{tag}_actT", 'c_sel'], [f"{tag}_actT"],
                     lambda e: e.tensor_scalar(out=actT[:, sl, :], in0=actT[:, sl, :], scalar1=sel[:, 0:1], scalar2=None,
                                               op0=ALU.mult))
                k.op('dve', [f"{tag}_actT", f"{tag}_tmpb", 'c_sel'], [f"{tag}_actT"],
                     lambda e: e.scalar_tensor_tensor(out=actT[:, sl, :], in0=tmpb[:], scalar=sel[:, 1:2],
                                                      in1=actT[:, sl, :], op0=ALU.mult, op1=ALU.add))
        else:
            x_ap = act_src[1]
            xkeys = list(act_src[2])
            ident = consts['ident']
            epst = consts['eps']
            xin = [wst[1][:, 0:16, :], wst[1][:, 16:32, :]]
            hn = sb(nc, es, f"{tag}_hn", [128, D], BF16)
            ss = sb(nc, es, f"{tag}_ss", [128, 8], F32)
            rstd = sb(nc, es, f"{tag}_rstd", [128, 8], F32)
            k.op('dve', [], [f"{tag}_ss{t}" for t in range(8)], lambda e: e.memset(ss[:], 0.0))
            for tt in range(8):
                j = tt % 2
                xv = xin[j].rearrange("p a b -> p (a b)")
                k.dma('sp', xkeys, [f"{tag}_xin{j}"], out=xv, in_=x_ap[tt * 128:(tt + 1) * 128, :],
                      key=f"{tag}_xin{j}")
                k.op('act', [f"{tag}_xin{j}"], [f"{tag}_hn", f"{tag}_ss{tt}"],
                     lambda e: e.activation(out=hn[:], in_=xv, func=AF.Square,
                                            accum_out=ss[:, tt:tt + 1]))
                k.op('act', [f"{tag}_ss{tt}", 'c_eps'], [f"{tag}_ss{tt}"],
                     lambda e: e.activation(out=ss[:, tt:tt + 1], in_=ss[:, tt:tt + 1], func=AF.Sqrt,
                                            bias=epst[:, 0:1], scale=1.0 / D))
                k.op('dve', [f"{tag}_ss{tt}"], [f"{tag}_rstd{tt}"],
                     lambda e: e.reciprocal(out=rstd[:, tt:tt + 1], in_=ss[:, tt:tt + 1]))
                k.op('act', [f"{tag}_xin{j}", f"{tag}_rstd{tt}"], [f"{tag}_hn"],
                     lambda e: e.activation(out=hn[:], in_=xv, func=AF.Copy,
                                            scale=rstd[:, tt:tt + 1]))
                for g in range(4):
                    s = g % 2
                    fns = []
                    for i in range(8):
                        kc = g * 8 + i
                        fns.append(lambda e, i=i, kc=kc, s=s: e.matmul(
                            psum[s][:, i * 128:(i + 1) * 128],
                            lhsT=hn[:, kc * 128:(kc + 1) * 128], rhs=ident[:],
                            start=True, stop=True))
                    k.pe([f"{tag}_hn", 'c_ident'], [f"{tag}_ps{s}"], fns)
                    src = psum[s][:, 0:1024].rearrange("p (a b) -> p a b", a=8)
                    dst = actT[:, g * 8:(g + 1) * 8, tt * 128:(tt + 1) * 128]
                    if g % 2 == 0:
                        k.op('dve', [f"{tag}_ps{s}"], [f"{tag}_actT"],
                             lambda e: e.tensor_copy(out=dst, in_=src))
                    else:
                        k.op('act', [f"{tag}_ps{s}"], [f"{tag}_actT"],
                             lambda e: e.activation(out=dst, in_=src, func=AF.Copy))
            k.readers.setdefault(f"{tag}_wst1", [])
            for j in range(2):
                k.readers[f"{tag}_wst1"] += k.readers.get(f"{tag}_xin{j}", []) + list(k.lastw.get(f"{tag}_xin{j}", {}).values())

        conv_w(0)
        if n_units > 1:
            load_w(1)
        prefetch(nc, k, st, 0, units[0])
        for u in range(n_units):
            if u + 1 < n_units:
                prefetch(nc, k, st, u + 1, units[u + 1])
            mm(u)
            if u + 1 < n_units:
                conv_w(u + 1)
            if u + 2 < n_units:
                load_w(u + 2)
            evac_unit(nc, k, st, u, units[u])


def prefetch(nc, k, st, u, unit):
    tag = st['tag']
    s = u % 2
    kind = unit['kind']
    if kind == 'residT':
        c0 = unit['col0']
        src = unit['res'][:, c0:c0 + FW].rearrange("(tt p) c -> p tt c", p=128)
        k.dma('sp', list(unit.get('res_keys', [])), [f"{tag}_outR{s}"], out=st['outR'][s][:], in_=src,
              key=f"{tag}_xres{s}")
    elif kind == 'gluF':
        c0 = unit['col0']
        src = unit['sg'][c0:c0 + FW, :].rearrange("(ft p) t -> p ft t", p=128)
        k.dma('sp', list(unit.get('sg_keys', [])), [f"{tag}_sgs{s}"], out=st['sgs'][s][:], in_=src,
              key=f"{tag}_sgs{s}")


def evac_unit(nc, k, st, u, unit, part='all'):
    tag = st['tag']
    s = u % 2
    kind = unit['kind']
    P = st['psum'][s]
    pk = f"{tag}_ps{s}"
    if kind in ('copyF', 'siluF', 'qkF', 'gluF', 'vT'):
        oF = st['outF'][s]
        ok = f"{tag}_outF{s}"
    if kind == 'copyF' or kind == 'siluF':
        for ft in range(2):
            for hf in range(2):
                b = ft * 2 + hf
                src = P[:, b * 512:(b + 1) * 512]
                dst = oF[:, ft, hf * 512:(hf + 1) * 512]
                if kind == 'siluF':
                    k.op('act', [pk], [ok], lambda e: e.activation(out=dst, in_=src, func=AF.Silu))
                elif b % 2 == 0:
                    k.op('act', [pk], [ok], lambda e: e.activation(out=dst, in_=src, func=AF.Copy))
                else:
                    k.op('dve', [pk], [ok], lambda e: e.tensor_copy(out=dst, in_=src))
    elif kind == 'qkF':
        qraw, sq, rt = st['qraw'], st['sq'], st['rt']
        ones, epst, gcol = st['ones'], st['eps'], unit['g']
        if part in ('all', 'early'):
            for b in range(4):
                src = P[:, b * 512:(b + 1) * 512]
                k.op('act', [pk], [f"{tag}_qraw{b}"],
                     lambda e: e.activation(out=qraw[:, b, :], in_=src, func=AF.Copy))
                k.op('act', [pk], [f"{tag}_sq{b}"],
                     lambda e: e.activation(out=sq[:, b, :], in_=src, func=AF.Square))
        if part == 'early':
            return
        for ft in range(2):
            for hf in range(2):
                b = ft * 2 + hf
                k.pe([f"{tag}_sq{b}", 'c_ones'], [pk],
                     [lambda e, b=b: e.matmul(P[:, b * 512:(b + 1) * 512], lhsT=ones[:], rhs=sq[:, b, :],
                                              start=True, stop=True)])
            for hf in range(2):
                b = ft * 2 + hf
                src = P[:, b * 512:(b + 1) * 512]
                k.op('act', [pk, 'c_eps'], [f"{tag}_rt{hf}"],
                     lambda e: e.activation(out=rt[:, hf, :], in_=src, func=AF.Sqrt, bias=epst[:, 0:1],
                                            scale=1.0 / 128))
                k.op('dve', [f"{tag}_rt{hf}"], [f"{tag}_rt{hf}"],
                     lambda e: e.reciprocal(out=rt[:, hf, :], in_=rt[:, hf, :]))
                k.op('dve', [f"{tag}_qraw{b}", f"{tag}_rt{hf}", 'c_qg', 'c_kg'], [ok],
                     lambda e: e.scalar_tensor_tensor(out=oF[:, ft, hf * 512:(hf + 1) * 512], in0=qraw[:, b, :],
                                                      scalar=gcol, in1=rt[:, hf, :], op0=ALU.mult, op1=ALU.mult))
    elif kind == 'gluF':
        sig, sgs, actT, bias = st['sig'], st['sgs'][s], st['actT'], st['glub']
        for b in range(4):
            ft, hf = b // 2, b % 2
            fidx = unit['col0'] // 128 + ft
            src = P[:, b * 512:(b + 1) * 512]
            k.op('act', [pk, 'c_glub'], [f"{tag}_sig{b}"],
                 lambda e: e.activation(out=sig[:, b, :], in_=src, func=AF.Sigmoid, bias=bias[:, fidx:fidx + 1]))
            k.op('dve', [f"{tag}_sig{b}", f"{tag}_actT"], [f"{tag}_sig{b}"],
                 lambda e: e.tensor_tensor(out=sig[:, b, :], in0=sig[:, b, :],
                                           in1=actT[:, fidx, hf * 512:(hf + 1) * 512], op=ALU.mult))
            k.op('dve', [f"{tag}_sig{b}", f"{tag}_sgs{s}"], [ok],
                 lambda e: e.tensor_tensor(out=oF[:, ft, hf * 512:(hf + 1) * 512], in0=sig[:, b, :],
                                           in1=sgs[:, ft, hf * 512:(hf + 1) * 512], op=ALU.mult))
    if kind in ('copyF', 'siluF', 'qkF', 'gluF'):
        r0 = unit['row0']
        dst = unit['dst'][r0:r0 + FW, :].rearrange("(ft p) t -> p ft t", p=128)
        k.dma('act', [ok], [unit['dst_key']], out=dst, in_=oF[:], key=ok)
    elif kind == 'vT':
        oT = oF[:].rearrange("p a (b c) -> p (a b) c", b=4)
        for h in range(2):
            src = P[:, h * 1024:(h + 1) * 1024].rearrange("p (a b) -> p a b", a=4)
            dst = oT[:, h * 4:(h + 1) * 4, :]
            if h == 0:
                k.op('act', [pk], [ok], lambda e: e.activation(out=dst, in_=src, func=AF.Copy))
            else:
                k.op('dve', [pk], [ok], lambda e: e.tensor_copy(out=dst, in_=src))
        r0 = unit['row0']
        dst = unit['dst'][:, r0:r0 + FW].rearrange("(tt p) c -> p tt c", p=128)
        k.dma('act', [ok], [unit['dst_key']], out=dst, in_=oT, key=ok)
    elif kind == 'residT':
        oR = st['outR'][s]
        ok = f"{tag}_outR{s}"
        xr = oR
        for h in range(2):
            src = P[:, h * 1024:(h + 1) * 1024].rearrange("p (a b) -> p a b", a=4)
            k.op('dve', [pk], [ok],
                 lambda e: e.tensor_tensor(out=oR[:, h * 4:(h + 1) * 4, :], in0=src,
                                           in1=xr[:, h * 4:(h + 1) * 4, :], op=ALU.add))
        c0 = unit['col0']
        dst = unit['dst'][:, c0:c0 + FW].rearrange("(tt p) c -> p tt c", p=128)
        k.dma('act', [ok], [unit['dst_key']], out=dst, in_=oR[:], key=ok, final=unit.get('final', False))


def phase_A(nc, k, io, cst):
    def alloc(nc_, es):
        st = {}
        st['outF'] = [sb(nc, es, f"A_outF{i}", [128, 2, NTOK], BF16) for i in range(2)]
        st['qraw'] = sb(nc, es, "A_qraw", [128, 4, 512], F32)
        st['sq'] = sb(nc, es, "A_sq", [128, 4, 512], BF16)
        st['rt'] = sb(nc, es, "A_rt", [128, 2, 512], F32)
        st['ones'] = cst['ones']
        st['eps'] = cst['eps']
        return st
    units = []
    for c0 in range(0, 4096, FW):
        units.append(dict(col0=c0, kind='qkF', orient='F', g=cst['qg'][:, 0:1], dst=io['qT'], row0=c0, dst_key='qT'))
    for c0 in range(0, 4096, FW):
        units.append(dict(col0=4096 + c0, kind='qkF', orient='F', g=cst['kg'][:, 0:1], dst=io['kT'], row0=c0, dst_key='kT'))
    for c0 in range(0, 4096, FW):
        units.append(dict(col0=8192 + c0, kind='vT', orient='T', dst=io['v'], row0=c0, dst_key='v'))
    for c0 in range(0, 4096, FW):
        units.append(dict(col0=12288 + c0, kind='siluF', orient='F', dst=io['sgT'], row0=c0, dst_key='sgT'))
    if io.get('limit_units'):
        units = [units[i] for i in io['limit_units']]
    dense_phase(nc, k, 'A', W=io['w_in'], units=units, act_src=('rms', io['x'], io.get('x_keys', [])),
                gk=cst['gk0'], consts=cst, extra_alloc=alloc)


def phase_B(nc, k, io, cst, heads=range(32), stage=99):
    SC = 1.0 / float(np.sqrt(128.0))
    ident = cst['ident']
    negm = cst['negm']
    zcol = cst['zcol']
    with ExitStack() as es:
        kTh = [sb(nc, es, f"B_k{i}", [128, 8, 2, 128], BF16) for i in range(2)]
        vh = [sb(nc, es, f"B_v{i}", [128, 8, 2, 128], BF16) for i in range(2)]
        qTh = [sb(nc, es, f"B_q{i}", [128, NTOK], BF16) for i in range(2)]
        sgh = [sb(nc, es, f"B_sg{i}", [128, NTOK], BF16) for i in range(2)]
        goh = [sb(nc, es, f"B_go{i}", [128, NTOK], BF16) for i in range(2)]
        bt = [sb(nc, es, f"B_bt{i}", [128, 2048], F32) for i in range(3)]
        ob = [sb(nc, es, f"B_ob{i}", [128, 2048], F32) for i in range(3)]
        inc = [sb(nc, es, f"B_inc{i}", [128, 2056], F32) for i in range(2)]
        wq = [sb(nc, es, f"B_wq{i}", [128, 2048], BF16) for i in range(2)]
        wT = [sb(nc, es, f"B_wT{i}", [128, 16, 128], BF16) for i in range(2)]
        zm = [sb(nc, es, f"B_zm{i}", [128, 256], F32) for i in range(3)]
        psz = ps(nc, es, "B_psz", [128, 2048])
        pst = [ps(nc, es, f"B_pst{i}", [128, 512]) for i in range(2)]
        pso = ps(nc, es, "B_pso", [128, 1024])
        kT_rows = io.get('kT_rows') or (lambda r, h: io['kT_all'][r, h * 128:(h + 1) * 128, :])
        v_parts = io.get('v_parts') or (lambda r, h: [(0, 8, io['v_all'][r, :, h * 128:(h + 1) * 128])])

        def load_head(h, hb):
            for r in range(2):
                k.dma('sp', ['kT_all'], [f"B_k{hb}"], out=kTh[hb][:, :, r, :],
                      in_=kT_rows(r, h).rearrange("d (m i) -> d m i", i=128), key=f"B_k{hb}")
                for m0, nm, vap in v_parts(r, h):
                    k.dma('sp', ['v_all'], [f"B_v{hb}"], out=vh[hb][:, m0:m0 + nm, r, :],
                          in_=vap.rearrange("(m p) d -> p m d", p=128), key=f"B_v{hb}")
            k.dma('sp', ['qT'], [f"B_q{hb}"], out=qTh[hb][:], in_=io['qT'][h * 128:(h + 1) * 128, :], key=f"B_q{hb}")
            k.dma('sp', ['sgT'], [f"B_sg{hb}"], out=sgh[hb][:], in_=io['sgT'][h * 128:(h + 1) * 128, :], key=f"B_sg{hb}")

        heads = list(heads)
        blocks = [(hi, m) for hi in range(len(heads)) for m in range(8)]
        tcnt = [0]

        def stage1(bi):
            hi, m = blocks[bi]
            hb, mb = hi % 2, bi % 3
            kl = 256 * (m + 1)
            kflat = kTh[hb][:].rearrange("p m r i -> p (m r i)")
            fns = []
            for c0 in range(0, kl, 512):
                c1 = min(kl, c0 + 512)
                fns.append(lambda e, c0=c0, c1=c1: e.matmul(psz[:, c0:c1], lhsT=qTh[hb][:, m * 128:(m + 1) * 128],
                                                            rhs=kflat[:, c0:c1], start=True, stop=True))
            k.pe([f"B_q{hb}", f"B_k{hb}"], ["B_psz"], fns)
            k.op('dve', ["B_psz", 'c_negm'], [f"B_zm{mb}"],
                 lambda e: e.tensor_tensor(out=zm[mb][:], in0=psz[:, kl - 256:kl], in1=negm[:], op=ALU.add))
            for c0 in range(0, kl - 256, 512):
                c1 = min(kl - 256, c0 + 512)
                k.op('act', ["B_psz", f"B_zm{mb}"], [f"B_bt{mb}"],
                     lambda e: e.activation(out=bt[mb][:, c0:c1], in_=psz[:, c0:c1], func=AF.Sigmoid, scale=SC))
                k.op('act', ["B_psz", f"B_zm{mb}"], [f"B_ob{mb}"],
                     lambda e: e.activation(out=ob[mb][:, c0:c1], in_=psz[:, c0:c1], func=AF.Sigmoid, scale=-SC))
            k.op('act', [f"B_zm{mb}"], [f"B_bt{mb}"],
                 lambda e: e.activation(out=bt[mb][:, kl - 256:kl], in_=zm[mb][:], func=AF.Sigmoid, scale=SC))
            k.op('act', [f"B_zm{mb}"], [f"B_ob{mb}"],
                 lambda e: e.activation(out=ob[mb][:, kl - 256:kl], in_=zm[mb][:], func=AF.Sigmoid, scale=-SC))

        def stage2(bi):
            hi, m = blocks[bi]
            mb, m3 = bi % 2, bi % 3
            kl = 256 * (m + 1)
            k.op('dve', [], [f"B_inc{mb}"], lambda e: e.memset(inc[mb][:, kl:kl + 1], 1.0))
            ob_rev = bass.AP(ob[m3][:].tensor, ob[m3][:, kl - 1:kl].offset, [list(ob[m3][:].ap[0]), [-1, kl]])
            inc_rev = bass.AP(inc[mb][:].tensor, inc[mb][:, kl - 1:kl].offset, [list(inc[mb][:].ap[0]), [-1, kl]])
            k.op('dve', [f"B_ob{m3}", 'c_zcol', f"B_inc{mb}"], [f"B_inc{mb}"],
                 lambda e: e.tensor_tensor_scan(out=inc_rev, data0=zcol[:, 0:1].to_broadcast([128, kl]),
                                                data1=ob_rev, initial=1.0, op0=ALU.add, op1=ALU.mult))
            for c0 in range(0, kl, 1024):
                c1 = min(kl, c0 + 1024)
                k.op('dve', [f"B_bt{m3}", f"B_inc{mb}"], [f"B_wq{mb}"],
                     lambda e: e.tensor_tensor(out=wq[mb][:, c0:c1], in0=bt[m3][:, c0:c1], in1=inc[mb][:, c0 + 1:c1 + 1], op=ALU.mult))

        def stage3(bi):
            hi, m = blocks[bi]
            hb, mb = hi % 2, bi % 2
            nkb = 2 * (m + 1)
            vflat = vh[hb][:].rearrange("p m r d -> p (m r) d")
            for g0 in range(0, nkb, 4):
                g1 = min(nkb, g0 + 4)
                tb = tcnt[0] % 2
                tcnt[0] += 1
                fns = [lambda e, kb=kb, tb=tb, g0=g0: e.matmul(pst[tb][:, (kb - g0) * 128:(kb - g0 + 1) * 128],
                                                               lhsT=wq[mb][:, kb * 128:(kb + 1) * 128], rhs=ident[:],
                                                               start=True, stop=True) for kb in range(g0, g1)]
                k.pe([f"B_wq{mb}", 'c_ident'], [f"B_pst{tb}"], fns)
                src = pst[tb][:, 0:(g1 - g0) * 128].rearrange("p (a b) -> p a b", b=128)
                dst = wT[mb][:, g0:g1, :]
                k.op('act', [f"B_pst{tb}"], [f"B_wT{mb}"], lambda e: e.activation(out=dst, in_=src, func=AF.Copy))
            ob_ = (bi % 2) * 512
            fns = [lambda e, kb=kb: e.matmul(pso[:, ob_:ob_ + 128], lhsT=vflat[:, kb, :], rhs=wT[mb][:, kb, :],
                                             start=(kb == 0), stop=(kb == nkb - 1)) for kb in range(nkb)]
            k.pe([f"B_v{hb}", f"B_wT{mb}"], [f"B_pso{bi % 2}"], fns)
            k.op('dve', [f"B_pso{bi % 2}", f"B_sg{hb}"], [f"B_go{hb}"],
                 lambda e: e.tensor_tensor(out=goh[hb][:, m * 128:(m + 1) * 128], in0=pso[:, ob_:ob_ + 128],
                                           in1=sgh[hb][:, m * 128:(m + 1) * 128], op=ALU.mult))
            if m == 7:
                h = heads[hi]
                k.dma('act', [f"B_go{hb}"], ['goT'], out=io['goT'][h * 128:(h + 1) * 128, :], in_=goh[hb][:], key=f"B_go{hb}")

        load_head(heads[0], 0)
        if len(heads) > 1:
            load_head(heads[1], 1)
        nb = len(blocks)
        stage1(0)
        if nb > 1:
            stage1(1)
        stage2(0)
        for bi in range(nb):
            if bi + 2 < nb:
                stage1(bi + 2)
            if bi + 1 < nb:
                stage2(bi + 1)
            stage3(bi)
            hi0, m0 = blocks[bi]
            if m0 == 7 and hi0 + 2 < len(heads):
                load_head(heads[hi0 + 2], hi0 % 2)


def phase_C(nc, k, io, cst):
    def alloc(nc_, es):
        return dict(outR=[sb(nc, es, f"C_outR{i}", [128, 8, FW], F32) for i in range(2)])
    units = [dict(col0=c0, kind='residT', orient='T', res=io['x'], dst=io['x1'], dst_key='x1', final=True)
             for c0 in range(0, 4096, FW)]
    dense_phase(nc, k, 'C', W=io['w_out'], units=units, act_src=('load', io['goT'], ['goT']),
                consts=cst, extra_alloc=alloc)


def _consts(nc, k, es, names, cin):
    return {n: load_const(nc, k, es, "c_" + {"gk0": "gk"}.get(n, n), cin[n], list(cin[n].shape), cin[n].dtype)
            for n in names}


def build_launch1():
    nc = bass.Bass("TRN2", target_bir_lowering=False)
    k = K(nc)
    dt = lambda n, s, d, kind: nc.dram_tensor(n, s, d, kind=kind).ap()
    io = dict(x=dt("x", [NTOK, D], F32, "ExternalInput"), w_in=dt("w_in", [D, 4 * D], F32, "ExternalInput"),
              qT=dt("qT", [D, NTOK], BF16, "ExternalOutput"), kT=dt("kT", [D, NTOK], BF16, "ExternalOutput"),
              v=dt("v", [NTOK, D], BF16, "ExternalOutput"), sgT=dt("sgT", [D, NTOK], BF16, "ExternalOutput"))
    cin = dict(gk0=dt("gk0", [128, 32], F32, "ExternalInput"), qg=dt("qg", [128, 1], F32, "ExternalInput"),
               kg=dt("kg", [128, 1], F32, "ExternalInput"), ident=dt("ident", [128, 128], BF16, "ExternalInput"),
               ones=dt("ones", [128, 128], BF16, "ExternalInput"), eps=dt("eps", [128, 1], F32, "ExternalInput"))
    with ExitStack() as es:
        cst = _consts(nc, k, es, list(cin), cin)
        phase_A(nc, k, io, cst)
        k.finish(['qT', 'kT', 'v', 'sgT'])
    return nc


def phase_D(nc, k, io, cst):
    def alloc(nc_, es):
        return dict(outF=[sb(nc, es, f"D_outF{i}", [128, 2, NTOK], BF16) for i in range(2)])
    units = [dict(col0=c0, kind='copyF', orient='F', dst=io['uT'], row0=c0, dst_key='uT') for c0 in range(0, D, FW)]
    units += [dict(col0=D + c0, kind='siluF', orient='F', dst=io['sg1T'], row0=c0, dst_key='sg1T') for c0 in range(0, D, FW)]
    dense_phase(nc, k, 'D', W=io['ssm_w_in'], units=units, act_src=('rms', io['x1'], ['x1']),
                gk=cst['gk1'], consts=cst, extra_alloc=alloc)


def phase_F(nc, k, io, cst):
    def alloc(nc_, es):
        return dict(outF=[sb(nc, es, f"F_outF{i}", [128, 2, NTOK], BF16) for i in range(2)],
                    sgs=[sb(nc, es, f"F_sgs{i}", [128, 2, NTOK], BF16) for i in range(2)],
                    sig=sb(nc, es, "F_sig", [128, 4, 512], F32), glub=cst['glub'],
                    **({'tmpb': sb(nc, es, "F_tmpb", [128, 8, NTOK], BF16)} if 'yg_cands' in io else {}))
    units = [dict(col0=c0, kind='gluF', orient='F', sg=io['sg1T'], sg_keys=['sg1T'], dst=io['mT'], row0=c0, dst_key='mT')
             for c0 in range(0, D, FW)]
    src = ('load2', io['yg_cands'], ['yG'], cst['sel']) if 'yg_cands' in io else ('load', io['ygT'], [])
    dense_phase(nc, k, 'F', W=io['glu_w'], units=units, act_src=src, consts=cst, extra_alloc=alloc)


def phase_G(nc, k, io, cst):
    def alloc(nc_, es):
        return dict(outR=[sb(nc, es, f"G_outR{i}", [128, 8, FW], F32) for i in range(2)])
    units = [dict(col0=c0, kind='residT', orient='T', res=io['x1'], res_keys=['x1'], dst=io['out'], dst_key='out', final=True)
             for c0 in range(0, D, FW)]
    dense_phase(nc, k, 'G', W=io['ssm_w_out'], units=units, act_src=('load', io['mT'], ['mT']), consts=cst,
                extra_alloc=alloc)


def build_launch2():
    nc = bass.Bass("TRN2", target_bir_lowering=False)
    k = K(nc)
    dt = lambda n, s, d, kind: nc.dram_tensor(n, s, d, kind=kind).ap()
    io = dict(qT=dt("qT", [D, NTOK], BF16, "ExternalInput"), kT_all=dt("kT_all", [2, D, NTOK], BF16, "ExternalInput"),
              v_all=dt("v_all", [2, NTOK, D], BF16, "ExternalInput"), sgT=dt("sgT", [D, NTOK], BF16, "ExternalInput"),
              goT=dt("goT", [D, NTOK], BF16, "Internal"),
              x=dt("x", [NTOK, D], F32, "ExternalInput"), w_out=dt("w_out", [D, D], F32, "ExternalInput"),
              x1=dt("x1", [NTOK, D], F32, "ExternalOutput"),
              ssm_w_in=dt("ssm_w_in", [D, 2 * D], F32, "ExternalInput"),
              uT=dt("uT", [D, NTOK], BF16, "ExternalOutput"), sg1T=dt("sg1T", [D, NTOK], BF16, "ExternalOutput"))
    cin = dict(ident=dt("ident", [128, 128], BF16, "ExternalInput"), negm=dt("negm", [128, 256], F32, "ExternalInput"),
               zcol=dt("zcol", [128, 1], F32, "ExternalInput"), eps=dt("eps", [128, 1], F32, "ExternalInput"),
               gk1=dt("gk1", [128, 32], F32, "ExternalInput"))
    with ExitStack() as es:
        cst = _consts(nc, k, es, list(cin), cin)
        phase_B(nc, k, io, cst)
        k.barrier()
        phase_C(nc, k, io, cst)
        k.barrier()
        phase_D(nc, k, io, cst)
        k.finish(['x1', 'uT', 'sg1T'])
    return nc


def build_launch3():
    nc = bass.Bass("TRN2", target_bir_lowering=False)
    k = K(nc)
    dt = lambda n, s, d, kind: nc.dram_tensor(n, s, d, kind=kind).ap()
    io = dict(uT_all=dt("uT_all", [2, 2048, NTOK], BF16, "ExternalInput"), ygT=dt("ygT", [2, 2048, NTOK], BF16, "ExternalOutput"))
    for n in ('Bp_re', 'Bp_im', 'Cp_re', 'Cp_im'):
        io[n] = dt(n, [128, 64, 128], F32, "ExternalInput")
    io['Ddiag'] = dt('Ddiag', [128, 16, 128], F32, "ExternalInput")
    for n in ('A_re_T', 'A_im_T', 'logdt_T'):
        io[n] = dt(n, [128, 64], F32, "ExternalInput")
    cin = dict(iota=dt("iota", [128, 2048], F32, "ExternalInput"), halfpi=dt("halfpi", [128, 1], F32, "ExternalInput"),
               zcol=dt("zcol", [128, 1], F32, "ExternalInput"))
    with ExitStack() as es:
        cst = _consts(nc, k, es, list(cin), cin)
        phase_E(nc, k, io, cst)
        k.finish(['ygT'])
    return nc


def build_launch4():
    nc = bass.Bass("TRN2", target_bir_lowering=False)
    k = K(nc)
    dt = lambda n, s, d, kind: nc.dram_tensor(n, s, d, kind=kind).ap()
    io = dict(ygT=dt("ygT_own", [D, NTOK], BF16, "ExternalInput"), sg1T=dt("sg1T", [D, NTOK], BF16, "ExternalInput"),
              glu_w=dt("glu_w", [D, D], F32, "ExternalInput"), mT=dt("mT", [D, NTOK], BF16, "Internal"),
              ssm_w_out=dt("ssm_w_out", [D, D], F32, "ExternalInput"), x1=dt("x1", [NTOK, D], F32, "ExternalInput"),
              out=dt("out", [NTOK, D], F32, "ExternalOutput"))
    cin = dict(glub=dt("glub", [128, 32], F32, "ExternalInput"))
    with ExitStack() as es:
        cst = _consts(nc, k, es, list(cin), cin)
        phase_F(nc, k, io, cst)
        k.barrier()
        phase_G(nc, k, io, cst)
        k.finish(['out'])
    return nc


PAIRS = [[0, 1], [2, 3], [4, 5], [6, 7]]


def build_fused():
    nc = bass.Bass("TRN2", target_bir_lowering=False)
    k = K(nc)
    dt = lambda n, s, d, kind="Internal": nc.dram_tensor(n, s, d, kind=kind).ap()
    EI = "ExternalInput"
    io = dict(x=dt("x", [NTOK, D], F32, EI), w_in=dt("w_in", [D, 4 * D], F32, EI), w_out=dt("w_out", [D, D], F32, EI),
              ssm_w_in=dt("ssm_w_in", [D, 2 * D], F32, EI), glu_w=dt("glu_w", [D, D], F32, EI),
              ssm_w_out=dt("ssm_w_out", [D, D], F32, EI), out=dt("out", [NTOK, D], F32, "ExternalOutput"),
              qT=dt("qT", [D, NTOK], BF16), kT=dt("kT", [D, NTOK], BF16), v=dt("v", [NTOK, D], BF16),
              sgT=dt("sgT", [D, NTOK], BF16), goT=dt("goT", [D, NTOK], BF16), x1=dt("x1", [NTOK, D], F32),
              uT=dt("uT", [D, NTOK], BF16), sg1T=dt("sg1T", [D, NTOK], BF16), mT=dt("mT", [D, NTOK], BF16))
    kT_g = dt("kT_g", [2 * D, NTOK], BF16)
    v_g = dt("v_g", [2 * NTOK, D], BF16)
    u_g = dt("u_g", [2 * D, NTOK], BF16)
    yg_l = dt("yg_l", [D, NTOK], BF16)
    yg_g = dt("yg_g", [2 * D, NTOK], BF16)
    NCH = 4
    k4 = kT_g.rearrange("(c r f) t -> c r f t", c=NCH, r=2)
    v4 = v_g.rearrange("(c r t) f -> c r t f", c=NCH, r=2)
    u4 = u_g.rearrange("(c r f) t -> c r f t", c=NCH, r=2)
    y4 = yg_g.rearrange("(c r f) t -> c r f t", c=NCH, r=2)
    io['kT_rows'] = lambda r, h: k4[h // 8, r, (h % 8) * 128:(h % 8 + 1) * 128, :]
    io['v_parts'] = lambda r, h: [(2 * c, 2, v4[c, r, :, h * 128:(h + 1) * 128]) for c in range(NCH)]
    io['ygT'] = yg_l.rearrange("(d c) t -> d c t", d=2)
    io['u_cands'] = lambda h, r, ct: u4[(h * 2048 + ct * 128) // 1024, r,
                                        (h * 2048 + ct * 128) % 1024:(h * 2048 + ct * 128) % 1024 + 128, :]
    io['yg_cands'] = [(lambda j, d=d: y4[d * 2 + j % 2, j // 2, :, :].rearrange("(kc p) t -> p kc t", p=128))
                      for d in range(2)]

    def gather(src, dst, in_key, out_key, scr):
        rows = src.shape[0] // NCH
        for c in range(NCH):
            k.collective("AllGather", PAIRS, src[c * rows:(c + 1) * rows, :], dst[c * 2 * rows:(c + 1) * 2 * rows, :],
                         in_key, out_key, scr)
    for n in ('Bp_re', 'Bp_im', 'Cp_re', 'Cp_im'):
        io[n] = dt(n, [128, 64, 128], F32, EI)
    io['Ddiag'] = dt('Ddiag', [128, 16, 128], F32, EI)
    for n in ('A_re_T', 'A_im_T', 'logdt_T'):
        io[n] = dt(n, [128, 64], F32, EI)
    cin = dict(gk0=dt("gk0", [128, 32], F32, EI), gk1=dt("gk1", [128, 32], F32, EI), qg=dt("qg", [128, 1], F32, EI),
               kg=dt("kg", [128, 1], F32, EI), ident=dt("ident", [128, 128], BF16, EI), ones=dt("ones", [128, 128], BF16, EI),
               eps=dt("eps", [128, 1], F32, EI), negm=dt("negm", [128, 256], F32, EI), zcol=dt("zcol", [128, 1], F32, EI),
               halfpi=dt("halfpi", [128, 1], F32, EI), glub=dt("glub", [128, 32], F32, EI), sel=dt("sel", [128, 2], F32, EI))
    iota_d = dt("iota", [128, 2048], F32, EI)
    with ExitStack() as es:
        cst = {n: load_const(nc, k, es, "c_" + n, cin[n], list(cin[n].shape), cin[n].dtype) for n in cin}
        scr = sb(nc, es, "cc_scr", [128, 8], F32)
        phase_A(nc, k, io, cst)
        k.barrier()
        gather(io['kT'], kT_g, 'kT', 'kT_all', scr[:])
        gather(io['v'], v_g, 'v', 'v_all', scr[:])
        phase_B(nc, k, io, cst)
        k.barrier()
        phase_C(nc, k, io, cst)
        k.barrier()
        phase_D(nc, k, io, cst)
        k.barrier()
        gather(io['uT'], u_g, 'uT', 'uT_all', scr[:])
        with ExitStack() as es2:
            cst_e = dict(cst)
            cst_e['iota'] = load_const(nc, k, es2, "c_iota", iota_d, [128, 2048], F32)
            phase_E(nc, k, io, cst_e)
            k.barrier()
        gather(yg_l, yg_g, 'ygT', 'yG', scr[:])
        phase_F(nc, k, io, cst)
        k.barrier()
        phase_G(nc, k, io, cst)
        k.finish(['out'])
    return nc


def _negmask(r):
    t = np.arange(128)[:, None]
    j = np.arange(128)[None, :]
    diag = np.where(j < t, 0.0, -1000.0)
    full = np.zeros((128, 128))
    none = np.full((128, 128), -1000.0)
    return np.ascontiguousarray(np.concatenate([diag, none] if r == 0 else [full, diag], axis=1).astype(np.float32))


def kernel_unfused(x, norm_g, attn_w_in, attn_q_g, attn_k_g, attn_w_out, ssm_w_in, ssm_A_re, ssm_A_im, ssm_log_dt,
           ssm_B_re, ssm_B_im, ssm_C_re, ssm_C_im, ssm_D, ssm_glu_w, ssm_glu_b, ssm_w_out):
    f32 = lambda a: np.ascontiguousarray(np.asarray(a, np.float32))
    x = f32(x)
    ncores = 8
    cores = list(range(ncores))
    bf = lambda a: np.asarray(a, np.float32).astype(NPBF)
    ident = bf(np.eye(128))
    ones = bf(np.ones((128, 128)))
    eps = np.full((128, 1), EPS, np.float32)
    zcol = np.zeros((128, 1), np.float32)
    xo = [np.ascontiguousarray(x[c // 2].reshape(8, 2, 128, D)[:, c % 2].reshape(NTOK, D)) for c in cores]
    norm_g = f32(norm_g)
    gk = [np.ascontiguousarray(norm_g[i].reshape(32, 128).T) for i in range(2)]
    qg = f32(attn_q_g)[0].reshape(128, 1)
    kg = f32(attn_k_g)[0].reshape(128, 1)
    w_in = f32(attn_w_in)[0]
    in1 = [dict(x=xo[c], w_in=w_in, gk0=gk[0], qg=qg, kg=kg, ident=ident, ones=ones, eps=eps) for c in cores]
    r1 = run_bass_kernel_spmd(build_launch1(), in1, core_ids=cores).results
    del in1, w_in
    w_out = f32(attn_w_out)[0]
    sw_in = f32(ssm_w_in)[0]
    in2 = []
    for c in cores:
        p = (c // 2) * 2
        kT_all = np.ascontiguousarray(np.stack([np.asarray(r1[p]['kT']), np.asarray(r1[p + 1]['kT'])]))
        v_all = np.ascontiguousarray(np.stack([np.asarray(r1[p]['v']), np.asarray(r1[p + 1]['v'])]))
        in2.append(dict(qT=np.asarray(r1[c]['qT']), kT_all=kT_all, v_all=v_all, sgT=np.asarray(r1[c]['sgT']),
                        x=xo[c], w_out=w_out, ssm_w_in=sw_in, ident=ident, negm=_negmask(c % 2), zcol=zcol,
                        eps=eps, gk1=gk[1]))
    r2 = run_bass_kernel_spmd(build_launch2(), in2, core_ids=cores).results
    del in2, r1, w_out, sw_in
    if os.environ.get('KDEBUG'):
        np.save(os.environ['KDEBUG'] + '_x1.npy', np.stack([np.asarray(r2[c]['x1']) for c in range(2)]))
        np.save(os.environ['KDEBUG'] + '_uT.npy', np.stack([np.asarray(r2[c]['uT']).astype(np.float32) for c in range(2)]))
    lay = [ssm_layouts(r, f32(ssm_A_re)[0], f32(ssm_A_im)[0], f32(ssm_log_dt)[0], f32(ssm_B_re)[0], f32(ssm_B_im)[0],
                       f32(ssm_C_re)[0], f32(ssm_C_im)[0], f32(ssm_D)[0]) for r in range(2)]
    iota = np.ascontiguousarray(np.broadcast_to(np.arange(2048, dtype=np.float32), (128, 2048)))
    halfpi = np.full((128, 1), np.pi / 2, np.float32)
    in3 = []
    for c in cores:
        p, r = (c // 2) * 2, c % 2
        uT_all = np.ascontiguousarray(np.stack([np.asarray(r2[p + rr]['uT'])[2048 * r:2048 * (r + 1)] for rr in range(2)]))
        in3.append(dict(uT_all=uT_all, iota=iota, halfpi=halfpi, zcol=zcol, **lay[r]))
    r3 = run_bass_kernel_spmd(build_launch3(), in3, core_ids=cores).results
    del in3
    glu_w = f32(ssm_glu_w)[0]
    sw_out = f32(ssm_w_out)[0]
    glub = np.ascontiguousarray(f32(ssm_glu_b)[0].reshape(32, 128).T)
    in4 = []
    for c in cores:
        p, r = (c // 2) * 2, c % 2
        yg_own = np.ascontiguousarray(np.concatenate([np.asarray(r3[p + rr]['ygT'])[r] for rr in range(2)], axis=0))
        in4.append(dict(ygT_own=yg_own, sg1T=np.asarray(r2[c]['sg1T']), glu_w=glu_w, ssm_w_out=sw_out,
                        x1=np.asarray(r2[c]['x1']), glub=glub))
    r4 = run_bass_kernel_spmd(build_launch4(), in4, core_ids=cores).results
    out = np.zeros((4, 2048, D), np.float32)
    for c in cores:
        b, r = c // 2, c % 2
        out[b].reshape(8, 2, 128, D)[:, r] = np.asarray(r4[c]['out'], np.float32).reshape(8, 128, D)
    return out


def phase_E(nc, k, io, cst, pairs=range(64)):
    iota, halfpi, zc = cst['iota'], cst['halfpi'], cst['zcol']
    pairs = list(pairs)
    cts = sorted(set(q // 4 for q in pairs))
    with ExitStack() as es:
        T = lambda name, shape, dt=F32: sb(nc, es, "E_" + name, shape, dt)
        Are = T("Are", [128, 64]); Aim = T("Aim", [128, 64]); dtt = T("dt", [128, 64])
        xx = T("xx", [128, 64]); tt = T("tt", [128, 64]); rho = T("rho", [128, 64]); rm1 = T("rm1", [128, 64])
        th = T("th", [128, 64]); thf = T("thf", [128, 64]); fa1 = T("fa1", [128, 64])
        s1 = T("s1", [128, 64]); c1 = T("c1", [128, 64]); sh = T("sh", [128, 64])
        am1 = T("am1", [128, 64]); abi = T("abi", [128, 64]); den = T("den", [128, 64])
        fre = T("fre", [128, 64]); fim = T("fim", [128, 64]); t2 = T("t2", [128, 64])
        k.dma('sp', [], ['E_Are'], out=Are[:], in_=io['A_re_T'], key='E_Are')
        k.dma('sp', [], ['E_Aim'], out=Aim[:], in_=io['A_im_T'], key='E_Aim')
        k.dma('sp', [], ['E_dt'], out=dtt[:], in_=io['logdt_T'], key='E_dt')
        nm_ = lambda r: r if r.startswith('c_') else "E_" + r
        V = lambda reads, writes, fn: k.op('dve', [nm_(r) for r in reads], [nm_(w) for w in writes], fn)
        A = lambda reads, writes, fn: k.op('act', [nm_(r) for r in reads], [nm_(w) for w in writes], fn)
        Pl = lambda reads, writes, fn: k.op('pool', [nm_(r) for r in reads], [nm_(w) for w in writes], fn)
        A(['dt'], ['dt'], lambda e: e.activation(out=dtt[:], in_=dtt[:], func=AF.Exp))
        V(['Are', 'dt'], ['xx'], lambda e: e.tensor_tensor(out=xx[:], in0=Are[:], in1=dtt[:], op=ALU.mult))
        V(['xx'], ['tt'], lambda e: e.tensor_scalar(out=tt[:], in0=xx[:], scalar1=1.0 / 24, scalar2=1.0 / 6, op0=ALU.mult, op1=ALU.add))
        V(['tt', 'xx'], ['tt'], lambda e: e.tensor_tensor(out=tt[:], in0=tt[:], in1=xx[:], op=ALU.mult))
        V(['tt'], ['tt'], lambda e: e.tensor_scalar(out=tt[:], in0=tt[:], scalar1=0.5, scalar2=None, op0=ALU.add))
        V(['tt', 'xx'], ['tt'], lambda e: e.tensor_tensor(out=tt[:], in0=tt[:], in1=xx[:], op=ALU.mult))
        V(['tt'], ['tt'], lambda e: e.tensor_scalar(out=tt[:], in0=tt[:], scalar1=1.0, scalar2=None, op0=ALU.add))
        V(['tt', 'xx'], ['rm1'], lambda e: e.tensor_tensor(out=rm1[:], in0=tt[:], in1=xx[:], op=ALU.mult))
        V(['rm1'], ['rho'], lambda e: e.tensor_scalar(out=rho[:], in0=rm1[:], scalar1=1.0, scalar2=None, op0=ALU.add))
        V(['Aim', 'dt'], ['th'], lambda e: e.tensor_tensor(out=th[:], in0=Aim[:], in1=dtt[:], op=ALU.mult))
        V(['th'], ['t2'], lambda e: e.tensor_scalar(out=t2[:], in0=th[:], scalar1=1.0 / TWO_PI, scalar2=MAGIC, op0=ALU.mult, op1=ALU.add))
        V(['t2'], ['t2'], lambda e: e.tensor_scalar(out=t2[:], in0=t2[:], scalar1=MAGIC, scalar2=None, op0=ALU.subtract))
        V(['th', 't2'], ['thf'], lambda e: e.scalar_tensor_tensor(out=thf[:], in0=th[:], scalar=1.0 / TWO_PI, in1=t2[:], op0=ALU.mult, op1=ALU.subtract))
        V(['thf'], ['fa1'], lambda e: e.scalar_tensor_tensor(out=fa1[:], in0=thf[:], scalar=-1.0, in1=thf[:], op0=ALU.mult, op1=ALU.max))
        A(['thf'], ['s1'], lambda e: e.activation(out=s1[:], in_=thf[:], func=AF.Sin, scale=TWO_PI))
        A(['fa1', 'c_halfpi'], ['c1'], lambda e: e.activation(out=c1[:], in_=fa1[:], func=AF.Sin, scale=-TWO_PI, bias=halfpi[:, 0:1]))
        A(['thf'], ['sh'], lambda e: e.activation(out=sh[:], in_=thf[:], func=AF.Sin, scale=TWO_PI / 2))
        V(['sh'], ['sh'], lambda e: e.tensor_tensor(out=sh[:], in0=sh[:], in1=sh[:], op=ALU.mult))
        V(['rm1', 'c1'], ['am1'], lambda e: e.tensor_tensor(out=am1[:], in0=rm1[:], in1=c1[:], op=ALU.mult))
        V(['sh', 'am1'], ['am1'], lambda e: e.scalar_tensor_tensor(out=am1[:], in0=sh[:], scalar=-2.0, in1=am1[:], op0=ALU.mult, op1=ALU.add))
        V(['rho', 's1'], ['abi'], lambda e: e.tensor_tensor(out=abi[:], in0=rho[:], in1=s1[:], op=ALU.mult))
        V(['Are'], ['den'], lambda e: e.tensor_tensor(out=den[:], in0=Are[:], in1=Are[:], op=ALU.mult))
        V(['Aim'], ['t2'], lambda e: e.tensor_tensor(out=t2[:], in0=Aim[:], in1=Aim[:], op=ALU.mult))
        V(['den', 't2'], ['den'], lambda e: e.tensor_tensor(out=den[:], in0=den[:], in1=t2[:], op=ALU.add))
        V(['den'], ['den'], lambda e: e.reciprocal(out=den[:], in_=den[:]))
        V(['am1', 'Are'], ['fre'], lambda e: e.tensor_tensor(out=fre[:], in0=am1[:], in1=Are[:], op=ALU.mult))
        V(['abi', 'Aim'], ['t2'], lambda e: e.tensor_tensor(out=t2[:], in0=abi[:], in1=Aim[:], op=ALU.mult))
        V(['fre', 't2'], ['fre'], lambda e: e.tensor_tensor(out=fre[:], in0=fre[:], in1=t2[:], op=ALU.add))
        V(['fre', 'den'], ['fre'], lambda e: e.tensor_tensor(out=fre[:], in0=fre[:], in1=den[:], op=ALU.mult))
        V(['abi', 'Are'], ['fim'], lambda e: e.tensor_tensor(out=fim[:], in0=abi[:], in1=Are[:], op=ALU.mult))
        V(['am1', 'Aim'], ['t2'], lambda e: e.tensor_tensor(out=t2[:], in0=am1[:], in1=Aim[:], op=ALU.mult))
        V(['fim', 't2'], ['fim'], lambda e: e.tensor_tensor(out=fim[:], in0=fim[:], in1=t2[:], op=ALU.subtract))
        V(['fim', 'den'], ['fim'], lambda e: e.tensor_tensor(out=fim[:], in0=fim[:], in1=den[:], op=ALU.mult))

        ut = [T(f"ut{i}", [128, 8, 2, 128], BF16) for i in range(2)]
        if 'u_cands' in io:
            ucand = [[T(f"uc{h}", [128, 8, 2, 128], BF16)] * 2 for h in range(2)]
        wstg = [T("wstg", [128, 4, 4, 128])] * 2
        dstg = [T("dstg", [128, 128])] * 2
        TA = T("ta", [128, 2048]); TB = T("tb", [128, 2048])
        bw = [T(f"bw{i}", [128, 2, 4, 128], BF16) for i in range(2)]
        cw = [T(f"cw{i}", [128, 2, 4, 128], BF16) for i in range(2)]
        dw = [T(f"dw{i}", [128, 128], BF16) for i in range(2)]
        ctmp = T("ctmp", [128, 2, 4, 128])
        cosT = [T(f"cos{i}", [128, 2048]) for i in range(2)]
        sinT = [T(f"sin{i}", [128, 2048]) for i in range(2)]
        fa = T("fa", [128, 2048]); fr = T("fr", [128, 2048])
        G = [T("Gre", [128, 2048]), T("Gim", [128, 2048])]
        Rb = [[T(f"Rre{i}", [128, 2048]), T(f"Rim{i}", [128, 2048])] for i in range(2)]
        mg = T("mg", [128, 2])
        V([], ['mg'], lambda e: e.memset(mg[:, 0:1], MAGIC))
        V(['mg'], ['mg'], lambda e: e.memset(mg[:, 1:2], -MAGIC))
        H = [T(f"H{i}", [128, 4, 2048], BF16) for i in range(2)]
        yg = [T(f"yg{i}", [128, 8, 2, 128], BF16) for i in range(2)]
        psb = [ps(nc, es, f"E_psb{i}", [128, 1024]) for i in range(2)]
        yps = ps(nc, es, "E_yps", [128, 2048])

        def load_ct(ci):
            ct = cts[ci]
            j = ci % 2
            if 'u_cands' in io:
                for h in range(2):
                    for r in range(2):
                        k.dma('sp', ['uT_all'], [f"E_uc{h}"], out=ucand[h][j][:, :, r, :],
                              in_=io['u_cands'](h, r, ct).rearrange("c (m i) -> c m i", i=128), key=f"E_uc{h}")
                k.op('pool', ["E_uc0", 'c_sel'], [f"E_ut{j}"],
                     lambda e: e.tensor_scalar(out=ut[j][:], in0=ucand[0][j][:], scalar1=cst['sel'][:, 0:1], scalar2=None,
                                               op0=ALU.mult))
                k.op('pool', ["E_uc1", 'c_sel'], ["E_uc1"],
                     lambda e: e.tensor_scalar(out=ucand[1][j][:], in0=ucand[1][j][:], scalar1=cst['sel'][:, 1:2],
                                               scalar2=None, op0=ALU.mult))
                k.op('pool', ["E_uc1", f"E_ut{j}"], [f"E_ut{j}"],
                     lambda e: e.tensor_tensor(out=ut[j][:], in0=ut[j][:], in1=ucand[1][j][:], op=ALU.add))
            else:
                for r in range(2):
                    k.dma('sp', ['uT_all'], [f"E_ut{j}"], out=ut[j][:, :, r, :],
                          in_=io['uT_all'][r, ct * 128:(ct + 1) * 128, :].rearrange("c (m i) -> c m i", i=128), key=f"E_ut{j}")
            for w, nm in enumerate(('Bp_re', 'Bp_im', 'Cp_re', 'Cp_im')):
                k.dma('sp', [], ["E_wstg"], out=wstg[j][:, w, :, :], in_=io[nm][:, ct * 4:(ct + 1) * 4, :], key="E_wstg")
            k.dma('sp', [], ["E_dstg"], out=dstg[j][:], in_=io['Ddiag'][:, ct, :], key="E_dstg")
            prep_ct(ci)

        def prep_ct(ci):
            ct = cts[ci]
            j = ci % 2
            k.op('pool', ["E_wstg"], [f"E_bw{j}"], lambda e: e.tensor_copy(out=bw[j][:], in_=wstg[j][:, 0:2, :, :]))
            k.op('pool', ["E_dstg"], [f"E_dw{j}"], lambda e: e.tensor_copy(out=dw[j][:], in_=dstg[j][:]))
            fr_b = fre[:, ct * 4:(ct + 1) * 4].unsqueeze(2).to_broadcast([128, 4, 128])
            fi_b = fim[:, ct * 4:(ct + 1) * 4].unsqueeze(2).to_broadcast([128, 4, 128])
            Cre, Cim = wstg[j][:, 2, :, :], wstg[j][:, 3, :, :]
            k.op('pool', ["E_wstg", 'E_fre'], ['E_ctmp'], lambda e: e.tensor_tensor(out=ctmp[:, 0, :, :], in0=Cre, in1=fr_b, op=ALU.mult))
            k.op('pool', ["E_wstg", 'E_fim'], ['E_ctmp'], lambda e: e.tensor_tensor(out=ctmp[:, 1, :, :], in0=Cim, in1=fi_b, op=ALU.mult))
            k.op('pool', ['E_ctmp'], [f"E_cw{j}"], lambda e: e.tensor_tensor(out=cw[j][:, 0, :, :], in0=ctmp[:, 0, :, :], in1=ctmp[:, 1, :, :], op=ALU.subtract))
            k.op('pool', ["E_wstg", 'E_fim'], ['E_ctmp'], lambda e: e.tensor_tensor(out=ctmp[:, 0, :, :], in0=Cre, in1=fi_b, op=ALU.mult))
            k.op('pool', ["E_wstg", 'E_fre'], ['E_ctmp'], lambda e: e.tensor_tensor(out=ctmp[:, 1, :, :], in0=Cim, in1=fr_b, op=ALU.mult))
            k.op('pool', ['E_ctmp'], ['E_ctmp'], lambda e: e.tensor_tensor(out=ctmp[:, 0, :, :], in0=ctmp[:, 0, :, :], in1=ctmp[:, 1, :, :], op=ALU.add))
            k.op('pool', ['E_ctmp'], [f"E_cw{j}"], lambda e: e.tensor_scalar(out=cw[j][:, 1, :, :], in0=ctmp[:, 0, :, :], scalar1=-1.0, scalar2=None, op0=ALU.mult))

        ci_of = {ct: ci for ci, ct in enumerate(cts)}

        def tables_a(qi):
            q = pairs[qi]
            thq = thf[:, q:q + 1]
            A(['thf', 'c_iota', 'mg'], ['fa'], lambda e: e.activation(out=fa[:], in_=iota[:], func=AF.Identity, scale=thq, bias=mg[:, 0:1]))
            A(['fa', 'mg'], ['fa'], lambda e: e.activation(out=fa[:], in_=fa[:], func=AF.Identity, bias=mg[:, 1:2]))

        def tables_b(qi):
            q = pairs[qi]
            tb = qi % 2
            thq = thf[:, q:q + 1]
            V(['fa', 'thf', 'c_iota'], ['fr'], lambda e: e.scalar_tensor_tensor(out=fr[:], in0=iota[:], scalar=thq, in1=fa[:], op0=ALU.mult, op1=ALU.subtract))
            A(['fr'], [f"sin{tb}"], lambda e: e.activation(out=sinT[tb][:], in_=fr[:], func=AF.Sin, scale=TWO_PI))
            A(['fr'], ['fa'], lambda e: e.activation(out=fa[:], in_=fr[:], func=AF.Abs))
            A(['fa', 'c_halfpi'], [f"cos{tb}"], lambda e: e.activation(out=cosT[tb][:], in_=fa[:], func=AF.Sin, scale=-TWO_PI, bias=halfpi[:, 0:1]))

        def open_ct(ct):
            ci = ci_of[ct]
            j = ci % 2
            uflat = ut[j][:].rearrange("p m r i -> p (m r i)")
            k.pe([f"E_dw{j}", f"E_ut{j}"], ['E_yps'],
                 [lambda e, sg=sg: e.matmul(yps[:, sg * 512:(sg + 1) * 512], lhsT=dw[j][:], rhs=uflat[:, sg * 512:(sg + 1) * 512],
                                            start=True, stop=False) for sg in range(4)])

        def stage_in(qi):
            q = pairs[qi]
            ct, b, tb = q // 4, q % 4, qi % 2
            j = ci_of[ct] % 2
            uflat = ut[j][:].rearrange("p m r i -> p (m r i)")
            cT, sT = cosT[tb], sinT[tb]
            for sg in range(4):
                sl = slice(sg * 512, (sg + 1) * 512)
                s_ = sg % 2
                k.pe([f"E_bw{j}", f"E_ut{j}"], [f"E_psb{s_}"],
                     [lambda e: e.matmul(psb[s_][:, 0:512], lhsT=bw[j][:, 0, b, :], rhs=uflat[:, sl], start=True, stop=True),
                      lambda e: e.matmul(psb[s_][:, 512:1024], lhsT=bw[j][:, 1, b, :], rhs=uflat[:, sl], start=True, stop=True)])
                A([f"psb{s_}"], ['Gre'], lambda e: e.activation(out=G[0][:, sl], in_=psb[s_][:, 0:512], func=AF.Copy))
                A([f"psb{s_}"], ['Gim'], lambda e: e.activation(out=G[1][:, sl], in_=psb[s_][:, 512:1024], func=AF.Copy))

        def stage_rot(qi):
            q = pairs[qi]
            tb = qi % 2
            cT, sT = cosT[tb], sinT[tb]
            V([f"sin{tb}", 'Gim'], ['ta'], lambda e: e.tensor_tensor(out=TA[:], in0=G[1][:], in1=sT[:], op=ALU.mult))
            V([f"sin{tb}", 'Gre'], ['tb'], lambda e: e.tensor_tensor(out=TB[:], in0=G[0][:], in1=sT[:], op=ALU.mult))
            V([f"cos{tb}", 'Gre'], ['Gre'], lambda e: e.tensor_tensor(out=G[0][:], in0=G[0][:], in1=cT[:], op=ALU.mult))
            V([f"cos{tb}", 'Gim'], ['Gim'], lambda e: e.tensor_tensor(out=G[1][:], in0=G[1][:], in1=cT[:], op=ALU.mult))
            V(['Gre', 'ta'], ['Gre'], lambda e: e.tensor_tensor(out=G[0][:], in0=G[0][:], in1=TA[:], op=ALU.add))
            V(['Gim', 'tb'], ['Gim'], lambda e: e.tensor_tensor(out=G[1][:], in0=G[1][:], in1=TB[:], op=ALU.subtract))

        def stage_scan(qi):
            q = pairs[qi]
            rq = rho[:, q:q + 1].to_broadcast([128, 2048])
            R = Rb[qi % 2]
            for c in range(2):
                V(['rho', ('Gre', 'Gim')[c]], [(f"Rre{qi % 2}", f"Rim{qi % 2}")[c]],
                  lambda e: e.tensor_tensor_scan(out=R[c][:], data0=rq, data1=G[c][:], initial=0.0, op0=ALU.mult, op1=ALU.add))

        def stage_in_b(qi):
            tb, hb = qi % 2, qi % 2
            cT, sT = cosT[tb], sinT[tb]
            R = Rb[qi % 2]
            rre, rim = f"Rre{qi % 2}", f"Rim{qi % 2}"
            V([f"cos{tb}", rre], [f"H{hb}"], lambda e: e.tensor_tensor(out=H[hb][:, 0, :], in0=R[0][:], in1=cT[:], op=ALU.mult))
            V([f"sin{tb}", rim], [f"H{hb}"], lambda e: e.scalar_tensor_tensor(out=H[hb][:, 1, :], in0=R[1][:], scalar=-1.0, in1=sT[:], op0=ALU.mult, op1=ALU.mult))
            V([f"cos{tb}", rim], [f"H{hb}"], lambda e: e.tensor_tensor(out=H[hb][:, 2, :], in0=R[1][:], in1=cT[:], op=ALU.mult))
            V([f"sin{tb}", rre], [f"H{hb}"], lambda e: e.tensor_tensor(out=H[hb][:, 3, :], in0=R[0][:], in1=sT[:], op=ALU.mult))

        def stage_out(qi):
            q = pairs[qi]
            ct, b, hb = q // 4, q % 4, qi % 2
            ci = ci_of[ct]
            j = ci % 2
            myp = [qq for qq in pairs if qq // 4 == ct]
            if q == myp[0]:
                open_ct(ct)
            last = (q == myp[-1])
            fns = []
            for sg in range(4):
                sl = slice(sg * 512, (sg + 1) * 512)
                fns.append(lambda e, sl=sl: e.matmul(yps[:, sl], lhsT=cw[j][:, 0, b, :], rhs=H[hb][:, 0, sl], start=False, stop=False))
                fns.append(lambda e, sl=sl: e.matmul(yps[:, sl], lhsT=cw[j][:, 0, b, :], rhs=H[hb][:, 1, sl], start=False, stop=False))
                fns.append(lambda e, sl=sl: e.matmul(yps[:, sl], lhsT=cw[j][:, 1, b, :], rhs=H[hb][:, 2, sl], start=False, stop=False))
                fns.append(lambda e, sl=sl: e.matmul(yps[:, sl], lhsT=cw[j][:, 1, b, :], rhs=H[hb][:, 3, sl], start=False, stop=last))
            k.pe([f"E_cw{j}", f"E_H{hb}"], ['E_yps'], fns)
            if last:
                ygf = yg[j][:].rearrange("p m r i -> p (m r i)")
                for sg in range(4):
                    sl = slice(sg * 512, (sg + 1) * 512)
                    A(['yps'], ['ta'], lambda e: e.activation(out=TA[:, sl], in_=yps[:, sl], func=AF.Copy))
                    A(['yps'], ['tb'], lambda e: e.activation(out=TB[:, sl], in_=yps[:, sl], func=AF.Square))
                V(['tb'], ['tb'], lambda e: e.tensor_scalar(out=TB[:], in0=TB[:], scalar1=0.044715, scalar2=1.0, op0=ALU.mult, op1=ALU.add))
                V(['ta', 'tb'], ['tb'], lambda e: e.tensor_tensor(out=TB[:], in0=TB[:], in1=TA[:], op=ALU.mult))
                A(['tb'], ['tb'], lambda e: e.activation(out=TB[:], in_=TB[:], func=AF.Sigmoid, scale=1.5957691216057308))
                V(['ta', 'tb'], [f"yg{j}"], lambda e: e.tensor_tensor(out=ygf, in0=TA[:], in1=TB[:], op=ALU.mult))
                for r in range(2):
                    k.dma('act', [f"E_yg{j}"], ['ygT'], out=io['ygT'][r, ct * 128:(ct + 1) * 128, :].rearrange("c (m i) -> c m i", i=128),
                          in_=yg[j][:, :, r, :], key=f"E_yg{j}")

        load_ct(0)
        if len(cts) > 1:
            load_ct(1)
        tables_a(0)
        tables_b(0)
        npairs = len(pairs)
        for qi in range(npairs):
            q = pairs[qi]
            stage_in(qi)
            if qi + 1 < npairs:
                tables_a(qi + 1)
            stage_rot(qi)
            if qi + 1 < npairs:
                tables_b(qi + 1)
            stage_scan(qi)
            stage_in_b(qi)
            if qi >= 1:
                stage_out(qi - 1)
                qp = pairs[qi - 1]
                if qp // 4 != q // 4 and ci_of[qp // 4] + 2 < len(cts):
                    load_ct(ci_of[qp // 4] + 2)
        stage_out(npairs - 1)


def ssm_layouts(r, A_re, A_im, log_dt, B_re, B_im, C_re, C_im, Dv):
    g0 = 128 * r
    Bp = [np.zeros((128, 64, 128), np.float32) for _ in range(2)]
    Cp = [np.zeros((128, 64, 128), np.float32) for _ in range(2)]
    for q in range(64):
        b = q % 4
        for gg in range(2):
            g = g0 + 2 * q + gg
            rows = slice(32 * b + 16 * gg, 32 * b + 16 * gg + 16)
            cols = slice(64 * gg, 64 * gg + 64)
            Bp[0][rows, q, cols] = B_re[g].T
            Bp[1][rows, q, cols] = B_im[g].T
            Cp[0][cols, q, rows] = C_re[g].T
            Cp[1][cols, q, rows] = C_im[g].T
    Dd = np.zeros((128, 16, 128), np.float32)
    for ct in range(16):
        Dd[np.arange(128), ct, np.arange(128)] = Dv[2048 * r + 128 * ct: 2048 * r + 128 * (ct + 1)]
    tr = lambda a: np.ascontiguousarray(a[g0:g0 + 128].reshape(64, 2, 64).transpose(1, 2, 0).reshape(128, 64))
    ldt = np.ascontiguousarray(np.broadcast_to(log_dt[g0:g0 + 128].reshape(64, 2, 1), (64, 2, 64)).transpose(1, 2, 0).reshape(128, 64))
    return dict(Bp_re=Bp[0], Bp_im=Bp[1], Cp_re=Cp[0], Cp_im=Cp[1], Ddiag=Dd, A_re_T=tr(A_re), A_im_T=tr(A_im),
                logdt_T=ldt.astype(np.float32))


def kernel(x, norm_g, attn_w_in, attn_q_g, attn_k_g, attn_w_out, ssm_w_in, ssm_A_re, ssm_A_im, ssm_log_dt,
           ssm_B_re, ssm_B_im, ssm_C_re, ssm_C_im, ssm_D, ssm_glu_w, ssm_glu_b, ssm_w_out):
    f32 = lambda a: np.ascontiguousarray(np.asarray(a, np.float32))
    x = f32(x)
    cores = list(range(8))
    bf = lambda a: np.asarray(a, np.float32).astype(NPBF)
    norm_g = f32(norm_g)
    lay = [ssm_layouts(r, f32(ssm_A_re)[0], f32(ssm_A_im)[0], f32(ssm_log_dt)[0], f32(ssm_B_re)[0], f32(ssm_B_im)[0],
                       f32(ssm_C_re)[0], f32(ssm_C_im)[0], f32(ssm_D)[0]) for r in range(2)]
    shared = dict(
        w_in=f32(attn_w_in)[0], w_out=f32(attn_w_out)[0], ssm_w_in=f32(ssm_w_in)[0], glu_w=f32(ssm_glu_w)[0],
        ssm_w_out=f32(ssm_w_out)[0],
        gk0=np.ascontiguousarray(norm_g[0].reshape(32, 128).T), gk1=np.ascontiguousarray(norm_g[1].reshape(32, 128).T),
        qg=f32(attn_q_g)[0].reshape(128, 1), kg=f32(attn_k_g)[0].reshape(128, 1),
        ident=bf(np.eye(128)), ones=bf(np.ones((128, 128))), eps=np.full((128, 1), EPS, np.float32),
        zcol=np.zeros((128, 1), np.float32), halfpi=np.full((128, 1), np.pi / 2, np.float32),
        glub=np.ascontiguousarray(f32(ssm_glu_b)[0].reshape(32, 128).T),
        iota=np.ascontiguousarray(np.broadcast_to(np.arange(2048, dtype=np.float32), (128, 2048))))
    in_maps = []
    for c in cores:
        b, r = c // 2, c % 2
        sel = np.zeros((128, 2), np.float32)
        sel[:, r] = 1.0
        m = dict(shared)
        m.update(x=np.ascontiguousarray(x[b].reshape(8, 2, 128, D)[:, r].reshape(NTOK, D)), negm=_negmask(r), sel=sel, **lay[r])
        in_maps.append(m)
    res = run_bass_kernel_spmd(build_fused(), in_maps, core_ids=cores).results
    out = np.zeros((4, 2048, D), np.float32)
    for c in cores:
        b, r = c // 2, c % 2
        out[b].reshape(8, 2, 128, D)[:, r] = np.asarray(res[c]['out'], np.float32).reshape(8, 128, D)
    return out
```

```python
import numpy as np
import os
from contextlib import ExitStack
import ml_dtypes
import concourse.bass as bass
import concourse.mybir as mybir
from concourse.bass_utils import run_bass_kernel_spmd

F32 = mybir.dt.float32
BF16 = mybir.dt.bfloat16
ALU = mybir.AluOpType
AF = mybir.ActivationFunctionType
NPBF = ml_dtypes.bfloat16

D = 4096
NTOK = 1024
KC = 32
FW = 256
EPS = 1e-6
MAGIC = 12582912.0
TWO_PI = 6.283185307179586


class K:
    def __init__(self, nc):
        self.nc = nc
        self.eng = {'pe': nc.tensor, 'act': nc.scalar, 'dve': nc.vector, 'pool': nc.gpsimd, 'sp': nc.sync}
        self.prog = {}
        self.waited = {}
        self.lastw = {}
        self.readers = {}
        self.dsem = {}
        self.nsem = 0
        self.out_tokens = []
        self.old_tokens = []

    def _newsem(self, tag):
        self.nsem += 1
        return self.nc.alloc_semaphore(f"s_{tag}_{self.nsem}")

    def _wait(self, e, tok):
        sem, val, owner, sid = tok
        key = (e, sid)
        if self.waited.get(key, 0) >= val:
            return
        self.eng[e].wait_ge(sem, val)
        self.waited[key] = val

    def _deps(self, e, reads, writes):
        toks = []
        for b in list(reads) + list(writes):
            for t in self.lastw.get(b, {}).values():
                toks.append((t, True))
        for b in writes:
            for t in self.readers.get(b, []):
                toks.append((t, False))
        for t, is_w in toks:
            if t[2] == e:
                if e in ('pe', 'sp'):
                    continue
            self._wait(e, t)

    def _mark(self, e, ins):
        p = self.prog.get(e)
        if p is None or p[1] >= 6000:
            if p is not None:
                self.old_tokens.append((p[0], p[1], e, p[2]))
            p = [self._newsem(e), 0, self.nsem]
            self.prog[e] = p
        p[1] += 1
        ins.then_inc(p[0], 1)
        return (p[0], p[1], e, p[2])

    def _record(self, tok, reads, writes):
        for b in writes:
            self.lastw.setdefault(b, {})[tok[2] if tok[2] != 'dma' else ('dma', tok[3])] = tok
            self.readers[b] = []
        for b in reads:
            self.readers.setdefault(b, []).append(tok)

    def op(self, e, reads, writes, fn):
        self._deps(e, reads, writes)
        ins = fn(self.eng[e])
        tok = self._mark(e, ins)
        self._record(tok, reads, writes)
        return tok

    def pe(self, reads, writes, fns):
        self._deps('pe', reads, writes)
        ins = None
        for f in fns:
            ins = f(self.nc.tensor)
        tok = self._mark('pe', ins)
        self._record(tok, reads, writes)
        return tok

    def dma(self, q, reads, writes, out, in_, key, final=False):
        self._deps(q, reads, writes)
        d = self.dsem.get(key)
        if d is None or d[1] >= 1500:
            if d is not None:
                self.old_tokens.append((d[0], 16 * d[1], 'dma', d[2]))
            d = [self._newsem('d'), 0, self.nsem]
            self.dsem[key] = d
        ins = self.eng[q].dma_start(out=out, in_=in_)
        d[1] += 1
        ins.then_inc(d[0], 16)
        tok = (d[0], 16 * d[1], 'dma', d[2])
        self._record(tok, reads, writes)
        if final:
            self.out_tokens.append(tok)
        return tok

    def collective(self, kind, groups, in_ap, out_ap, in_key, out_key, scratch):
        self._deps('pool', [in_key], [out_key])
        if not hasattr(self, 'ccsem'):
            self.ccsem = [self._newsem('cc'), 0]
        ins = self.nc.gpsimd.collective_compute(kind, ALU.bypass, replica_groups=groups,
                                                ins=[in_ap.opt()], outs=[out_ap.opt()])
        self.ccsem[1] += 1
        ins.then_inc(self.ccsem[0])
        self.nc.gpsimd.wait_ge(self.ccsem[0], self.ccsem[1])
        return self.op('pool', [in_key], [out_key], lambda e: e.memset(scratch, 0.0))

    def barrier(self):
        toks = []
        for e, p in self.prog.items():
            toks.append((p[0], p[1], e, p[2]))
        for key, d in self.dsem.items():
            toks.append((d[0], 16 * d[1], 'dma', d[2]))
        toks += self.old_tokens
        for e in self.eng:
            for t in toks:
                if t[2] == e:
                    continue
                self._wait(e, t)

    def finish(self, keys=()):
        for tok in self.out_tokens:
            self._wait('sp', tok)
        for key in keys:
            for t in self.lastw.get(key, {}).values():
                self._wait('sp', t)


def sb(nc, es, name, shape, dt):
    return es.enter_context(nc.sbuf_tensor(name, shape, dt))


def ps(nc, es, name, shape, dt=F32):
    return es.enter_context(nc.psum_tensor(name, shape, dt))


def load_const(nc, k, es, name, src, shape, dt):
    t = sb(nc, es, name, shape, dt)
    k.dma('sp', [], [name], out=t[:], in_=src, key=name)
    return t


def dense_phase(nc, k, tag, *, W, units, act_src, gk=None, consts=None, extra_alloc=None):
    with ExitStack() as es:
        actT = sb(nc, es, f"{tag}_actT", [128, KC, NTOK], BF16)
        wst = [sb(nc, es, f"{tag}_wst{i}", [128, KC, FW], F32) for i in range(2)]
        wbf = [sb(nc, es, f"{tag}_wbf{i}", [128, KC, FW], BF16) for i in range(2)]
        psum = [ps(nc, es, f"{tag}_ps{i}", [128, 2048]) for i in range(2)]
        st = extra_alloc(nc, es) if extra_alloc else {}
        st.update(actT=actT, psum=psum, tag=tag)
        Wv = W.rearrange("(kc p) n -> p kc n", p=128)
        n_units = len(units)

        def load_w(u):
            s = u % 2
            c0 = units[u]['col0']
            for j in range(4):
                k.dma('sp', [], [f"{tag}_wst{s}"],
                      out=wst[s][:, j * 8:(j + 1) * 8, :], in_=Wv[:, j * 8:(j + 1) * 8, c0:c0 + FW],
                      key=f"{tag}_wst{s}")

        def conv_w(u):
            s = u % 2
            for j in range(4):
                sl = slice(j * 8, (j + 1) * 8)
                eng = ('pool', 'pool', 'dve', 'act')[j]
                if gk is not None:
                    if eng == 'act':
                        for kc in range(j * 8, (j + 1) * 8):
                            k.op('act', [f"{tag}_wst{s}", 'c_gk'], [f"{tag}_wbf{s}"],
                                 lambda e: e.activation(out=wbf[s][:, kc, :], in_=wst[s][:, kc, :], func=AF.Copy,
                                                        scale=gk[:, kc:kc + 1]))
                    else:
                        k.op(eng, [f"{tag}_wst{s}", 'c_gk'], [f"{tag}_wbf{s}"],
                             lambda e: e.tensor_tensor(out=wbf[s][:, sl, :], in0=wst[s][:, sl, :],
                                                       in1=gk[:, sl].unsqueeze(2).to_broadcast([128, 8, FW]),
                                                       op=ALU.mult))
                elif eng == 'act':
                    k.op('act', [f"{tag}_wst{s}"], [f"{tag}_wbf{s}"],
                         lambda e: e.activation(out=wbf[s][:, sl, :], in_=wst[s][:, sl, :], func=AF.Copy))
                else:
                    k.op(eng, [f"{tag}_wst{s}"], [f"{tag}_wbf{s}"],
                         lambda e: e.tensor_copy(out=wbf[s][:, sl, :], in_=wst[s][:, sl, :]))

        def mm(u):
            s = u % 2
            fns = []
            if units[u]['orient'] == 'F':
                for ft in range(2):
                    for hf in range(2):
                        bank = ft * 2 + hf
                        for kc in range(KC):
                            fns.append(lambda e, ft=ft, hf=hf, kc=kc, bank=bank: e.matmul(
                                psum[s][:, bank * 512:(bank + 1) * 512],
                                lhsT=wbf[s][:, kc, ft * 128:(ft + 1) * 128],
                                rhs=actT[:, kc, hf * 512:(hf + 1) * 512],
                                start=(kc == 0), stop=(kc == KC - 1)))
            else:
                for tt in range(8):
                    for kc in range(KC):
                        fns.append(lambda e, tt=tt, kc=kc: e.matmul(
                            psum[s][:, tt * FW:(tt + 1) * FW],
                            lhsT=actT[:, kc, tt * 128:(tt + 1) * 128],
                            rhs=wbf[s][:, kc, :],
                            start=(kc == 0), stop=(kc == KC - 1)))
            k.pe([f"{tag}_wbf{s}", f"{tag}_actT"], [f"{tag}_ps{s}"], fns)

        load_w(0)
        if act_src[0] == 'load':
            aT = act_src[1].rearrange("(kc p) t -> p kc t", p=128)
            for j in range(4):
                k.dma('sp', list(act_src[2]), [f"{tag}_actT"],
                      out=actT[:, j * 8:(j + 1) * 8, :], in_=aT[:, j * 8:(j + 1) * 8, :],
                      key=f"{tag}_actT")
        elif act_src[0] == 'load2':
            cands, keys, sel = act_src[1], list(act_src[2]), act_src[3]
            tmpb = st['tmpb']
            for j in range(4):
                sl = slice(j * 8, (j + 1) * 8)
                k.dma('sp', keys, [f"{tag}_actT"], out=actT[:, sl, :], in_=cands[0](j), key=f"{tag}_actT")
                k.dma('sp', keys, [f"{tag}_tmpb"], out=tmpb[:], in_=cands[1](j), key=f"{tag}_tmpb")
                k.op('act', [f"{tag}_actT", 'c_sel'], [f"{tag}_actT"],
                     lambda e: e.activation(out=actT[:, sl, :], in_=actT[:, sl, :], func=AF.Copy, scale=sel[:, 0:1]))
                k.op('dve', [f"{tag}_actT", f"{tag}_tmpb", 'c_sel'], [f"{tag}_actT"],
                     lambda e: e.scalar_tensor_tensor(out=actT[:, sl, :], in0=tmpb[:], scalar=sel[:, 1:2],
                                                      in1=actT[:, sl, :], op0=ALU.mult, op1=ALU.add))
        else:
            x_ap = act_src[1]
            xkeys = list(act_src[2])
            ident = consts['ident']
            epst = consts['eps']
            xin = [wst[1][:, 0:16, :], wst[1][:, 16:32, :]]
            hn = sb(nc, es, f"{tag}_hn", [128, D], BF16)
            ss = sb(nc, es, f"{tag}_ss", [128, 8], F32)
            rstd = sb(nc, es, f"{tag}_rstd", [128, 8], F32)
            k.op('dve', [], [f"{tag}_ss{t}" for t in range(8)], lambda e: e.memset(ss[:], 0.0))
            for tt in range(8):
                j = tt % 2
                xv = xin[j].rearrange("p a b -> p (a b)")
                k.dma('sp', xkeys, [f"{tag}_xin{j}"], out=xv, in_=x_ap[tt * 128:(tt + 1) * 128, :],
                      key=f"{tag}_xin{j}")
                k.op('act', [f"{tag}_xin{j}"], [f"{tag}_hn", f"{tag}_ss{tt}"],
                     lambda e: e.activation(out=hn[:], in_=xv, func=AF.Square,
                                            accum_out=ss[:, tt:tt + 1]))
                k.op('act', [f"{tag}_ss{tt}", 'c_eps'], [f"{tag}_ss{tt}"],
                     lambda e: e.activation(out=ss[:, tt:tt + 1], in_=ss[:, tt:tt + 1], func=AF.Sqrt,
                                            bias=epst[:, 0:1], scale=1.0 / D))
                k.op('dve', [f"{tag}_ss{tt}"], [f"{tag}_rstd{tt}"],
                     lambda e: e.reciprocal(out=rstd[:, tt:tt + 1], in_=ss[:, tt:tt + 1]))
                k.op('act', [f"{tag}_xin{j}", f"{tag}_rstd{tt}"], [f"{tag}_hn"],
                     lambda e: e.activation(out=hn[:], in_=xv, func=AF.Copy,
                                            scale=rstd[:, tt:tt + 1]))
                for g in range(4):
                    s = g % 2
                    fns = []
                    for i in range(8):
                        kc = g * 8 + i
                        fns.append(lambda e, i=i, kc=kc, s=s: e.matmul(
                            psum[s][:, i * 128:(i + 1) * 128],
                            lhsT=hn[:, kc * 128:(kc + 1) * 128], rhs=ident[:],
                            start=True, stop=True))
                    k.pe([f"{tag}_hn", 'c_ident'], [f"{tag}_ps{s}"], fns)
                    src = psum[s][:, 0:1024].rearrange("p (a b) -> p a b", a=8)
                    dst = actT[:, g * 8:(g + 1) * 8, tt * 128:(tt + 1) * 128]
                    if g % 2 == 0:
                        k.op('dve', [f"{tag}_ps{s}"], [f"{tag}_actT"],
                             lambda e: e.tensor_copy(out=dst, in_=src))
                    else:
                        k.op('act', [f"{tag}_ps{s}"], [f"{tag}_actT"],
                             lambda e: e.activation(out=dst, in_=src, func=AF.Copy))
            k.readers.setdefault(f"{tag}_wst1", [])
            for j in range(2):
                k.readers[f"{tag}_wst1"] += k.readers.get(f"{tag}_xin{j}", []) + list(k.lastw.get(f"{tag}_xin{j}", {}).values())

        conv_w(0)
        if n_units > 1:
            load_w(1)
        prefetch(nc, k, st, 0, units[0])
        for u in range(n_units):
            if u + 1 < n_units:
                prefetch(nc, k, st, u + 1, units[u + 1])
            mm(u)
            if u + 1 < n_units:
                conv_w(u + 1)
            if u + 2 < n_units:
                load_w(u + 2)
            evac_unit(nc, k, st, u, units[u])


def prefetch(nc, k, st, u, unit):
    tag = st['tag']
    s = u % 2
    kind = unit['kind']
    if kind == 'residT':
        c0 = unit['col0']
        src = unit['res'][:, c0:c0 + FW].rearrange("(tt p) c -> p tt c", p=128)
        k.dma('sp', list(unit.get('res_keys', [])), [f"{tag}_outR{s}"], out=st['outR'][s][:], in_=src,
              key=f"{tag}_xres{s}")
    elif kind == 'gluF':
        c0 = unit['col0']
        src = unit['sg'][c0:c0 + FW, :].rearrange("(ft p) t -> p ft t", p=128)
        k.dma('sp', list(unit.get('sg_keys', [])), [f"{tag}_sgs{s}"], out=st['sgs'][s][:], in_=src,
              key=f"{tag}_sgs{s}")


def evac_unit(nc, k, st, u, unit, part='all'):
    tag = st['tag']
    s = u % 2
    kind = unit['kind']
    P = st['psum'][s]
    pk = f"{tag}_ps{s}"
    if kind in ('copyF', 'siluF', 'qkF', 'gluF', 'vT'):
        oF = st['outF'][s]
        ok = f"{tag}_outF{s}"
    if kind == 'copyF' or kind == 'siluF':
        for ft in range(2):
            for hf in range(2):
                b = ft * 2 + hf
                src = P[:, b * 512:(b + 1) * 512]
                dst = oF[:, ft, hf * 512:(hf + 1) * 512]
                if kind == 'siluF':
                    k.op('act', [pk], [ok], lambda e: e.activation(out=dst, in_=src, func=AF.Silu))
                elif b % 2 == 0:
                    k.op('act', [pk], [ok], lambda e: e.activation(out=dst, in_=src, func=AF.Copy))
                else:
                    k.op('dve', [pk], [ok], lambda e: e.tensor_copy(out=dst, in_=src))
    elif kind == 'qkF':
        qraw, sq, rt = st['qraw'], st['sq'], st['rt']
        ones, epst, gcol = st['ones'], st['eps'], unit['g']
        if part in ('all', 'early'):
            for b in range(4):
                src = P[:, b * 512:(b + 1) * 512]
                k.op('act', [pk], [f"{tag}_qraw{b}"],
                     lambda e: e.activation(out=qraw[:, b, :], in_=src, func=AF.Copy))
                k.op('act', [pk], [f"{tag}_sq{b}"],
                     lambda e: e.activation(out=sq[:, b, :], in_=src, func=AF.Square))
        if part == 'early':
            return
        for ft in range(2):
            for hf in range(2):
                b = ft * 2 + hf
                k.pe([f"{tag}_sq{b}", 'c_ones'], [pk],
                     [lambda e, b=b: e.matmul(P[:, b * 512:(b + 1) * 512], lhsT=ones[:], rhs=sq[:, b, :],
                                              start=True, stop=True)])
            for hf in range(2):
                b = ft * 2 + hf
                src = P[:, b * 512:(b + 1) * 512]
                k.op('act', [pk, 'c_eps'], [f"{tag}_rt{hf}"],
                     lambda e: e.activation(out=rt[:, hf, :], in_=src, func=AF.Sqrt, bias=epst[:, 0:1],
                                            scale=1.0 / 128))
                k.op('dve', [f"{tag}_rt{hf}"], [f"{tag}_rt{hf}"],
                     lambda e: e.reciprocal(out=rt[:, hf, :], in_=rt[:, hf, :]))
                k.op('dve', [f"{tag}_qraw{b}", f"{tag}_rt{hf}", 'c_qg', 'c_kg'], [ok],
                     lambda e: e.scalar_tensor_tensor(out=oF[:, ft, hf * 512:(hf + 1) * 512], in0=qraw[:, b, :],
                                                      scalar=gcol, in1=rt[:, hf, :], op0=ALU.mult, op1=ALU.mult))
    elif kind == 'gluF':
        sig, sgs, actT, bias = st['sig'], st['sgs'][s], st['actT'], st['glub']
        for b in range(4):
            ft, hf = b // 2, b % 2
            fidx = unit['col0'] // 128 + ft
            src = P[:, b * 512:(b + 1) * 512]
            k.op('act', [pk, 'c_glub'], [f"{tag}_sig{b}"],
                 lambda e: e.activation(out=sig[:, b, :], in_=src, func=AF.Sigmoid, bias=bias[:, fidx:fidx + 1]))
            k.op('dve', [f"{tag}_sig{b}", f"{tag}_actT"], [f"{tag}_sig{b}"],
                 lambda e: e.tensor_tensor(out=sig[:, b, :], in0=sig[:, b, :],
                                           in1=actT[:, fidx, hf * 512:(hf + 1) * 512], op=ALU.mult))
            k.op('dve', [f"{tag}_sig{b}", f"{tag}_sgs{s}"], [ok],
                 lambda e: e.tensor_tensor(out=oF[:, ft, hf * 512:(hf + 1) * 512], in0=sig[:, b, :],
                                           in1=sgs[:, ft, hf * 512:(hf + 1) * 512], op=ALU.mult))
    if kind in ('copyF', 'siluF', 'qkF', 'gluF'):
        r0 = unit['row0']
        dst = unit['dst'][r0:r0 + FW, :].rearrange("(ft p) t -> p ft t", p=128)
        k.dma('act', [ok], [unit['dst_key']], out=dst, in_=oF[:], key=ok)
    elif kind == 'vT':
        oT = oF[:].rearrange("p a (b c) -> p (a b) c", b=4)
        for h in range(2):
            src = P[:, h * 1024:(h + 1) * 1024].rearrange("p (a b) -> p a b", a=4)
            dst = oT[:, h * 4:(h + 1) * 4, :]
            if h == 0:
                k.op('act', [pk], [ok], lambda e: e.activation(out=dst, in_=src, func=AF.Copy))
            else:
                k.op('dve', [pk], [ok], lambda e: e.tensor_copy(out=dst, in_=src))
        r0 = unit['row0']
        dst = unit['dst'][:, r0:r0 + FW].rearrange("(tt p) c -> p tt c", p=128)
        k.dma('act', [ok], [unit['dst_key']], out=dst, in_=oT, key=ok)
    elif kind == 'residT':
        oR = st['outR'][s]
        ok = f"{tag}_outR{s}"
        xr = oR
        for h in range(2):
            src = P[:, h * 1024:(h + 1) * 1024].rearrange("p (a b) -> p a b", a=4)
            k.op('dve', [pk], [ok],
                 lambda e: e.tensor_tensor(out=oR[:, h * 4:(h + 1) * 4, :], in0=src,
                                           in1=xr[:, h * 4:(h + 1) * 4, :], op=ALU.add))
        c0 = unit['col0']
        dst = unit['dst'][:, c0:c0 + FW].rearrange("(tt p) c -> p tt c", p=128)
        k.dma('act', [ok], [unit['dst_key']], out=dst, in_=oR[:], key=ok, final=unit.get('final', False))


def phase_A(nc, k, io, cst):
    def alloc(nc_, es):
        st = {}
        st['outF'] = [sb(nc, es, f"A_outF{i}", [128, 2, NTOK], BF16) for i in range(2)]
        st['qraw'] = sb(nc, es, "A_qraw", [128, 4, 512], F32)
        st['sq'] = sb(nc, es, "A_sq", [128, 4, 512], BF16)
        st['rt'] = sb(nc, es, "A_rt", [128, 2, 512], F32)
        st['ones'] = cst['ones']
        st['eps'] = cst['eps']
        return st
    units = []
    for c0 in range(0, 4096, FW):
        units.append(dict(col0=c0, kind='qkF', orient='F', g=cst['qg'][:, 0:1], dst=io['qT'], row0=c0, dst_key='qT'))
    for c0 in range(0, 4096, FW):
        units.append(dict(col0=4096 + c0, kind='qkF', orient='F', g=cst['kg'][:, 0:1], dst=io['kT'], row0=c0, dst_key='kT'))
    for c0 in range(0, 4096, FW):
        units.append(dict(col0=8192 + c0, kind='vT', orient='T', dst=io['v'], row0=c0, dst_key='v'))
    for c0 in range(0, 4096, FW):
        units.append(dict(col0=12288 + c0, kind='siluF', orient='F', dst=io['sgT'], row0=c0, dst_key='sgT'))
    if io.get('limit_units'):
        units = [units[i] for i in io['limit_units']]
    dense_phase(nc, k, 'A', W=io['w_in'], units=units, act_src=('rms', io['x'], io.get('x_keys', [])),
                gk=cst['gk0'], consts=cst, extra_alloc=alloc)


def phase_B(nc, k, io, cst, heads=range(32), stage=99):
    SC = 1.0 / float(np.sqrt(128.0))
    ident = cst['ident']
    negm = cst['negm']
    zcol = cst['zcol']
    with ExitStack() as es:
        kTh = [sb(nc, es, f"B_k{i}", [128, 8, 2, 128], BF16) for i in range(2)]
        vh = [sb(nc, es, f"B_v{i}", [128, 8, 2, 128], BF16) for i in range(2)]
        qTh = [sb(nc, es, f"B_q{i}", [128, NTOK], BF16) for i in range(2)]
        sgh = [sb(nc, es, f"B_sg{i}", [128, NTOK], BF16) for i in range(2)]
        goh = [sb(nc, es, f"B_go{i}", [128, NTOK], BF16) for i in range(2)]
        bt = [sb(nc, es, f"B_bt{i}", [128, 2048], F32) for i in range(3)]
        ob = [sb(nc, es, f"B_ob{i}", [128, 2048], F32) for i in range(3)]
        inc = [sb(nc, es, f"B_inc{i}", [128, 2056], F32) for i in range(2)]
        wq = [sb(nc, es, f"B_wq{i}", [128, 2048], BF16) for i in range(2)]
        wT = [sb(nc, es, f"B_wT{i}", [128, 16, 128], BF16) for i in range(2)]
        zm = [sb(nc, es, f"B_zm{i}", [128, 256], F32) for i in range(3)]
        psz = ps(nc, es, "B_psz", [128, 2048])
        pst = [ps(nc, es, f"B_pst{i}", [128, 512]) for i in range(2)]
        pso = ps(nc, es, "B_pso", [128, 1024])
        kT_rows = io.get('kT_rows') or (lambda r, h: io['kT_all'][r, h * 128:(h + 1) * 128, :])
        v_parts = io.get('v_parts') or (lambda r, h: [(0, 8, io['v_all'][r, :, h * 128:(h + 1) * 128])])

        def load_head(h, hb):
            for r in range(2):
                k.dma('sp', ['kT_all'], [f"B_k{hb}"], out=kTh[hb][:, :, r, :],
                      in_=kT_rows(r, h).rearrange("d (m i) -> d m i", i=128), key=f"B_k{hb}")
                for m0, nm, vap in v_parts(r, h):
                    k.dma('sp', ['v_all'], [f"B_v{hb}"], out=vh[hb][:, m0:m0 + nm, r, :],
                          in_=vap.rearrange("(m p) d -> p m d", p=128), key=f"B_v{hb}")
            k.dma('sp', ['qT'], [f"B_q{hb}"], out=qTh[hb][:], in_=io['qT'][h * 128:(h + 1) * 128, :], key=f"B_q{hb}")
            k.dma('sp', ['sgT'], [f"B_sg{hb}"], out=sgh[hb][:], in_=io['sgT'][h * 128:(h + 1) * 128, :], key=f"B_sg{hb}")

        heads = list(heads)
        blocks = [(hi, m) for hi in range(len(heads)) for m in range(8)]
        tcnt = [0]

        def stage1(bi):
            hi, m = blocks[bi]
            hb, mb = hi % 2, bi % 3
            kl = 256 * (m + 1)
            kflat = kTh[hb][:].rearrange("p m r i -> p (m r i)")
            fns = []
            for c0 in range(0, kl, 512):
                c1 = min(kl, c0 + 512)
                fns.append(lambda e, c0=c0, c1=c1: e.matmul(psz[:, c0:c1], lhsT=qTh[hb][:, m * 128:(m + 1) * 128],
                                                            rhs=kflat[:, c0:c1], start=True, stop=True))
            k.pe([f"B_q{hb}", f"B_k{hb}"], ["B_psz"], fns)
            k.op('dve', ["B_psz", 'c_negm'], [f"B_zm{mb}"],
                 lambda e: e.tensor_tensor(out=zm[mb][:], in0=psz[:, kl - 256:kl], in1=negm[:], op=ALU.add))
            for c0 in range(0, kl - 256, 512):
                c1 = min(kl - 256, c0 + 512)
                k.op('act', ["B_psz", f"B_zm{mb}"], [f"B_bt{mb}"],
                     lambda e: e.activation(out=bt[mb][:, c0:c1], in_=psz[:, c0:c1], func=AF.Sigmoid, scale=SC))
                k.op('act', ["B_psz", f"B_zm{mb}"], [f"B_ob{mb}"],
                     lambda e: e.activation(out=ob[mb][:, c0:c1], in_=psz[:, c0:c1], func=AF.Sigmoid, scale=-SC))
            k.op('act', [f"B_zm{mb}"], [f"B_bt{mb}"],
                 lambda e: e.activation(out=bt[mb][:, kl - 256:kl], in_=zm[mb][:], func=AF.Sigmoid, scale=SC))
            k.op('act', [f"B_zm{mb}"], [f"B_ob{mb}"],
                 lambda e: e.activation(out=ob[mb][:, kl - 256:kl], in_=zm[mb][:], func=AF.Sigmoid, scale=-SC))

        def stage2(bi):
            hi, m = blocks[bi]
            mb, m3 = bi % 2, bi % 3
            kl = 256 * (m + 1)
            k.op('dve', [], [f"B_inc{mb}"], lambda e: e.memset(inc[mb][:, kl:kl + 1], 1.0))
            ob_rev = bass.AP(ob[m3][:].tensor, ob[m3][:, kl - 1:kl].offset, [list(ob[m3][:].ap[0]), [-1, kl]])
            inc_rev = bass.AP(inc[mb][:].tensor, inc[mb][:, kl - 1:kl].offset, [list(inc[mb][:].ap[0]), [-1, kl]])
            k.op('dve', [f"B_ob{m3}", 'c_zcol', f"B_inc{mb}"], [f"B_inc{mb}"],
                 lambda e: e.tensor_tensor_scan(out=inc_rev, data0=zcol[:, 0:1].to_broadcast([128, kl]),
                                                data1=ob_rev, initial=1.0, op0=ALU.add, op1=ALU.mult))
            for c0 in range(0, kl, 1024):
                c1 = min(kl, c0 + 1024)
                k.op('dve', [f"B_bt{m3}", f"B_inc{mb}"], [f"B_wq{mb}"],
                     lambda e: e.tensor_tensor(out=wq[mb][:, c0:c1], in0=bt[m3][:, c0:c1], in1=inc[mb][:, c0 + 1:c1 + 1], op=ALU.mult))

        def stage3(bi):
            hi, m = blocks[bi]
            hb, mb = hi % 2, bi % 2
            nkb = 2 * (m + 1)
            vflat = vh[hb][:].rearrange("p m r d -> p (m r) d")
            for g0 in range(0, nkb, 4):
                g1 = min(nkb, g0 + 4)
                tb = tcnt[0] % 2
                tcnt[0] += 1
                fns = [lambda e, kb=kb, tb=tb, g0=g0: e.matmul(pst[tb][:, (kb - g0) * 128:(kb - g0 + 1) * 128],
                                                               lhsT=wq[mb][:, kb * 128:(kb + 1) * 128], rhs=ident[:],
                                                               start=True, stop=True) for kb in range(g0, g1)]
                k.pe([f"B_wq{mb}", 'c_ident'], [f"B_pst{tb}"], fns)
                src = pst[tb][:, 0:(g1 - g0) * 128].rearrange("p (a b) -> p a b", b=128)
                dst = wT[mb][:, g0:g1, :]
                k.op('act', [f"B_pst{tb}"], [f"B_wT{mb}"], lambda e: e.activation(out=dst, in_=src, func=AF.Copy))
            ob_ = (bi % 2) * 512
            fns = [lambda e, kb=kb: e.matmul(pso[:, ob_:ob_ + 128], lhsT=vflat[:, kb, :], rhs=wT[mb][:, kb, :],
                                             start=(kb == 0), stop=(kb == nkb - 1)) for kb in range(nkb)]
            k.pe([f"B_v{hb}", f"B_wT{mb}"], [f"B_pso{bi % 2}"], fns)
            k.op('dve', [f"B_pso{bi % 2}", f"B_sg{hb}"], [f"B_go{hb}"],
                 lambda e: e.tensor_tensor(out=goh[hb][:, m * 128:(m + 1) * 128], in0=pso[:, ob_:ob_ + 128],
                                           in1=sgh[hb][:, m * 128:(m + 1) * 128], op=ALU.mult))
            if m == 7:
                h = heads[hi]
                k.dma('act', [f"B_go{hb}"], ['goT'], out=io['goT'][h * 128:(h + 1) * 128, :], in_=goh[hb][:], key=f"B_go{hb}")

        load_head(heads[0], 0)
        if len(heads) > 1:
            load_head(heads[1], 1)
        nb = len(blocks)
        stage1(0)
        if nb > 1:
            stage1(1)
        stage2(0)
        for bi in range(nb):
            if bi + 2 < nb:
                stage1(bi + 2)
            if bi + 1 < nb:
                stage2(bi + 1)
            stage3(bi)
            hi0, m0 = blocks[bi]
            if m0 == 7 and hi0 + 2 < len(heads):
                load_head(heads[hi0 + 2], hi0 % 2)


def phase_C(nc, k, io, cst):
    def alloc(nc_, es):
        return dict(outR=[sb(nc, es, f"C_outR{i}", [128, 8, FW], F32) for i in range(2)])
    units = [dict(col0=c0, kind='residT', orient='T', res=io['x'], dst=io['x1'], dst_key='x1', final=True)
             for c0 in range(0, 4096, FW)]
    dense_phase(nc, k, 'C', W=io['w_out'], units=units, act_src=('load', io['goT'], ['goT']),
                consts=cst, extra_alloc=alloc)


def _consts(nc, k, es, names, cin):
    return {n: load_const(nc, k, es, "c_" + {"gk0": "gk"}.get(n, n), cin[n], list(cin[n].shape), cin[n].dtype)
            for n in names}


def build_launch1():
    nc = bass.Bass("TRN2", target_bir_lowering=False)
    k = K(nc)
    dt = lambda n, s, d, kind: nc.dram_tensor(n, s, d, kind=kind).ap()
    io = dict(x=dt("x", [NTOK, D], F32, "ExternalInput"), w_in=dt("w_in", [D, 4 * D], F32, "ExternalInput"),
              qT=dt("qT", [D, NTOK], BF16, "ExternalOutput"), kT=dt("kT", [D, NTOK], BF16, "ExternalOutput"),
              v=dt("v", [NTOK, D], BF16, "ExternalOutput"), sgT=dt("sgT", [D, NTOK], BF16, "ExternalOutput"))
    cin = dict(gk0=dt("gk0", [128, 32], F32, "ExternalInput"), qg=dt("qg", [128, 1], F32, "ExternalInput"),
               kg=dt("kg", [128, 1], F32, "ExternalInput"), ident=dt("ident", [128, 128], BF16, "ExternalInput"),
               ones=dt("ones", [128, 128], BF16, "ExternalInput"), eps=dt("eps", [128, 1], F32, "ExternalInput"))
    with ExitStack() as es:
        cst = _consts(nc, k, es, list(cin), cin)
        phase_A(nc, k, io, cst)
        k.finish(['qT', 'kT', 'v', 'sgT'])
    return nc


def phase_D(nc, k, io, cst):
    def alloc(nc_, es):
        return dict(outF=[sb(nc, es, f"D_outF{i}", [128, 2, NTOK], BF16) for i in range(2)])
    units = [dict(col0=c0, kind='copyF', orient='F', dst=io['uT'], row0=c0, dst_key='uT') for c0 in range(0, D, FW)]
    units += [dict(col0=D + c0, kind='siluF', orient='F', dst=io['sg1T'], row0=c0, dst_key='sg1T') for c0 in range(0, D, FW)]
    dense_phase(nc, k, 'D', W=io['ssm_w_in'], units=units, act_src=('rms', io['x1'], ['x1']),
                gk=cst['gk1'], consts=cst, extra_alloc=alloc)


def phase_F(nc, k, io, cst):
    def alloc(nc_, es):
        return dict(outF=[sb(nc, es, f"F_outF{i}", [128, 2, NTOK], BF16) for i in range(2)],
                    sgs=[sb(nc, es, f"F_sgs{i}", [128, 2, NTOK], BF16) for i in range(2)],
                    sig=sb(nc, es, "F_sig", [128, 4, 512], F32), glub=cst['glub'],
                    **({'tmpb': sb(nc, es, "F_tmpb", [128, 8, NTOK], BF16)} if 'yg_cands' in io else {}))
    units = [dict(col0=c0, kind='gluF', orient='F', sg=io['sg1T'], sg_keys=['sg1T'], dst=io['mT'], row0=c0, dst_key='mT')
             for c0 in range(0, D, FW)]
    src = ('load2', io['yg_cands'], ['yG'], cst['sel']) if 'yg_cands' in io else ('load', io['ygT'], [])
    dense_phase(nc, k, 'F', W=io['glu_w'], units=units, act_src=src, consts=cst, extra_alloc=alloc)


def phase_G(nc, k, io, cst):
    def alloc(nc_, es):
        return dict(outR=[sb(nc, es, f"G_outR{i}", [128, 8, FW], F32) for i in range(2)])
    units = [dict(col0=c0, kind='residT', orient='T', res=io['x1'], res_keys=['x1'], dst=io['out'], dst_key='out', final=True)
             for c0 in range(0, D, FW)]
    dense_phase(nc, k, 'G', W=io['ssm_w_out'], units=units, act_src=('load', io['mT'], ['mT']), consts=cst,
                extra_alloc=alloc)


def build_launch2():
    nc = bass.Bass("TRN2", target_bir_lowering=False)
    k = K(nc)
    dt = lambda n, s, d, kind: nc.dram_tensor(n, s, d, kind=kind).ap()
    io = dict(qT=dt("qT", [D, NTOK], BF16, "ExternalInput"), kT_all=dt("kT_all", [2, D, NTOK], BF16, "ExternalInput"),
              v_all=dt("v_all", [2, NTOK, D], BF16, "ExternalInput"), sgT=dt("sgT", [D, NTOK], BF16, "ExternalInput"),
              goT=dt("goT", [D, NTOK], BF16, "Internal"),
              x=dt("x", [NTOK, D], F32, "ExternalInput"), w_out=dt("w_out", [D, D], F32, "ExternalInput"),
              x1=dt("x1", [NTOK, D], F32, "ExternalOutput"),
              ssm_w_in=dt("ssm_w_in", [D, 2 * D], F32, "ExternalInput"),
              uT=dt("uT", [D, NTOK], BF16, "ExternalOutput"), sg1T=dt("sg1T", [D, NTOK], BF16, "ExternalOutput"))
    cin = dict(ident=dt("ident", [128, 128], BF16, "ExternalInput"), negm=dt("negm", [128, 256], F32, "ExternalInput"),
               zcol=dt("zcol", [128, 1], F32, "ExternalInput"), eps=dt("eps", [128, 1], F32, "ExternalInput"),
               gk1=dt("gk1", [128, 32], F32, "ExternalInput"))
    with ExitStack() as es:
        cst = _consts(nc, k, es, list(cin), cin)
        phase_B(nc, k, io, cst)
        k.barrier()
        phase_C(nc, k, io, cst)
        k.barrier()
        phase_D(nc, k, io, cst)
        k.finish(['x1', 'uT', 'sg1T'])
    return nc


def build_launch3():
    nc = bass.Bass("TRN2", target_bir_lowering=False)
    k = K(nc)
    dt = lambda n, s, d, kind: nc.dram_tensor(n, s, d, kind=kind).ap()
    io = dict(uT_all=dt("uT_all", [2, 2048, NTOK], BF16, "ExternalInput"), ygT=dt("ygT", [2, 2048, NTOK], BF16, "ExternalOutput"))
    for n in ('Bp_re', 'Bp_im', 'Cp_re', 'Cp_im'):
        io[n] = dt(n, [128, 64, 128], F32, "ExternalInput")
    io['Ddiag'] = dt('Ddiag', [128, 16, 128], F32, "ExternalInput")
    for n in ('A_re_T', 'A_im_T', 'logdt_T'):
        io[n] = dt(n, [128, 64], F32, "ExternalInput")
    cin = dict(iota=dt("iota", [128, 2048], F32, "ExternalInput"), halfpi=dt("halfpi", [128, 1], F32, "ExternalInput"),
               zcol=dt("zcol", [128, 1], F32, "ExternalInput"))
    with ExitStack() as es:
        cst = _consts(nc, k, es, list(cin), cin)
        phase_E(nc, k, io, cst)
        k.finish(['ygT'])
    return nc


def build_launch4():
    nc = bass.Bass("TRN2", target_bir_lowering=False)
    k = K(nc)
    dt = lambda n, s, d, kind: nc.dram_tensor(n, s, d, kind=kind).ap()
    io = dict(ygT=dt("ygT_own", [D, NTOK], BF16, "ExternalInput"), sg1T=dt("sg1T", [D, NTOK], BF16, "ExternalInput"),
              glu_w=dt("glu_w", [D, D], F32, "ExternalInput"), mT=dt("mT", [D, NTOK], BF16, "Internal"),
              ssm_w_out=dt("ssm_w_out", [D, D], F32, "ExternalInput"), x1=dt("x1", [NTOK, D], F32, "ExternalInput"),
              out=dt("out", [NTOK, D], F32, "ExternalOutput"))
    cin = dict(glub=dt("glub", [128, 32], F32, "ExternalInput"))
    with ExitStack() as es:
        cst = _consts(nc, k, es, list(cin), cin)
        phase_F(nc, k, io, cst)
        k.barrier()
        phase_G(nc, k, io, cst)
        k.finish(['out'])
    return nc


PAIRS = [[0, 1], [2, 3], [4, 5], [6, 7]]


def build_fused():
    nc = bass.Bass("TRN2", target_bir_lowering=False)
    k = K(nc)
    dt = lambda n, s, d, kind="Internal": nc.dram_tensor(n, s, d, kind=kind).ap()
    EI = "ExternalInput"
    io = dict(x=dt("x", [NTOK, D], F32, EI), w_in=dt("w_in", [D, 4 * D], F32, EI), w_out=dt("w_out", [D, D], F32, EI),
              ssm_w_in=dt("ssm_w_in", [D, 2 * D], F32, EI), glu_w=dt("glu_w", [D, D], F32, EI),
              ssm_w_out=dt("ssm_w_out", [D, D], F32, EI), out=dt("out", [NTOK, D], F32, "ExternalOutput"),
              qT=dt("qT", [D, NTOK], BF16), kT=dt("kT", [D, NTOK], BF16), v=dt("v", [NTOK, D], BF16),
              sgT=dt("sgT", [D, NTOK], BF16), goT=dt("goT", [D, NTOK], BF16), x1=dt("x1", [NTOK, D], F32),
              uT=dt("uT", [D, NTOK], BF16), sg1T=dt("sg1T", [D, NTOK], BF16), mT=dt("mT", [D, NTOK], BF16))
    kT_g = dt("kT_g", [2 * D, NTOK], BF16)
    v_g = dt("v_g", [2 * NTOK, D], BF16)
    u_g = dt("u_g", [2 * D, NTOK], BF16)
    yg_l = dt("yg_l", [D, NTOK], BF16)
    yg_g = dt("yg_g", [2 * D, NTOK], BF16)
    NCH = 4
    k4 = kT_g.rearrange("(c r f) t -> c r f t", c=NCH, r=2)
    v4 = v_g.rearrange("(c r t) f -> c r t f", c=NCH, r=2)
    u4 = u_g.rearrange("(c r f) t -> c r f t", c=NCH, r=2)
    y4 = yg_g.rearrange("(c r f) t -> c r f t", c=NCH, r=2)
    io['kT_rows'] = lambda r, h: k4[h // 8, r, (h % 8) * 128:(h % 8 + 1) * 128, :]
    io['v_parts'] = lambda r, h: [(2 * c, 2, v4[c, r, :, h * 128:(h + 1) * 128]) for c in range(NCH)]
    io['ygT'] = yg_l.rearrange("(d c) t -> d c t", d=2)
    io['u_cands'] = lambda h, r, ct: u4[(h * 2048 + ct * 128) // 1024, r,
                                        (h * 2048 + ct * 128) % 1024:(h * 2048 + ct * 128) % 1024 + 128, :]
    io['yg_cands'] = [(lambda j, d=d: y4[d * 2 + j % 2, j // 2, :, :].rearrange("(kc p) t -> p kc t", p=128))
                      for d in range(2)]

    def gather(src, dst, in_key, out_key, scr):
        rows = src.shape[0] // NCH
        for c in range(NCH):
            k.collective("AllGather", PAIRS, src[c * rows:(c + 1) * rows, :], dst[c * 2 * rows:(c + 1) * 2 * rows, :],
                         in_key, out_key, scr)
    for n in ('Bp_re', 'Bp_im', 'Cp_re', 'Cp_im'):
        io[n] = dt(n, [128, 64, 128], F32, EI)
    io['Ddiag'] = dt('Ddiag', [128, 16, 128], F32, EI)
    for n in ('A_re_T', 'A_im_T', 'logdt_T'):
        io[n] = dt(n, [128, 64], F32, EI)
    cin = dict(gk0=dt("gk0", [128, 32], F32, EI), gk1=dt("gk1", [128, 32], F32, EI), qg=dt("qg", [128, 1], F32, EI),
               kg=dt("kg", [128, 1], F32, EI), ident=dt("ident", [128, 128], BF16, EI), ones=dt("ones", [128, 128], BF16, EI),
               eps=dt("eps", [128, 1], F32, EI), negm=dt("negm", [128, 256], F32, EI), zcol=dt("zcol", [128, 1], F32, EI),
               halfpi=dt("halfpi", [128, 1], F32, EI), glub=dt("glub", [128, 32], F32, EI), sel=dt("sel", [128, 2], F32, EI))
    iota_d = dt("iota", [128, 2048], F32, EI)
    with ExitStack() as es:
        cst = {n: load_const(nc, k, es, "c_" + n, cin[n], list(cin[n].shape), cin[n].dtype) for n in cin}
        scr = sb(nc, es, "cc_scr", [128, 8], F32)
        phase_A(nc, k, io, cst)
        k.barrier()
        gather(io['kT'], kT_g, 'kT', 'kT_all', scr[:])
        gather(io['v'], v_g, 'v', 'v_all', scr[:])
        phase_B(nc, k, io, cst)
        k.barrier()
        phase_C(nc, k, io, cst)
        k.barrier()
        phase_D(nc, k, io, cst)
        k.barrier()
        gather(io['uT'], u_g, 'uT', 'uT_all', scr[:])
        with ExitStack() as es2:
            cst_e = dict(cst)
            cst_e['iota'] = load_const(nc, k, es2, "c_iota", iota_d, [128, 2048], F32)
            phase_E(nc, k, io, cst_e)
            k.barrier()
        gather(yg_l, yg_g, 'ygT', 'yG', scr[:])
        phase_F(nc, k, io, cst)
        k.barrier()
        phase_G(nc, k, io, cst)
        k.finish(['out'])
    return nc


def _negmask(r):
    t = np.arange(128)[:, None]
    j = np.arange(128)[None, :]
    diag = np.where(j < t, 0.0, -1000.0)
    full = np.zeros((128, 128))
    none = np.full((128, 128), -1000.0)
    return np.ascontiguousarray(np.concatenate([diag, none] if r == 0 else [full, diag], axis=1).astype(np.float32))


def kernel_unfused(x, norm_g, attn_w_in, attn_q_g, attn_k_g, attn_w_out, ssm_w_in, ssm_A_re, ssm_A_im, ssm_log_dt,
           ssm_B_re, ssm_B_im, ssm_C_re, ssm_C_im, ssm_D, ssm_glu_w, ssm_glu_b, ssm_w_out):
    f32 = lambda a: np.ascontiguousarray(np.asarray(a, np.float32))
    x = f32(x)
    ncores = 8
    cores = list(range(ncores))
    bf = lambda a: np.asarray(a, np.float32).astype(NPBF)
    ident = bf(np.eye(128))
    ones = bf(np.ones((128, 128)))
    eps = np.full((128, 1), EPS, np.float32)
    zcol = np.zeros((128, 1), np.float32)
    xo = [np.ascontiguousarray(x[c // 2].reshape(8, 2, 128, D)[:, c % 2].reshape(NTOK, D)) for c in cores]
    norm_g = f32(norm_g)
    gk = [np.ascontiguousarray(norm_g[i].reshape(32, 128).T) for i in range(2)]
    qg = f32(attn_q_g)[0].reshape(128, 1)
    kg = f32(attn_k_g)[0].reshape(128, 1)
    w_in = f32(attn_w_in)[0]
    in1 = [dict(x=xo[c], w_in=w_in, gk0=gk[0], qg=qg, kg=kg, ident=ident, ones=ones, eps=eps) for c in cores]
    r1 = run_bass_kernel_spmd(build_launch1(), in1, core_ids=cores).results
    del in1, w_in
    w_out = f32(attn_w_out)[0]
    sw_in = f32(ssm_w_in)[0]
    in2 = []
    for c in cores:
        p = (c // 2) * 2
        kT_all = np.ascontiguousarray(np.stack([np.asarray(r1[p]['kT']), np.asarray(r1[p + 1]['kT'])]))
        v_all = np.ascontiguousarray(np.stack([np.asarray(r1[p]['v']), np.asarray(r1[p + 1]['v'])]))
        in2.append(dict(qT=np.asarray(r1[c]['qT']), kT_all=kT_all, v_all=v_all, sgT=np.asarray(r1[c]['sgT']),
                        x=xo[c], w_out=w_out, ssm_w_in=sw_in, ident=ident, negm=_negmask(c % 2), zcol=zcol,
                        eps=eps, gk1=gk[1]))
    r2 = run_bass_kernel_spmd(build_launch2(), in2, core_ids=cores).results
    del in2, r1, w_out, sw_in
    if os.environ.get('KDEBUG'):
        np.save(os.environ['KDEBUG'] + '_x1.npy', np.stack([np.asarray(r2[c]['x1']) for c in range(2)]))
        np.save(os.environ['KDEBUG'] + '_uT.npy', np.stack([np.asarray(r2[c]['uT']).astype(np.float32) for c in range(2)]))
    lay = [ssm_layouts(r, f32(ssm_A_re)[0], f32(ssm_A_im)[0], f32(ssm_log_dt)[0], f32(ssm_B_re)[0], f32(ssm_B_im)[0],
                       f32(ssm_C_re)[0], f32(ssm_C_im)[0], f32(ssm_D)[0]) for r in range(2)]
    iota = np.ascontiguousarray(np.broadcast_to(np.arange(2048, dtype=np.float32), (128, 2048)))
    halfpi = np.full((128, 1), np.pi / 2, np.float32)
    in3 = []
    for c in cores:
        p, r = (c // 2) * 2, c % 2
        uT_all = np.ascontiguousarray(np.stack([np.asarray(r2[p + rr]['uT'])[2048 * r:2048 * (r + 1)] for rr in range(2)]))
        in3.append(dict(uT_all=uT_all, iota=iota, halfpi=halfpi, zcol=zcol, **lay[r]))
    r3 = run_bass_kernel_spmd(build_launch3(), in3, core_ids=cores).results
    del in3
    glu_w = f32(ssm_glu_w)[0]
    sw_out = f32(ssm_w_out)[0]
    glub = np.ascontiguousarray(f32(ssm_glu_b)[0].reshape(32, 128).T)
    in4 = []
    for c in cores:
        p, r = (c // 2) * 2, c % 2
        yg_own = np.ascontiguousarray(np.concatenate([np.asarray(r3[p + rr]['ygT'])[r] for rr in range(2)], axis=0))
        in4.append(dict(ygT_own=yg_own, sg1T=np.asarray(r2[c]['sg1T']), glu_w=glu_w, ssm_w_out=sw_out,
                        x1=np.asarray(r2[c]['x1']), glub=glub))
    r4 = run_bass_kernel_spmd(build_launch4(), in4, core_ids=cores).results
    out = np.zeros((4, 2048, D), np.float32)
    for c in cores:
        b, r = c // 2, c % 2
        out[b].reshape(8, 2, 128, D)[:, r] = np.asarray(r4[c]['out'], np.float32).reshape(8, 128, D)
    return out


def phase_E(nc, k, io, cst, pairs=range(64)):
    iota, halfpi, zc = cst['iota'], cst['halfpi'], cst['zcol']
    pairs = list(pairs)
    cts = sorted(set(q // 4 for q in pairs))
    with ExitStack() as es:
        T = lambda name, shape, dt=F32: sb(nc, es, "E_" + name, shape, dt)
        Are = T("Are", [128, 64]); Aim = T("Aim", [128, 64]); dtt = T("dt", [128, 64])
        xx = T("xx", [128, 64]); tt = T("tt", [128, 64]); rho = T("rho", [128, 64]); rm1 = T("rm1", [128, 64])
        th = T("th", [128, 64]); thf = T("thf", [128, 64]); fa1 = T("fa1", [128, 64])
        s1 = T("s1", [128, 64]); c1 = T("c1", [128, 64]); sh = T("sh", [128, 64])
        am1 = T("am1", [128, 64]); abi = T("abi", [128, 64]); den = T("den", [128, 64])
        fre = T("fre", [128, 64]); fim = T("fim", [128, 64]); t2 = T("t2", [128, 64])
        k.dma('sp', [], ['E_Are'], out=Are[:], in_=io['A_re_T'], key='E_Are')
        k.dma('sp', [], ['E_Aim'], out=Aim[:], in_=io['A_im_T'], key='E_Aim')
        k.dma('sp', [], ['E_dt'], out=dtt[:], in_=io['logdt_T'], key='E_dt')
        nm_ = lambda r: r if r.startswith('c_') else "E_" + r
        V = lambda reads, writes, fn: k.op('dve', [nm_(r) for r in reads], [nm_(w) for w in writes], fn)
        A = lambda reads, writes, fn: k.op('act', [nm_(r) for r in reads], [nm_(w) for w in writes], fn)
        Pl = lambda reads, writes, fn: k.op('pool', [nm_(r) for r in reads], [nm_(w) for w in writes], fn)
        A(['dt'], ['dt'], lambda e: e.activation(out=dtt[:], in_=dtt[:], func=AF.Exp))
        V(['Are', 'dt'], ['xx'], lambda e: e.tensor_tensor(out=xx[:], in0=Are[:], in1=dtt[:], op=ALU.mult))
        V(['xx'], ['tt'], lambda e: e.tensor_scalar(out=tt[:], in0=xx[:], scalar1=1.0 / 24, scalar2=1.0 / 6, op0=ALU.mult, op1=ALU.add))
        V(['tt', 'xx'], ['tt'], lambda e: e.tensor_tensor(out=tt[:], in0=tt[:], in1=xx[:], op=ALU.mult))
        V(['tt'], ['tt'], lambda e: e.tensor_scalar(out=tt[:], in0=tt[:], scalar1=0.5, scalar2=None, op0=ALU.add))
        V(['tt', 'xx'], ['tt'], lambda e: e.tensor_tensor(out=tt[:], in0=tt[:], in1=xx[:], op=ALU.mult))
        V(['tt'], ['tt'], lambda e: e.tensor_scalar(out=tt[:], in0=tt[:], scalar1=1.0, scalar2=None, op0=ALU.add))
        V(['tt', 'xx'], ['rm1'], lambda e: e.tensor_tensor(out=rm1[:], in0=tt[:], in1=xx[:], op=ALU.mult))
        V(['rm1'], ['rho'], lambda e: e.tensor_scalar(out=rho[:], in0=rm1[:], scalar1=1.0, scalar2=None, op0=ALU.add))
        V(['Aim', 'dt'], ['th'], lambda e: e.tensor_tensor(out=th[:], in0=Aim[:], in1=dtt[:], op=ALU.mult))
        V(['th'], ['t2'], lambda e: e.tensor_scalar(out=t2[:], in0=th[:], scalar1=1.0 / TWO_PI, scalar2=MAGIC, op0=ALU.mult, op1=ALU.add))
        V(['t2'], ['t2'], lambda e: e.tensor_scalar(out=t2[:], in0=t2[:], scalar1=MAGIC, scalar2=None, op0=ALU.subtract))
        V(['th', 't2'], ['thf'], lambda e: e.scalar_tensor_tensor(out=thf[:], in0=th[:], scalar=1.0 / TWO_PI, in1=t2[:], op0=ALU.mult, op1=ALU.subtract))
        V(['thf'], ['fa1'], lambda e: e.scalar_tensor_tensor(out=fa1[:], in0=thf[:], scalar=-1.0, in1=thf[:], op0=ALU.mult, op1=ALU.max))
        A(['thf'], ['s1'], lambda e: e.activation(out=s1[:], in_=thf[:], func=AF.Sin, scale=TWO_PI))
        A(['fa1', 'c_halfpi'], ['c1'], lambda e: e.activation(out=c1[:], in_=fa1[:], func=AF.Sin, scale=-TWO_PI, bias=halfpi[:, 0:1]))
        A(['thf'], ['sh'], lambda e: e.activation(out=sh[:], in_=thf[:], func=AF.Sin, scale=TWO_PI / 2))
        V(['sh'], ['sh'], lambda e: e.tensor_tensor(out=sh[:], in0=sh[:], in1=sh[:], op=ALU.mult))
        V(['rm1', 'c1'], ['am1'], lambda e: e.tensor_tensor(out=am1[:], in0=rm1[:], in1=c1[:], op=ALU.mult))
        V(['sh', 'am1'], ['am1'], lambda e: e.scalar_tensor_tensor(out=am1[:], in0=sh[:], scalar=-2.0, in1=am1[:], op0=ALU.mult, op1=ALU.add))
        V(['rho', 's1'], ['abi'], lambda e: e.tensor_tensor(out=abi[:], in0=rho[:], in1=s1[:], op=ALU.mult))
        V(['Are'], ['den'], lambda e: e.tensor_tensor(out=den[:], in0=Are[:], in1=Are[:], op=ALU.mult))
        V(['Aim'], ['t2'], lambda e: e.tensor_tensor(out=t2[:], in0=Aim[:], in1=Aim[:], op=ALU.mult))
        V(['den', 't2'], ['den'], lambda e: e.tensor_tensor(out=den[:], in0=den[:], in1=t2[:], op=ALU.add))
        V(['den'], ['den'], lambda e: e.reciprocal(out=den[:], in_=den[:]))
        V(['am1', 'Are'], ['fre'], lambda e: e.tensor_tensor(out=fre[:], in0=am1[:], in1=Are[:], op=ALU.mult))
        V(['abi', 'Aim'], ['t2'], lambda e: e.tensor_tensor(out=t2[:], in0=abi[:], in1=Aim[:], op=ALU.mult))
        V(['fre', 't2'], ['fre'], lambda e: e.tensor_tensor(out=fre[:], in0=fre[:], in1=t2[:], op=ALU.add))
        V(['fre', 'den'], ['fre'], lambda e: e.tensor_tensor(out=fre[:], in0=fre[:], in1=den[:], op=ALU.mult))
        V(['abi', 'Are'], ['fim'], lambda e: e.tensor_tensor(out=fim[:], in0=abi[:], in1=Are[:], op=ALU.mult))
        V(['am1', 'Aim'], ['t2'], lambda e: e.tensor_tensor(out=t2[:], in0=am1[:], in1=Aim[:], op=ALU.mult))
        V(['fim', 't2'], ['fim'], lambda e: e.tensor_tensor(out=fim[:], in0=fim[:], in1=t2[:], op=ALU.subtract))
        V(['fim', 'den'], ['fim'], lambda e: e.tensor_tensor(out=fim[:], in0=fim[:], in1=den[:], op=ALU.mult))

        ut = [T(f"ut{i}", [128, 8, 2, 128], BF16) for i in range(2)]
        if 'u_cands' in io:
            ucand = [[T(f"uc{h}", [128, 8, 2, 128], BF16)] * 2 for h in range(2)]
        wstg = [T("wstg", [128, 4, 4, 128])] * 2
        dstg = [T("dstg", [128, 128])] * 2
        TA = T("ta", [128, 2048]); TB = T("tb", [128, 2048])
        bw = [T(f"bw{i}", [128, 2, 4, 128], BF16) for i in range(2)]
        cw = [T(f"cw{i}", [128, 2, 4, 128], BF16) for i in range(2)]
        dw = [T(f"dw{i}", [128, 128], BF16) for i in range(2)]
        ctmp = T("ctmp", [128, 2, 4, 128])
        cosT = [T(f"cos{i}", [128, 2048]) for i in range(2)]
        sinT = [T(f"sin{i}", [128, 2048]) for i in range(2)]
        fa = T("fa", [128, 2048]); fr = T("fr", [128, 2048])
        G = [T("Gre", [128, 2048]), T("Gim", [128, 2048])]
        Rb = [[T(f"Rre{i}", [128, 2048]), T(f"Rim{i}", [128, 2048])] for i in range(2)]
        mg = T("mg", [128, 2])
        V([], ['mg'], lambda e: e.memset(mg[:, 0:1], MAGIC))
        V(['mg'], ['mg'], lambda e: e.memset(mg[:, 1:2], -MAGIC))
        H = [T(f"H{i}", [128, 4, 2048], BF16) for i in range(2)]
        yg = [T(f"yg{i}", [128, 8, 2, 128], BF16) for i in range(2)]
        psb = [ps(nc, es, f"E_psb{i}", [128, 1024]) for i in range(2)]
        yps = ps(nc, es, "E_yps", [128, 2048])

        def load_ct(ci):
            ct = cts[ci]
            j = ci % 2
            if 'u_cands' in io:
                for h in range(2):
                    for r in range(2):
                        k.dma('sp', ['uT_all'], [f"E_uc{h}"], out=ucand[h][j][:, :, r, :],
                              in_=io['u_cands'](h, r, ct).rearrange("c (m i) -> c m i", i=128), key=f"E_uc{h}")
                k.op('act', ["E_uc0", 'c_sel'], [f"E_ut{j}"],
                     lambda e: e.activation(out=ut[j][:], in_=ucand[0][j][:], func=AF.Copy, scale=cst['sel'][:, 0:1]))
                k.op('act', ["E_uc1", 'c_sel'], ["E_uc1"],
                     lambda e: e.activation(out=ucand[1][j][:], in_=ucand[1][j][:], func=AF.Copy, scale=cst['sel'][:, 1:2]))
                k.op('dve', ["E_uc1", f"E_ut{j}"], [f"E_ut{j}"],
                     lambda e: e.tensor_tensor(out=ut[j][:], in0=ut[j][:], in1=ucand[1][j][:], op=ALU.add))
            else:
                for r in range(2):
                    k.dma('sp', ['uT_all'], [f"E_ut{j}"], out=ut[j][:, :, r, :],
                          in_=io['uT_all'][r, ct * 128:(ct + 1) * 128, :].rearrange("c (m i) -> c m i", i=128), key=f"E_ut{j}")
            for w, nm in enumerate(('Bp_re', 'Bp_im', 'Cp_re', 'Cp_im')):
                k.dma('sp', [], ["E_wstg"], out=wstg[j][:, w, :, :], in_=io[nm][:, ct * 4:(ct + 1) * 4, :], key="E_wstg")
            k.dma('sp', [], ["E_dstg"], out=dstg[j][:], in_=io['Ddiag'][:, ct, :], key="E_dstg")
            prep_ct(ci)

        def prep_ct(ci):
            ct = cts[ci]
            j = ci % 2
            k.op('act', ["E_wstg"], [f"E_bw{j}"], lambda e: e.activation(out=bw[j][:], in_=wstg[j][:, 0:2, :, :], func=AF.Copy))
            k.op('act', ["E_dstg"], [f"E_dw{j}"], lambda e: e.activation(out=dw[j][:], in_=dstg[j][:], func=AF.Copy))
            Cre, Cim = wstg[j][:, 2, :, :], wstg[j][:, 3, :, :]
            for b in range(4):
                q = ct * 4 + b
                k.op('act', ["E_wstg", 'E_fre'], ['E_ctmp'], lambda e: e.activation(out=ctmp[:, 0, b, :], in_=Cre[:, b, :], func=AF.Copy, scale=fre[:, q:q + 1]))
                k.op('act', ["E_wstg", 'E_fim'], ['E_ctmp'], lambda e: e.activation(out=ctmp[:, 1, b, :], in_=Cim[:, b, :], func=AF.Copy, scale=fim[:, q:q + 1]))
            k.op('dve', ['E_ctmp'], [f"E_cw{j}"], lambda e: e.tensor_tensor(out=cw[j][:, 0, :, :], in0=ctmp[:, 0, :, :], in1=ctmp[:, 1, :, :], op=ALU.subtract))
            for b in range(4):
                q = ct * 4 + b
                k.op('act', ["E_wstg", 'E_fim'], ['E_ctmp'], lambda e: e.activation(out=ctmp[:, 0, b, :], in_=Cre[:, b, :], func=AF.Copy, scale=fim[:, q:q + 1]))
                k.op('act', ["E_wstg", 'E_fre'], ['E_ctmp'], lambda e: e.activation(out=ctmp[:, 1, b, :], in_=Cim[:, b, :], func=AF.Copy, scale=fre[:, q:q + 1]))
            k.op('dve', ['E_ctmp'], [f"E_cw{j}"],
                 lambda e: e.scalar_tensor_tensor(out=cw[j][:, 1, :, :], in0=ctmp[:, 0, :, :], scalar=-1.0, in1=ctmp[:, 1, :, :],
                                                  op0=ALU.mult, op1=ALU.subtract))

        ci_of = {ct: ci for ci, ct in enumerate(cts)}

        def tables_a(qi):
            q = pairs[qi]
            thq = thf[:, q:q + 1]
            A(['thf', 'c_iota', 'mg'], ['fa'], lambda e: e.activation(out=fa[:], in_=iota[:], func=AF.Identity, scale=thq, bias=mg[:, 0:1]))
            A(['fa', 'mg'], ['fa'], lambda e: e.activation(out=fa[:], in_=fa[:], func=AF.Identity, bias=mg[:, 1:2]))

        def tables_b(qi):
            q = pairs[qi]
            tb = qi % 2
            thq = thf[:, q:q + 1]
            V(['fa', 'thf', 'c_iota'], ['fr'], lambda e: e.scalar_tensor_tensor(out=fr[:], in0=iota[:], scalar=thq, in1=fa[:], op0=ALU.mult, op1=ALU.subtract))
            A(['fr'], [f"sin{tb}"], lambda e: e.activation(out=sinT[tb][:], in_=fr[:], func=AF.Sin, scale=TWO_PI))
            A(['fr'], ['fa'], lambda e: e.activation(out=fa[:], in_=fr[:], func=AF.Abs))
            A(['fa', 'c_halfpi'], [f"cos{tb}"], lambda e: e.activation(out=cosT[tb][:], in_=fa[:], func=AF.Sin, scale=-TWO_PI, bias=halfpi[:, 0:1]))

        def open_ct(ct):
            ci = ci_of[ct]
            j = ci % 2
            uflat = ut[j][:].rearrange("p m r i -> p (m r i)")
            k.pe([f"E_dw{j}", f"E_ut{j}"], ['E_yps'],
                 [lambda e, sg=sg: e.matmul(yps[:, sg * 512:(sg + 1) * 512], lhsT=dw[j][:], rhs=uflat[:, sg * 512:(sg + 1) * 512],
                                            start=True, stop=False) for sg in range(4)])

        def stage_in(qi):
            q = pairs[qi]
            ct, b, tb = q // 4, q % 4, qi % 2
            j = ci_of[ct] % 2
            uflat = ut[j][:].rearrange("p m r i -> p (m r i)")
            cT, sT = cosT[tb], sinT[tb]
            for sg in range(4):
                sl = slice(sg * 512, (sg + 1) * 512)
                s_ = sg % 2
                k.pe([f"E_bw{j}", f"E_ut{j}"], [f"E_psb{s_}"],
                     [lambda e: e.matmul(psb[s_][:, 0:512], lhsT=bw[j][:, 0, b, :], rhs=uflat[:, sl], start=True, stop=True),
                      lambda e: e.matmul(psb[s_][:, 512:1024], lhsT=bw[j][:, 1, b, :], rhs=uflat[:, sl], start=True, stop=True)])
                A([f"psb{s_}"], ['Gre'], lambda e: e.activation(out=G[0][:, sl], in_=psb[s_][:, 0:512], func=AF.Copy))
                A([f"psb{s_}"], ['Gim'], lambda e: e.activation(out=G[1][:, sl], in_=psb[s_][:, 512:1024], func=AF.Copy))

        def stage_rot(qi):
            q = pairs[qi]
            tb = qi % 2
            cT, sT = cosT[tb], sinT[tb]
            V([f"sin{tb}", 'Gim'], ['ta'], lambda e: e.tensor_tensor(out=TA[:], in0=G[1][:], in1=sT[:], op=ALU.mult))
            V([f"sin{tb}", 'Gre'], ['tb'], lambda e: e.tensor_tensor(out=TB[:], in0=G[0][:], in1=sT[:], op=ALU.mult))
            V([f"cos{tb}", 'Gre'], ['Gre'], lambda e: e.tensor_tensor(out=G[0][:], in0=G[0][:], in1=cT[:], op=ALU.mult))
            V([f"cos{tb}", 'Gim'], ['Gim'], lambda e: e.tensor_tensor(out=G[1][:], in0=G[1][:], in1=cT[:], op=ALU.mult))
            V(['Gre', 'ta'], ['Gre'], lambda e: e.tensor_tensor(out=G[0][:], in0=G[0][:], in1=TA[:], op=ALU.add))
            V(['Gim', 'tb'], ['Gim'], lambda e: e.tensor_tensor(out=G[1][:], in0=G[1][:], in1=TB[:], op=ALU.subtract))

        def stage_scan(qi):
            q = pairs[qi]
            rq = rho[:, q:q + 1].to_broadcast([128, 2048])
            R = Rb[qi % 2]
            for c in range(2):
                V(['rho', ('Gre', 'Gim')[c]], [(f"Rre{qi % 2}", f"Rim{qi % 2}")[c]],
                  lambda e: e.tensor_tensor_scan(out=R[c][:], data0=rq, data1=G[c][:], initial=0.0, op0=ALU.mult, op1=ALU.add))

        def stage_in_b(qi):
            tb, hb = qi % 2, qi % 2
            cT, sT = cosT[tb], sinT[tb]
            R = Rb[qi % 2]
            rre, rim = f"Rre{qi % 2}", f"Rim{qi % 2}"
            V([f"cos{tb}", rre], [f"H{hb}"], lambda e: e.tensor_tensor(out=H[hb][:, 0, :], in0=R[0][:], in1=cT[:], op=ALU.mult))
            V([f"sin{tb}", rim], [f"H{hb}"], lambda e: e.scalar_tensor_tensor(out=H[hb][:, 1, :], in0=R[1][:], scalar=-1.0, in1=sT[:], op0=ALU.mult, op1=ALU.mult))
            V([f"cos{tb}", rim], [f"H{hb}"], lambda e: e.tensor_tensor(out=H[hb][:, 2, :], in0=R[1][:], in1=cT[:], op=ALU.mult))
            V([f"sin{tb}", rre], [f"H{hb}"], lambda e: e.tensor_tensor(out=H[hb][:, 3, :], in0=R[0][:], in1=sT[:], op=ALU.mult))

        def stage_out(qi):
            q = pairs[qi]
            ct, b, hb = q // 4, q % 4, qi % 2
            ci = ci_of[ct]
            j = ci % 2
            myp = [qq for qq in pairs if qq // 4 == ct]
            if q == myp[0]:
                open_ct(ct)
            last = (q == myp[-1])
            fns = []
            for sg in range(4):
                sl = slice(sg * 512, (sg + 1) * 512)
                fns.append(lambda e, sl=sl: e.matmul(yps[:, sl], lhsT=cw[j][:, 0, b, :], rhs=H[hb][:, 0, sl], start=False, stop=False))
                fns.append(lambda e, sl=sl: e.matmul(yps[:, sl], lhsT=cw[j][:, 0, b, :], rhs=H[hb][:, 1, sl], start=False, stop=False))
                fns.append(lambda e, sl=sl: e.matmul(yps[:, sl], lhsT=cw[j][:, 1, b, :], rhs=H[hb][:, 2, sl], start=False, stop=False))
                fns.append(lambda e, sl=sl: e.matmul(yps[:, sl], lhsT=cw[j][:, 1, b, :], rhs=H[hb][:, 3, sl], start=False, stop=last))
            k.pe([f"E_cw{j}", f"E_H{hb}"], ['E_yps'], fns)
            if last:
                ygf = yg[j][:].rearrange("p m r i -> p (m r i)")
                for sg in range(4):
                    sl = slice(sg * 512, (sg + 1) * 512)
                    A(['yps'], ['ta'], lambda e: e.activation(out=TA[:, sl], in_=yps[:, sl], func=AF.Copy))
                    A(['yps'], ['tb'], lambda e: e.activation(out=TB[:, sl], in_=yps[:, sl], func=AF.Square))
                V(['tb'], ['tb'], lambda e: e.tensor_scalar(out=TB[:], in0=TB[:], scalar1=0.044715, scalar2=1.0, op0=ALU.mult, op1=ALU.add))
                V(['ta', 'tb'], ['tb'], lambda e: e.tensor_tensor(out=TB[:], in0=TB[:], in1=TA[:], op=ALU.mult))
                A(['tb'], ['tb'], lambda e: e.activation(out=TB[:], in_=TB[:], func=AF.Sigmoid, scale=1.5957691216057308))
                V(['ta', 'tb'], [f"yg{j}"], lambda e: e.tensor_tensor(out=ygf, in0=TA[:], in1=TB[:], op=ALU.mult))
                for r in range(2):
                    k.dma('act', [f"E_yg{j}"], ['ygT'], out=io['ygT'][r, ct * 128:(ct + 1) * 128, :].rearrange("c (m i) -> c m i", i=128),
                          in_=yg[j][:, :, r, :], key=f"E_yg{j}")

        load_ct(0)
        if len(cts) > 1:
            load_ct(1)
        tables_a(0)
        tables_b(0)
        npairs = len(pairs)
        for qi in range(npairs):
            q = pairs[qi]
            stage_in(qi)
            if qi + 1 < npairs:
                tables_a(qi + 1)
            stage_rot(qi)
            if qi + 1 < npairs:
                tables_b(qi + 1)
            stage_scan(qi)
            stage_in_b(qi)
            if qi >= 1:
                stage_out(qi - 1)
                qp = pairs[qi - 1]
                if qp // 4 != q // 4 and ci_of[qp // 4] + 2 < len(cts):
                    load_ct(ci_of[qp // 4] + 2)
        stage_out(npairs - 1)


def ssm_layouts(r, A_re, A_im, log_dt, B_re, B_im, C_re, C_im, Dv):
    g0 = 128 * r
    Bp = [np.zeros((128, 64, 128), np.float32) for _ in range(2)]
    Cp = [np.zeros((128, 64, 128), np.float32) for _ in range(2)]
    for q in range(64):
        b = q % 4
        for gg in range(2):
            g = g0 + 2 * q + gg
            rows = slice(32 * b + 16 * gg, 32 * b + 16 * gg + 16)
            cols = slice(64 * gg, 64 * gg + 64)
            Bp[0][rows, q, cols] = B_re[g].T
            Bp[1][rows, q, cols] = B_im[g].T
            Cp[0][cols, q, rows] = C_re[g].T
            Cp[1][cols, q, rows] = C_im[g].T
    Dd = np.zeros((128, 16, 128), np.float32)
    for ct in range(16):
        Dd[np.arange(128), ct, np.arange(128)] = Dv[2048 * r + 128 * ct: 2048 * r + 128 * (ct + 1)]
    tr = lambda a: np.ascontiguousarray(a[g0:g0 + 128].reshape(64, 2, 64).transpose(1, 2, 0).reshape(128, 64))
    ldt = np.ascontiguousarray(np.broadcast_to(log_dt[g0:g0 + 128].reshape(64, 2, 1), (64, 2, 64)).transpose(1, 2, 0).reshape(128, 64))
    return dict(Bp_re=Bp[0], Bp_im=Bp[1], Cp_re=Cp[0], Cp_im=Cp[1], Ddiag=Dd, A_re_T=tr(A_re), A_im_T=tr(A_im),
                logdt_T=ldt.astype(np.float32))


def kernel(x, norm_g, attn_w_in, attn_q_g, attn_k_g, attn_w_out, ssm_w_in, ssm_A_re, ssm_A_im, ssm_log_dt,
           ssm_B_re, ssm_B_im, ssm_C_re, ssm_C_im, ssm_D, ssm_glu_w, ssm_glu_b, ssm_w_out):
    f32 = lambda a: np.ascontiguousarray(np.asarray(a, np.float32))
    x = f32(x)
    cores = list(range(8))
    bf = lambda a: np.asarray(a, np.float32).astype(NPBF)
    norm_g = f32(norm_g)
    lay = [ssm_layouts(r, f32(ssm_A_re)[0], f32(ssm_A_im)[0], f32(ssm_log_dt)[0], f32(ssm_B_re)[0], f32(ssm_B_im)[0],
                       f32(ssm_C_re)[0], f32(ssm_C_im)[0], f32(ssm_D)[0]) for r in range(2)]
    shared = dict(
        w_in=f32(attn_w_in)[0], w_out=f32(attn_w_out)[0], ssm_w_in=f32(ssm_w_in)[0], glu_w=f32(ssm_glu_w)[0],
        ssm_w_out=f32(ssm_w_out)[0],
        gk0=np.ascontiguousarray(norm_g[0].reshape(32, 128).T), gk1=np.ascontiguousarray(norm_g[1].reshape(32, 128).T),
        qg=f32(attn_q_g)[0].reshape(128, 1), kg=f32(attn_k_g)[0].reshape(128, 1),
        ident=bf(np.eye(128)), ones=bf(np.ones((128, 128))), eps=np.full((128, 1), EPS, np.float32),
        zcol=np.zeros((128, 1), np.float32), halfpi=np.full((128, 1), np.pi / 2, np.float32),
        glub=np.ascontiguousarray(f32(ssm_glu_b)[0].reshape(32, 128).T),
        iota=np.ascontiguousarray(np.broadcast_to(np.arange(2048, dtype=np.float32), (128, 2048))))
    in_maps = []
    for c in cores:
        b, r = c // 2, c % 2
        sel = np.zeros((128, 2), np.float32)
        sel[:, r] = 1.0
        m = dict(shared)
        m.update(x=np.ascontiguousarray(x[b].reshape(8, 2, 128, D)[:, r].reshape(NTOK, D)), negm=_negmask(r), sel=sel, **lay[r])
        in_maps.append(m)
    res = run_bass_kernel_spmd(build_fused(), in_maps, core_ids=cores).results
    out = np.zeros((4, 2048, D), np.float32)
    for c in cores:
        b, r = c // 2, c % 2
        out[b].reshape(8, 2, 128, D)[:, r] = np.asarray(res[c]['out'], np.float32).reshape(8, 128, D)
    return out
```

```python
import numpy as np
import os
from contextlib import ExitStack
import ml_dtypes
import concourse.bass as bass
import concourse.mybir as mybir
from concourse.bass_utils import run_bass_kernel_spmd

F32 = mybir.dt.float32
BF16 = mybir.dt.bfloat16
ALU = mybir.AluOpType
AF = mybir.ActivationFunctionType
NPBF = ml_dtypes.bfloat16

D = 4096
NTOK = 1024
KC = 32
FW = 256
EPS = 1e-6
MAGIC = 12582912.0
TWO_PI = 6.283185307179586


class K:
    def __init__(self, nc):
        self.nc = nc
        self.eng = {'pe': nc.tensor, 'act': nc.scalar, 'dve': nc.vector, 'pool': nc.gpsimd, 'sp': nc.sync}
        self.prog = {}
        self.waited = {}
        self.lastw = {}
        self.readers = {}
        self.dsem = {}
        self.nsem = 0
        self.out_tokens = []
        self.old_tokens = []

    def _newsem(self, tag):
        self.nsem += 1
        return self.nc.alloc_semaphore(f"s_{tag}_{self.nsem}")

    def _wait(self, e, tok):
        sem, val, owner, sid = tok
        key = (e, sid)
        if self.waited.get(key, 0) >= val:
            return
        self.eng[e].wait_ge(sem, val)
        self.waited[key] = val

    def _deps(self, e, reads, writes):
        toks = []
        for b in list(reads) + list(writes):
            for t in self.lastw.get(b, {}).values():
                toks.append((t, True))
        for b in writes:
            for t in self.readers.get(b, []):
                toks.append((t, False))
        for t, is_w in toks:
            if t[2] == e:
                if e in ('pe', 'sp'):
                    continue
            self._wait(e, t)

    def _mark(self, e, ins):
        p = self.prog.get(e)
        if p is None or p[1] >= 6000:
            if p is not None:
                self.old_tokens.append((p[0], p[1], e, p[2]))
            p = [self._newsem(e), 0, self.nsem]
            self.prog[e] = p
        p[1] += 1
        ins.then_inc(p[0], 1)
        return (p[0], p[1], e, p[2])

    def _record(self, tok, reads, writes):
        for b in writes:
            self.lastw.setdefault(b, {})[tok[2] if tok[2] != 'dma' else ('dma', tok[3])] = tok
            self.readers[b] = []
        for b in reads:
            self.readers.setdefault(b, []).append(tok)

    def op(self, e, reads, writes, fn):
        self._deps(e, reads, writes)
        ins = fn(self.eng[e])
        tok = self._mark(e, ins)
        self._record(tok, reads, writes)
        return tok

    def pe(self, reads, writes, fns):
        self._deps('pe', reads, writes)
        ins = None
        for f in fns:
            ins = f(self.nc.tensor)
        tok = self._mark('pe', ins)
        self._record(tok, reads, writes)
        return tok

    def dma(self, q, reads, writes, out, in_, key, final=False):
        self._deps(q, reads, writes)
        d = self.dsem.get(key)
        if d is None or d[1] >= 1500:
            if d is not None:
                self.old_tokens.append((d[0], 16 * d[1], 'dma', d[2]))
            d = [self._newsem('d'), 0, self.nsem]
            self.dsem[key] = d
        ins = self.eng[q].dma_start(out=out, in_=in_)
        d[1] += 1
        ins.then_inc(d[0], 16)
        tok = (d[0], 16 * d[1], 'dma', d[2])
        self._record(tok, reads, writes)
        if final:
            self.out_tokens.append(tok)
        return tok

    def collective(self, kind, groups, in_ap, out_ap, in_key, out_key, scratch):
        self._deps('pool', [in_key], [out_key])
        if not hasattr(self, 'ccsem'):
            self.ccsem = [self._newsem('cc'), 0]
        ins = self.nc.gpsimd.collective_compute(kind, ALU.bypass, replica_groups=groups,
                                                ins=[in_ap.opt()], outs=[out_ap.opt()])
        self.ccsem[1] += 1
        ins.then_inc(self.ccsem[0])
        self.nc.gpsimd.wait_ge(self.ccsem[0], self.ccsem[1])
        return self.op('pool', [in_key], [out_key], lambda e: e.memset(scratch, 0.0))

    def barrier(self):
        toks = []
        for e, p in self.prog.items():
            toks.append((p[0], p[1], e, p[2]))
        for key, d in self.dsem.items():
            toks.append((d[0], 16 * d[1], 'dma', d[2]))
        toks += self.old_tokens
        for e in self.eng:
            for t in toks:
                if t[2] == e:
                    continue
                self._wait(e, t)

    def finish(self, keys=()):
        for tok in self.out_tokens:
            self._wait('sp', tok)
        for key in keys:
            for t in self.lastw.get(key, {}).values():
                self._wait('sp', t)


def sb(nc, es, name, shape, dt):
    return es.enter_context(nc.sbuf_tensor(name, shape, dt))


def ps(nc, es, name, shape, dt=F32):
    return es.enter_context(nc.psum_tensor(name, shape, dt))


def load_const(nc, k, es, name, src, shape, dt):
    t = sb(nc, es, name, shape, dt)
    k.dma('sp', [], [name], out=t[:], in_=src, key=name)
    return t


def dense_phase(nc, k, tag, *, W, units, act_src, gk=None, consts=None, extra_alloc=None):
    with ExitStack() as es:
        actT = sb(nc, es, f"{tag}_actT", [128, KC, NTOK], BF16)
        wst = [sb(nc, es, f"{tag}_wst{i}", [128, KC, FW], F32) for i in range(2)]
        wbf = [sb(nc, es, f"{tag}_wbf{i}", [128, KC, FW], BF16) for i in range(2)]
        psum = [ps(nc, es, f"{tag}_ps{i}", [128, 2048]) for i in range(2)]
        st = extra_alloc(nc, es) if extra_alloc else {}
        st.update(actT=actT, psum=psum, tag=tag)
        Wv = W.rearrange("(kc p) n -> p kc n", p=128)
        n_units = len(units)

        def load_w(u):
            s = u % 2
            c0 = units[u]['col0']
            for j in range(4):
                k.dma('sp', [], [f"{tag}_wst{s}"],
                      out=wst[s][:, j * 8:(j + 1) * 8, :], in_=Wv[:, j * 8:(j + 1) * 8, c0:c0 + FW],
                      key=f"{tag}_wst{s}")

        def conv_w(u):
            s = u % 2
            for j in range(4):
                sl = slice(j * 8, (j + 1) * 8)
                eng = ('pool', 'pool', 'dve', 'act')[j]
                if gk is not None:
                    if eng == 'act':
                        for kc in range(j * 8, (j + 1) * 8):
                            k.op('act', [f"{tag}_wst{s}", 'c_gk'], [f"{tag}_wbf{s}"],
                                 lambda e: e.activation(out=wbf[s][:, kc, :], in_=wst[s][:, kc, :], func=AF.Copy,
                                                        scale=gk[:, kc:kc + 1]))
                    else:
                        k.op(eng, [f"{tag}_wst{s}", 'c_gk'], [f"{tag}_wbf{s}"],
                             lambda e: e.tensor_tensor(out=wbf[s][:, sl, :], in0=wst[s][:, sl, :],
                                                       in1=gk[:, sl].unsqueeze(2).to_broadcast([128, 8, FW]),
                                                       op=ALU.mult))
                elif eng == 'act':
                    k.op('act', [f"{tag}_wst{s}"], [f"{tag}_wbf{s}"],
                         lambda e: e.activation(out=wbf[s][:, sl, :], in_=wst[s][:, sl, :], func=AF.Copy))
                else:
                    k.op(eng, [f"{tag}_wst{s}"], [f"{tag}_wbf{s}"],
                         lambda e: e.tensor_copy(out=wbf[s][:, sl, :], in_=wst[s][:, sl, :]))

        def mm(u):
            s = u % 2
            fns = []
            if units[u]['orient'] == 'F':
                for ft in range(2):
                    for hf in range(2):
                        bank = ft * 2 + hf
                        for kc in range(KC):
                            fns.append(lambda e, ft=ft, hf=hf, kc=kc, bank=bank: e.matmul(
                                psum[s][:, bank * 512:(bank + 1) * 512],
                                lhsT=wbf[s][:, kc, ft * 128:(ft + 1) * 128],
                                rhs=actT[:, kc, hf * 512:(hf + 1) * 512],
                                start=(kc == 0), stop=(kc == KC - 1)))
            else:
                for tt in range(8):
                    for kc in range(KC):
                        fns.append(lambda e, tt=tt, kc=kc: e.matmul(
                            psum[s][:, tt * FW:(tt + 1) * FW],
                            lhsT=actT[:, kc, tt * 128:(tt + 1) * 128],
                            rhs=wbf[s][:, kc, :],
                            start=(kc == 0), stop=(kc == KC - 1)))
            k.pe([f"{tag}_wbf{s}", f"{tag}_actT"], [f"{tag}_ps{s}"], fns)

        load_w(0)
        if act_src[0] == 'load':
            aT = act_src[1].rearrange("(kc p) t -> p kc t", p=128)
            for j in range(4):
                k.dma('sp', list(act_src[2]), [f"{tag}_actT"],
                      out=actT[:, j * 8:(j + 1) * 8, :], in_=aT[:, j * 8:(j + 1) * 8, :],
                      key=f"{tag}_actT")
        elif act_src[0] == 'load2':
            cands, keys, sel = act_src[1], list(act_src[2]), act_src[3]
            tmpb = st['tmpb']
            for j in range(4):
                sl = slice(j * 8, (j + 1) * 8)
                k.dma('sp', keys, [f"{tag}_actT"], out=actT[:, sl, :], in_=cands[0](j), key=f"{tag}_actT")
                k.dma('sp', keys, [f"{tag}_tmpb"], out=tmpb[:], in_=cands[1](j), key=f"{tag}_tmpb")
                k.op('act', [f"{tag}_actT", 'c_sel'], [f"{tag}_actT"],
                     lambda e: e.activation(out=actT[:, sl, :], in_=actT[:, sl, :], func=AF.Copy, scale=sel[:, 0:1]))
                k.op('dve', [f"{tag}_actT", f"{tag}_tmpb", 'c_sel'], [f"{tag}_actT"],
                     lambda e: e.scalar_tensor_tensor(out=actT[:, sl, :], in0=tmpb[:], scalar=sel[:, 1:2],
                                                      in1=actT[:, sl, :], op0=ALU.mult, op1=ALU.add))
        else:
            x_ap = act_src[1]
            xkeys = list(act_src[2])
            ident = consts['ident']
            epst = consts['eps']
            xin = [wst[1][:, 0:16, :], wst[1][:, 16:32, :]]
            hn = sb(nc, es, f"{tag}_hn", [128, D], BF16)
            ss = sb(nc, es, f"{tag}_ss", [128, 8], F32)
            rstd = sb(nc, es, f"{tag}_rstd", [128, 8], F32)
            k.op('dve', [], [f"{tag}_ss{t}" for t in range(8)], lambda e: e.memset(ss[:], 0.0))
            for tt in range(8):
                j = tt % 2
                xv = xin[j].rearrange("p a b -> p (a b)")
                k.dma('sp', xkeys, [f"{tag}_xin{j}"], out=xv, in_=x_ap[tt * 128:(tt + 1) * 128, :],
                      key=f"{tag}_xin{j}")
                k.op('act', [f"{tag}_xin{j}"], [f"{tag}_hn", f"{tag}_ss{tt}"],
                     lambda e: e.activation(out=hn[:], in_=xv, func=AF.Square,
                                            accum_out=ss[:, tt:tt + 1]))
                k.op('act', [f"{tag}_ss{tt}", 'c_eps'], [f"{tag}_ss{tt}"],
                     lambda e: e.activation(out=ss[:, tt:tt + 1], in_=ss[:, tt:tt + 1], func=AF.Sqrt,
                                            bias=epst[:, 0:1], scale=1.0 / D))
                k.op('dve', [f"{tag}_ss{tt}"], [f"{tag}_rstd{tt}"],
                     lambda e: e.reciprocal(out=rstd[:, tt:tt + 1], in_=ss[:, tt:tt + 1]))
                k.op('act', [f"{tag}_xin{j}", f"{tag}_rstd{tt}"], [f"{tag}_hn"],
                     lambda e: e.activation(out=hn[:], in_=xv, func=AF.Copy,
                                            scale=rstd[:, tt:tt + 1]))
                for g in range(4):
                    s = g % 2
                    fns = []
                    for i in range(8):
                        kc = g * 8 + i
                        fns.append(lambda e, i=i, kc=kc, s=s: e.matmul(
                            psum[s][:, i * 128:(i + 1) * 128],
                            lhsT=hn[:, kc * 128:(kc + 1) * 128], rhs=ident[:],
                            start=True, stop=True))
                    k.pe([f"{tag}_hn", 'c_ident'], [f"{tag}_ps{s}"], fns)
                    src = psum[s][:, 0:1024].rearrange("p (a b) -> p a b", a=8)
                    dst = actT[:, g * 8:(g + 1) * 8, tt * 128:(tt + 1) * 128]
                    if g % 2 == 0:
                        k.op('dve', [f"{tag}_ps{s}"], [f"{tag}_actT"],
                             lambda e: e.tensor_copy(out=dst, in_=src))
                    else:
                        k.op('act', [f"{tag}_ps{s}"], [f"{tag}_actT"],
                             lambda e: e.activation(out=dst, in_=src, func=AF.Copy))
            k.readers.setdefault(f"{tag}_wst1", [])
            for j in range(2):
                k.readers[f"{tag}_wst1"] += k.readers.get(f"{tag}_xin{j}", []) + list(k.lastw.get(f"{tag}_xin{j}", {}).values())

        conv_w(0)
        if n_units > 1:
            load_w(1)
        prefetch(nc, k, st, 0, units[0])
        for u in range(n_units):
            if u + 1 < n_units:
                prefetch(nc, k, st, u + 1, units[u + 1])
            mm(u)
            if u + 1 < n_units:
                conv_w(u + 1)
            if u + 2 < n_units:
                load_w(u + 2)
            evac_unit(nc, k, st, u, units[u])


def prefetch(nc, k, st, u, unit):
    tag = st['tag']
    s = u % 2
    kind = unit['kind']
    if kind == 'residT':
        c0 = unit['col0']
        src = unit['res'][:, c0:c0 + FW].rearrange("(tt p) c -> p tt c", p=128)
        k.dma('sp', list(unit.get('res_keys', [])), [f"{tag}_outR{s}"], out=st['outR'][s][:], in_=src,
              key=f"{tag}_xres{s}")
    elif kind == 'gluF':
        c0 = unit['col0']
        src = unit['sg'][c0:c0 + FW, :].rearrange("(ft p) t -> p ft t", p=128)
        k.dma('sp', list(unit.get('sg_keys', [])), [f"{tag}_sgs{s}"], out=st['sgs'][s][:], in_=src,
              key=f"{tag}_sgs{s}")


def evac_unit(nc, k, st, u, unit, part='all'):
    tag = st['tag']
    s = u % 2
    kind = unit['kind']
    P = st['psum'][s]
    pk = f"{tag}_ps{s}"
    if kind in ('copyF', 'siluF', 'qkF', 'gluF', 'vT'):
        oF = st['outF'][s]
        ok = f"{tag}_outF{s}"
    if kind == 'copyF' or kind == 'siluF':
        for ft in range(2):
            for hf in range(2):
                b = ft * 2 + hf
                src = P[:, b * 512:(b + 1) * 512]
                dst = oF[:, ft, hf * 512:(hf + 1) * 512]
                if kind == 'siluF':
                    k.op('act', [pk], [ok], lambda e: e.activation(out=dst, in_=src, func=AF.Silu))
                elif b % 2 == 0:
                    k.op('act', [pk], [ok], lambda e: e.activation(out=dst, in_=src, func=AF.Copy))
                else:
                    k.op('dve', [pk], [ok], lambda e: e.tensor_copy(out=dst, in_=src))
    elif kind == 'qkF':
        qraw, sq, rt = st['qraw'], st['sq'], st['rt']
        ones, epst, gcol = st['ones'], st['eps'], unit['g']
        if part in ('all', 'early'):
            for b in range(4):
                src = P[:, b * 512:(b + 1) * 512]
                k.op('act', [pk], [f"{tag}_qraw{b}"],
                     lambda e: e.activation(out=qraw[:, b, :], in_=src, func=AF.Copy))
                k.op('act', [pk], [f"{tag}_sq{b}"],
                     lambda e: e.activation(out=sq[:, b, :], in_=src, func=AF.Square))
        if part == 'early':
            return
        for ft in range(2):
            for hf in range(2):
                b = ft * 2 + hf
                k.pe([f"{tag}_sq{b}", 'c_ones'], [pk],
                     [lambda e, b=b: e.matmul(P[:, b * 512:(b + 1) * 512], lhsT=ones[:], rhs=sq[:, b, :],
                                              start=True, stop=True)])
            for hf in range(2):
                b = ft * 2 + hf
                src = P[:, b * 512:(b + 1) * 512]
                k.op('act', [pk, 'c_eps'], [f"{tag}_rt{hf}"],
                     lambda e: e.activation(out=rt[:, hf, :], in_=src, func=AF.Sqrt, bias=epst[:, 0:1],
                                            scale=1.0 / 128))
                k.op('dve', [f"{tag}_rt{hf}"], [f"{tag}_rt{hf}"],
                     lambda e: e.reciprocal(out=rt[:, hf, :], in_=rt[:, hf, :]))
                k.op('dve', [f"{tag}_qraw{b}", f"{tag}_rt{hf}", 'c_qg', 'c_kg'], [ok],
                     lambda e: e.scalar_tensor_tensor(out=oF[:, ft, hf * 512:(hf + 1) * 512], in0=qraw[:, b, :],
                                                      scalar=gcol, in1=rt[:, hf, :], op0=ALU.mult, op1=ALU.mult))
    elif kind == 'gluF':
        sig, sgs, actT, bias = st['sig'], st['sgs'][s], st['actT'], st['glub']
        for b in range(4):
            ft, hf = b // 2, b % 2
            fidx = unit['col0'] // 128 + ft
            src = P[:, b * 512:(b + 1) * 512]
            k.op('act', [pk, 'c_glub'], [f"{tag}_sig{b}"],
                 lambda e: e.activation(out=sig[:, b, :], in_=src, func=AF.Sigmoid, bias=bias[:, fidx:fidx + 1]))
            k.op('dve', [f"{tag}_sig{b}", f"{tag}_actT"], [f"{tag}_sig{b}"],
                 lambda e: e.tensor_tensor(out=sig[:, b, :], in0=sig[:, b, :],
                                           in1=actT[:, fidx, hf * 512:(hf + 1) * 512], op=ALU.mult))
            k.op('dve', [f"{tag}_sig{b}", f"{tag}_sgs{s}"], [ok],
                 lambda e: e.tensor_tensor(out=oF[:, ft, hf * 512:(hf + 1) * 512], in0=sig[:, b, :],
                                           in1=sgs[:, ft, hf * 512:(hf + 1) * 512], op=ALU.mult))
    if kind in ('copyF', 'siluF', 'qkF', 'gluF'):
        r0 = unit['row0']
        dst = unit['dst'][r0:r0 + FW, :].rearrange("(ft p) t -> p ft t", p=128)
        k.dma('act', [ok], [unit['dst_key']], out=dst, in_=oF[:], key=ok)
    elif kind == 'vT':
        oT = oF[:].rearrange("p a (b c) -> p (a b) c", b=4)
        for h in range(2):
            src = P[:, h * 1024:(h + 1) * 1024].rearrange("p (a b) -> p a b", a=4)
            dst = oT[:, h * 4:(h + 1) * 4, :]
            if h == 0:
                k.op('act', [pk], [ok], lambda e: e.activation(out=dst, in_=src, func=AF.Copy))
            else:
                k.op('dve', [pk], [ok], lambda e: e.tensor_copy(out=dst, in_=src))
        r0 = unit['row0']
        dst = unit['dst'][:, r0:r0 + FW].rearrange("(tt p) c -> p tt c", p=128)
        k.dma('act', [ok], [unit['dst_key']], out=dst, in_=oT, key=ok)
    elif kind == 'residT':
        oR = st['outR'][s]
        ok = f"{tag}_outR{s}"
        xr = oR
        for h in range(2):
            src = P[:, h * 1024:(h + 1) * 1024].rearrange("p (a b) -> p a b", a=4)
            k.op('dve', [pk], [ok],
                 lambda e: e.tensor_tensor(out=oR[:, h * 4:(h + 1) * 4, :], in0=src,
                                           in1=xr[:, h * 4:(h + 1) * 4, :], op=ALU.add))
        c0 = unit['col0']
        dst = unit['dst'][:, c0:c0 + FW].rearrange("(tt p) c -> p tt c", p=128)
        k.dma('act', [ok], [unit['dst_key']], out=dst, in_=oR[:], key=ok, final=unit.get('final', False))


def phase_A(nc, k, io, cst):
    def alloc(nc_, es):
        st = {}
        st['outF'] = [sb(nc, es, f"A_outF{i}", [128, 2, NTOK], BF16) for i in range(2)]
        st['qraw'] = sb(nc, es, "A_qraw", [128, 4, 512], F32)
        st['sq'] = sb(nc, es, "A_sq", [128, 4, 512], BF16)
        st['rt'] = sb(nc, es, "A_rt", [128, 2, 512], F32)
        st['ones'] = cst['ones']
        st['eps'] = cst['eps']
        return st
    units = []
    for c0 in range(0, 4096, FW):
        units.append(dict(col0=c0, kind='qkF', orient='F', g=cst['qg'][:, 0:1], dst=io['qT'], row0=c0, dst_key='qT'))
    for c0 in range(0, 4096, FW):
        units.append(dict(col0=4096 + c0, kind='qkF', orient='F', g=cst['kg'][:, 0:1], dst=io['kT'], row0=c0, dst_key='kT'))
    for c0 in range(0, 4096, FW):
        units.append(dict(col0=8192 + c0, kind='vT', orient='T', dst=io['v'], row0=c0, dst_key='v'))
    for c0 in range(0, 4096, FW):
        units.append(dict(col0=12288 + c0, kind='siluF', orient='F', dst=io['sgT'], row0=c0, dst_key='sgT'))
    if io.get('limit_units'):
        units = [units[i] for i in io['limit_units']]
    dense_phase(nc, k, 'A', W=io['w_in'], units=units, act_src=('rms', io['x'], io.get('x_keys', [])),
                gk=cst['gk0'], consts=cst, extra_alloc=alloc)


def phase_B(nc, k, io, cst, heads=range(32), stage=99):
    SC = 1.0 / float(np.sqrt(128.0))
    ident = cst['ident']
    negm = cst['negm']
    zcol = cst['zcol']
    with ExitStack() as es:
        kTh = [sb(nc, es, f"B_k{i}", [128, 8, 2, 128], BF16) for i in range(2)]
        vh = [sb(nc, es, f"B_v{i}", [128, 8, 2, 128], BF16) for i in range(2)]
        qTh = [sb(nc, es, f"B_q{i}", [128, NTOK], BF16) for i in range(2)]
        sgh = [sb(nc, es, f"B_sg{i}", [128, NTOK], BF16) for i in range(2)]
        goh = [sb(nc, es, f"B_go{i}", [128, NTOK], BF16) for i in range(2)]
        L1 = int(os.environ.get('B_L1', '2'))
        L2 = int(os.environ.get('B_L2', '1'))
        N1, N2 = L1 + 1, L2 + 1
        bt = [sb(nc, es, f"B_bt{i}", [128, 2048], F32) for i in range(N1)]
        ob = [sb(nc, es, f"B_ob{i}", [128, 2048], F32) for i in range(N1)]
        inc = [sb(nc, es, f"B_inc{i}", [128, 2056], F32) for i in range(N2)]
        wq = [sb(nc, es, f"B_wq{i}", [128, 2048], BF16) for i in range(N2)]
        wT = [sb(nc, es, f"B_wT{i}", [128, 16, 4, 128], BF16) for i in range(2)]
        zm = [sb(nc, es, f"B_zm{i}", [128, 256], F32) for i in range(N1)]
        psz = ps(nc, es, "B_psz", [128, 2048])
        pst = [ps(nc, es, f"B_pst{i}", [128, 512]) for i in range(2)]
        pso = ps(nc, es, "B_pso", [128, 1024])
        kT_rows = io.get('kT_rows') or (lambda r, h: io['kT_all'][r, h * 128:(h + 1) * 128, :])
        v_parts = io.get('v_parts') or (lambda r, h: [(0, 8, io['v_all'][r, :, h * 128:(h + 1) * 128])])

        def load_head(h, hb):
            for r in range(2):
                k.dma('sp', ['kT_all'], [f"B_k{hb}"], out=kTh[hb][:, :, r, :],
                      in_=kT_rows(r, h).rearrange("d (m i) -> d m i", i=128), key=f"B_k{hb}")
                for m0, nm, vap in v_parts(r, h):
                    k.dma('sp', ['v_all'], [f"B_v{hb}"], out=vh[hb][:, m0:m0 + nm, r, :],
                          in_=vap.rearrange("(m p) d -> p m d", p=128), key=f"B_v{hb}")
            k.dma('sp', ['qT'], [f"B_q{hb}"], out=qTh[hb][:], in_=io['qT'][h * 128:(h + 1) * 128, :], key=f"B_q{hb}")
            k.dma('sp', ['sgT'], [f"B_sg{hb}"], out=sgh[hb][:], in_=io['sgT'][h * 128:(h + 1) * 128, :], key=f"B_sg{hb}")

        heads = list(heads)
        blocks = [(hi, m) for hi in range(len(heads)) for m in range(8)]
        tcnt = [0]

        def stage1(bi):
            hi, m = blocks[bi]
            hb, mb = hi % 2, bi % N1
            kl = 256 * (m + 1)
            kflat = kTh[hb][:].rearrange("p m r i -> p (m r i)")
            fns = []
            for c0 in range(0, kl, 512):
                c1 = min(kl, c0 + 512)
                fns.append(lambda e, c0=c0, c1=c1: e.matmul(psz[:, c0:c1], lhsT=qTh[hb][:, m * 128:(m + 1) * 128],
                                                            rhs=kflat[:, c0:c1], start=True, stop=True))
            k.pe([f"B_q{hb}", f"B_k{hb}"], ["B_psz"], fns)
            k.op('dve', ["B_psz", 'c_negm'], [f"B_zm{mb}"],
                 lambda e: e.tensor_tensor(out=zm[mb][:], in0=psz[:, kl - 256:kl], in1=negm[:], op=ALU.add))
            for c0 in range(0, kl - 256, 512):
                c1 = min(kl - 256, c0 + 512)
                k.op('act', ["B_psz", f"B_zm{mb}"], [f"B_bt{mb}"],
                     lambda e: e.activation(out=bt[mb][:, c0:c1], in_=psz[:, c0:c1], func=AF.Sigmoid, scale=SC))
                k.op('act', ["B_psz", f"B_zm{mb}"], [f"B_ob{mb}"],
                     lambda e: e.activation(out=ob[mb][:, c0:c1], in_=psz[:, c0:c1], func=AF.Sigmoid, scale=-SC))
            k.op('act', [f"B_zm{mb}"], [f"B_bt{mb}"],
                 lambda e: e.activation(out=bt[mb][:, kl - 256:kl], in_=zm[mb][:], func=AF.Sigmoid, scale=SC))
            k.op('act', [f"B_zm{mb}"], [f"B_ob{mb}"],
                 lambda e: e.activation(out=ob[mb][:, kl - 256:kl], in_=zm[mb][:], func=AF.Sigmoid, scale=-SC))

        def stage2(bi):
            hi, m = blocks[bi]
            mb, m3 = bi % N2, bi % N1
            kl = 256 * (m + 1)
            k.op('dve', [], [f"B_inc{mb}"], lambda e: e.memset(inc[mb][:, kl:kl + 1], 1.0))
            ob_rev = bass.AP(ob[m3][:].tensor, ob[m3][:, kl - 1:kl].offset, [list(ob[m3][:].ap[0]), [-1, kl]])
            inc_rev = bass.AP(inc[mb][:].tensor, inc[mb][:, kl - 1:kl].offset, [list(inc[mb][:].ap[0]), [-1, kl]])
            k.op('dve', [f"B_ob{m3}", 'c_zcol', f"B_inc{mb}"], [f"B_inc{mb}"],
                 lambda e: e.tensor_tensor_scan(out=inc_rev, data0=zcol[:, 0:1].to_broadcast([128, kl]),
                                                data1=ob_rev, initial=1.0, op0=ALU.add, op1=ALU.mult))
            for c0 in range(0, kl, 1024):
                c1 = min(kl, c0 + 1024)
                k.op('dve', [f"B_bt{m3}", f"B_inc{mb}"], [f"B_wq{mb}"],
                     lambda e: e.tensor_tensor(out=wq[mb][:, c0:c1], in0=bt[m3][:, c0:c1], in1=inc[mb][:, c0 + 1:c1 + 1], op=ALU.mult))

        def stage3(bi):
            hi, m = blocks[bi]
            hb, wb = hi % 2, bi % N2
            a, mi = m // 4, m % 4
            gi = hi * 2 + a
            gb = gi % 2
            nkb = 2 * (m + 1)
            vflat = vh[hb][:].rearrange("p m r d -> p (m r) d")
            for g0 in range(0, nkb, 4):
                g1 = min(nkb, g0 + 4)
                tb = tcnt[0] % 2
                tcnt[0] += 1
                fns = [lambda e, kb=kb, tb=tb, g0=g0: e.matmul(pst[tb][:, (kb - g0) * 128:(kb - g0 + 1) * 128],
                                                               lhsT=wq[wb][:, kb * 128:(kb + 1) * 128], rhs=ident[:],
                                                               start=True, stop=True) for kb in range(g0, g1)]
                k.pe([f"B_wq{wb}", 'c_ident'], [f"B_pst{tb}"], fns)
                src = pst[tb][:, 0:(g1 - g0) * 128].rearrange("p (a b) -> p a b", b=128)
                dst = wT[gb][:, g0:g1, mi, :]
                k.op('act', [f"B_pst{tb}"], [f"B_wT{gb}"], lambda e: e.activation(out=dst, in_=src, func=AF.Copy))
            if mi < 3:
                return
            ob_ = gb * 512
            nk = 2 * (4 * a + 4)
            fns = []
            for kb in range(nk):
                m0 = max(0, kb // 2 - 4 * a)
                fns.append(lambda e, kb=kb, m0=m0: e.matmul(
                    pso[:, ob_ + m0 * 128:ob_ + 512], lhsT=vflat[:, kb, :],
                    rhs=wT[gb][:, kb, m0:4, :].rearrange("p a b -> p (a b)"),
                    start=(kb == 0), stop=(kb == nk - 1), skip_group_check=True))
            k.pe([f"B_v{hb}", f"B_wT{gb}"], [f"B_pso{gb}"], fns)
            c0 = 4 * a * 128
            k.op('dve', [f"B_pso{gb}", f"B_sg{hb}"], [f"B_go{hb}"],
                 lambda e: e.tensor_tensor(out=goh[hb][:, c0:c0 + 512], in0=pso[:, ob_:ob_ + 512],
                                           in1=sgh[hb][:, c0:c0 + 512], op=ALU.mult))
            if m == 7:
                h = heads[hi]
                k.dma('act', [f"B_go{hb}"], ['goT'], out=io['goT'][h * 128:(h + 1) * 128, :], in_=goh[hb][:], key=f"B_go{hb}")

        nb = len(blocks)
        load_head(heads[0], 0)
        if len(heads) > 1:
            load_head(heads[1], 1)
        for i in range(min(L1, nb)):
            stage1(i)
        for i in range(min(L2, nb)):
            stage2(i)
        for bi in range(nb):
            if bi + L1 < nb:
                stage1(bi + L1)
            if bi + L2 < nb:
                stage2(bi + L2)
            stage3(bi)
            hi0, m0 = blocks[bi]
            if m0 == 7 and hi0 + 2 < len(heads):
                load_head(heads[hi0 + 2], hi0 % 2)


def phase_C(nc, k, io, cst):
    def alloc(nc_, es):
        return dict(outR=[sb(nc, es, f"C_outR{i}", [128, 8, FW], F32) for i in range(2)])
    units = [dict(col0=c0, kind='residT', orient='T', res=io['x'], dst=io['x1'], dst_key='x1', final=True)
             for c0 in range(0, 4096, FW)]
    dense_phase(nc, k, 'C', W=io['w_out'], units=units, act_src=('load', io['goT'], ['goT']),
                consts=cst, extra_alloc=alloc)


def _consts(nc, k, es, names, cin):
    return {n: load_const(nc, k, es, "c_" + {"gk0": "gk"}.get(n, n), cin[n], list(cin[n].shape), cin[n].dtype)
            for n in names}


def build_launch1():
    nc = bass.Bass("TRN2", target_bir_lowering=False)
    k = K(nc)
    dt = lambda n, s, d, kind: nc.dram_tensor(n, s, d, kind=kind).ap()
    io = dict(x=dt("x", [NTOK, D], F32, "ExternalInput"), w_in=dt("w_in", [D, 4 * D], F32, "ExternalInput"),
              qT=dt("qT", [D, NTOK], BF16, "ExternalOutput"), kT=dt("kT", [D, NTOK], BF16, "ExternalOutput"),
              v=dt("v", [NTOK, D], BF16, "ExternalOutput"), sgT=dt("sgT", [D, NTOK], BF16, "ExternalOutput"))
    cin = dict(gk0=dt("gk0", [128, 32], F32, "ExternalInput"), qg=dt("qg", [128, 1], F32, "ExternalInput"),
               kg=dt("kg", [128, 1], F32, "ExternalInput"), ident=dt("ident", [128, 128], BF16, "ExternalInput"),
               ones=dt("ones", [128, 128], BF16, "ExternalInput"), eps=dt("eps", [128, 1], F32, "ExternalInput"))
    with ExitStack() as es:
        cst = _consts(nc, k, es, list(cin), cin)
        phase_A(nc, k, io, cst)
        k.finish(['qT', 'kT', 'v', 'sgT'])
    return nc


def phase_D(nc, k, io, cst):
    def alloc(nc_, es):
        return dict(outF=[sb(nc, es, f"D_outF{i}", [128, 2, NTOK], BF16) for i in range(2)])
    units = [dict(col0=c0, kind='copyF', orient='F', dst=io['uT'], row0=c0, dst_key='uT') for c0 in range(0, D, FW)]
    units += [dict(col0=D + c0, kind='siluF', orient='F', dst=io['sg1T'], row0=c0, dst_key='sg1T') for c0 in range(0, D, FW)]
    dense_phase(nc, k, 'D', W=io['ssm_w_in'], units=units, act_src=('rms', io['x1'], ['x1']),
                gk=cst['gk1'], consts=cst, extra_alloc=alloc)


def phase_F(nc, k, io, cst):
    def alloc(nc_, es):
        return dict(outF=[sb(nc, es, f"F_outF{i}", [128, 2, NTOK], BF16) for i in range(2)],
                    sgs=[sb(nc, es, f"F_sgs{i}", [128, 2, NTOK], BF16) for i in range(2)],
                    sig=sb(nc, es, "F_sig", [128, 4, 512], F32), glub=cst['glub'],
                    **({'tmpb': sb(nc, es, "F_tmpb", [128, 8, NTOK], BF16)} if 'yg_cands' in io else {}))
    units = [dict(col0=c0, kind='gluF', orient='F', sg=io['sg1T'], sg_keys=['sg1T'], dst=io['mT'], row0=c0, dst_key='mT')
             for c0 in range(0, D, FW)]
    src = ('load2', io['yg_cands'], ['yG'], cst['sel']) if 'yg_cands' in io else ('load', io['ygT'], [])
    dense_phase(nc, k, 'F', W=io['glu_w'], units=units, act_src=src, consts=cst, extra_alloc=alloc)


def phase_G(nc, k, io, cst):
    def alloc(nc_, es):
        return dict(outR=[sb(nc, es, f"G_outR{i}", [128, 8, FW], F32) for i in range(2)])
    units = [dict(col0=c0, kind='residT', orient='T', res=io['x1'], res_keys=['x1'], dst=io['out'], dst_key='out', final=True)
             for c0 in range(0, D, FW)]
    dense_phase(nc, k, 'G', W=io['ssm_w_out'], units=units, act_src=('load', io['mT'], ['mT']), consts=cst,
                extra_alloc=alloc)


def build_launch2():
    nc = bass.Bass("TRN2", target_bir_lowering=False)
    k = K(nc)
    dt = lambda n, s, d, kind: nc.dram_tensor(n, s, d, kind=kind).ap()
    io = dict(qT=dt("qT", [D, NTOK], BF16, "ExternalInput"), kT_all=dt("kT_all", [2, D, NTOK], BF16, "ExternalInput"),
              v_all=dt("v_all", [2, NTOK, D], BF16, "ExternalInput"), sgT=dt("sgT", [D, NTOK], BF16, "ExternalInput"),
              goT=dt("goT", [D, NTOK], BF16, "Internal"),
              x=dt("x", [NTOK, D], F32, "ExternalInput"), w_out=dt("w_out", [D, D], F32, "ExternalInput"),
              x1=dt("x1", [NTOK, D], F32, "ExternalOutput"),
              ssm_w_in=dt("ssm_w_in", [D, 2 * D], F32, "ExternalInput"),
              uT=dt("uT", [D, NTOK], BF16, "ExternalOutput"), sg1T=dt("sg1T", [D, NTOK], BF16, "ExternalOutput"))
    cin = dict(ident=dt("ident", [128, 128], BF16, "ExternalInput"), negm=dt("negm", [128, 256], F32, "ExternalInput"),
               zcol=dt("zcol", [128, 1], F32, "ExternalInput"), eps=dt("eps", [128, 1], F32, "ExternalInput"),
               gk1=dt("gk1", [128, 32], F32, "ExternalInput"))
    with ExitStack() as es:
        cst = _consts(nc, k, es, list(cin), cin)
        phase_B(nc, k, io, cst)
        k.barrier()
        phase_C(nc, k, io, cst)
        k.barrier()
        phase_D(nc, k, io, cst)
        k.finish(['x1', 'uT', 'sg1T'])
    return nc


def build_launch3():
    nc = bass.Bass("TRN2", target_bir_lowering=False)
    k = K(nc)
    dt = lambda n, s, d, kind: nc.dram_tensor(n, s, d, kind=kind).ap()
    io = dict(uT_all=dt("uT_all", [2, 2048, NTOK], BF16, "ExternalInput"), ygT=dt("ygT", [2, 2048, NTOK], BF16, "ExternalOutput"))
    for n in ('Bp_re', 'Bp_im', 'Cp_re', 'Cp_im'):
        io[n] = dt(n, [128, 64, 128], F32, "ExternalInput")
    io['Ddiag'] = dt('Ddiag', [128, 16, 128], F32, "ExternalInput")
    for n in ('A_re_T', 'A_im_T', 'logdt_T'):
        io[n] = dt(n, [128, 64], F32, "ExternalInput")
    cin = dict(iota=dt("iota", [128, 2048], F32, "ExternalInput"), halfpi=dt("halfpi", [128, 1], F32, "ExternalInput"),
               zcol=dt("zcol", [128, 1], F32, "ExternalInput"))
    with ExitStack() as es:
        cst = _consts(nc, k, es, list(cin), cin)
        phase_E(nc, k, io, cst)
        k.finish(['ygT'])
    return nc


def build_launch4():
    nc = bass.Bass("TRN2", target_bir_lowering=False)
    k = K(nc)
    dt = lambda n, s, d, kind: nc.dram_tensor(n, s, d, kind=kind).ap()
    io = dict(ygT=dt("ygT_own", [D, NTOK], BF16, "ExternalInput"), sg1T=dt("sg1T", [D, NTOK], BF16, "ExternalInput"),
              glu_w=dt("glu_w", [D, D], F32, "ExternalInput"), mT=dt("mT", [D, NTOK], BF16, "Internal"),
              ssm_w_out=dt("ssm_w_out", [D, D], F32, "ExternalInput"), x1=dt("x1", [NTOK, D], F32, "ExternalInput"),
              out=dt("out", [NTOK, D], F32, "ExternalOutput"))
    cin = dict(glub=dt("glub", [128, 32], F32, "ExternalInput"))
    with ExitStack() as es:
        cst = _consts(nc, k, es, list(cin), cin)
        phase_F(nc, k, io, cst)
        k.barrier()
        phase_G(nc, k, io, cst)
        k.finish(['out'])
    return nc


PAIRS = [[0, 1], [2, 3], [4, 5], [6, 7]]


def build_fused():
    nc = bass.Bass("TRN2", target_bir_lowering=False)
    k = K(nc)
    dt = lambda n, s, d, kind="Internal": nc.dram_tensor(n, s, d, kind=kind).ap()
    EI = "ExternalInput"
    io = dict(x=dt("x", [NTOK, D], F32, EI), w_in=dt("w_in", [D, 4 * D], F32, EI), w_out=dt("w_out", [D, D], F32, EI),
              ssm_w_in=dt("ssm_w_in", [D, 2 * D], F32, EI), glu_w=dt("glu_w", [D, D], F32, EI),
              ssm_w_out=dt("ssm_w_out", [D, D], F32, EI), out=dt("out", [NTOK, D], F32, "ExternalOutput"),
              qT=dt("qT", [D, NTOK], BF16), kT=dt("kT", [D, NTOK], BF16), v=dt("v", [NTOK, D], BF16),
              sgT=dt("sgT", [D, NTOK], BF16), goT=dt("goT", [D, NTOK], BF16), x1=dt("x1", [NTOK, D], F32),
              uT=dt("uT", [D, NTOK], BF16), sg1T=dt("sg1T", [D, NTOK], BF16), mT=dt("mT", [D, NTOK], BF16))
    kT_g = dt("kT_g", [2 * D, NTOK], BF16)
    v_g = dt("v_g", [2 * NTOK, D], BF16)
    u_g = dt("u_g", [2 * D, NTOK], BF16)
    yg_l = dt("yg_l", [D, NTOK], BF16)
    yg_g = dt("yg_g", [2 * D, NTOK], BF16)
    NCH = 4
    k4 = kT_g.rearrange("(c r f) t -> c r f t", c=NCH, r=2)
    v4 = v_g.rearrange("(c r t) f -> c r t f", c=NCH, r=2)
    u4 = u_g.rearrange("(c r f) t -> c r f t", c=NCH, r=2)
    y4 = yg_g.rearrange("(c r f) t -> c r f t", c=NCH, r=2)
    io['kT_rows'] = lambda r, h: k4[h // 8, r, (h % 8) * 128:(h % 8 + 1) * 128, :]
    io['v_parts'] = lambda r, h: [(2 * c, 2, v4[c, r, :, h * 128:(h + 1) * 128]) for c in range(NCH)]
    io['ygT'] = yg_l.rearrange("(d c) t -> d c t", d=2)
    io['u_cands'] = lambda h, r, ct: u4[(h * 2048 + ct * 128) // 1024, r,
                                        (h * 2048 + ct * 128) % 1024:(h * 2048 + ct * 128) % 1024 + 128, :]
    io['yg_cands'] = [(lambda j, d=d: y4[d * 2 + j % 2, j // 2, :, :].rearrange("(kc p) t -> p kc t", p=128))
                      for d in range(2)]

    def gather(src, dst, in_key, out_key, scr):
        rows = src.shape[0] // NCH
        for c in range(NCH):
            k.collective("AllGather", PAIRS, src[c * rows:(c + 1) * rows, :], dst[c * 2 * rows:(c + 1) * 2 * rows, :],
                         in_key, out_key, scr)
    for n in ('Bp_re', 'Bp_im', 'Cp_re', 'Cp_im'):
        io[n] = dt(n, [128, 64, 128], F32, EI)
    io['Ddiag'] = dt('Ddiag', [128, 16, 128], F32, EI)
    for n in ('A_re_T', 'A_im_T', 'logdt_T'):
        io[n] = dt(n, [128, 64], F32, EI)
    cin = dict(gk0=dt("gk0", [128, 32], F32, EI), gk1=dt("gk1", [128, 32], F32, EI), qg=dt("qg", [128, 1], F32, EI),
               kg=dt("kg", [128, 1], F32, EI), ident=dt("ident", [128, 128], BF16, EI), ones=dt("ones", [128, 128], BF16, EI),
               eps=dt("eps", [128, 1], F32, EI), negm=dt("negm", [128, 256], F32, EI), zcol=dt("zcol", [128, 1], F32, EI),
               halfpi=dt("halfpi", [128, 1], F32, EI), glub=dt("glub", [128, 32], F32, EI), sel=dt("sel", [128, 2], F32, EI))
    iota_d = dt("iota", [128, 2048], F32, EI)
    with ExitStack() as es:
        cst = {n: load_const(nc, k, es, "c_" + n, cin[n], list(cin[n].shape), cin[n].dtype) for n in cin}
        scr = sb(nc, es, "cc_scr", [128, 8], F32)
        phase_A(nc, k, io, cst)
        k.barrier()
        gather(io['kT'], kT_g, 'kT', 'kT_all', scr[:])
        gather(io['v'], v_g, 'v', 'v_all', scr[:])
        phase_B(nc, k, io, cst)
        k.barrier()
        phase_C(nc, k, io, cst)
        k.barrier()
        phase_D(nc, k, io, cst)
        k.barrier()
        gather(io['uT'], u_g, 'uT', 'uT_all', scr[:])
        with ExitStack() as es2:
            cst_e = dict(cst)
            cst_e['iota'] = load_const(nc, k, es2, "c_iota", iota_d, [128, 2048], F32)
            phase_E(nc, k, io, cst_e)
            k.barrier()
        gather(yg_l, yg_g, 'ygT', 'yG', scr[:])
        phase_F(nc, k, io, cst)
        k.barrier()
        phase_G(nc, k, io, cst)
        k.finish(['out'])
    return nc


def _negmask(r):
    t = np.arange(128)[:, None]
    j = np.arange(128)[None, :]
    diag = np.where(j < t, 0.0, -1000.0)
    full = np.zeros((128, 128))
    none = np.full((128, 128), -1000.0)
    return np.ascontiguousarray(np.concatenate([diag, none] if r == 0 else [full, diag], axis=1).astype(np.float32))


def kernel_unfused(x, norm_g, attn_w_in, attn_q_g, attn_k_g, attn_w_out, ssm_w_in, ssm_A_re, ssm_A_im, ssm_log_dt,
           ssm_B_re, ssm_B_im, ssm_C_re, ssm_C_im, ssm_D, ssm_glu_w, ssm_glu_b, ssm_w_out):
    f32 = lambda a: np.ascontiguousarray(np.asarray(a, np.float32))
    x = f32(x)
    ncores = 8
    cores = list(range(ncores))
    bf = lambda a: np.asarray(a, np.float32).astype(NPBF)
    ident = bf(np.eye(128))
    ones = bf(np.ones((128, 128)))
    eps = np.full((128, 1), EPS, np.float32)
    zcol = np.zeros((128, 1), np.float32)
    xo = [np.ascontiguousarray(x[c // 2].reshape(8, 2, 128, D)[:, c % 2].reshape(NTOK, D)) for c in cores]
    norm_g = f32(norm_g)
    gk = [np.ascontiguousarray(norm_g[i].reshape(32, 128).T) for i in range(2)]
    qg = f32(attn_q_g)[0].reshape(128, 1)
    kg = f32(attn_k_g)[0].reshape(128, 1)
    w_in = f32(attn_w_in)[0]
    in1 = [dict(x=xo[c], w_in=w_in, gk0=gk[0], qg=qg, kg=kg, ident=ident, ones=ones, eps=eps) for c in cores]
    r1 = run_bass_kernel_spmd(build_launch1(), in1, core_ids=cores).results
    del in1, w_in
    w_out = f32(attn_w_out)[0]
    sw_in = f32(ssm_w_in)[0]
    in2 = []
    for c in cores:
        p = (c // 2) * 2
        kT_all = np.ascontiguousarray(np.stack([np.asarray(r1[p]['kT']), np.asarray(r1[p + 1]['kT'])]))
        v_all = np.ascontiguousarray(np.stack([np.asarray(r1[p]['v']), np.asarray(r1[p + 1]['v'])]))
        in2.append(dict(qT=np.asarray(r1[c]['qT']), kT_all=kT_all, v_all=v_all, sgT=np.asarray(r1[c]['sgT']),
                        x=xo[c], w_out=w_out, ssm_w_in=sw_in, ident=ident, negm=_negmask(c % 2), zcol=zcol,
                        eps=eps, gk1=gk[1]))
    r2 = run_bass_kernel_spmd(build_launch2(), in2, core_ids=cores).results
    del in2, r1, w_out, sw_in
    if os.environ.get('KDEBUG'):
        np.save(os.environ['KDEBUG'] + '_x1.npy', np.stack([np.asarray(r2[c]['x1']) for c in range(2)]))
        np.save(os.environ['KDEBUG'] + '_uT.npy', np.stack([np.asarray(r2[c]['uT']).astype(np.float32) for c in range(2)]))
    lay = [ssm_layouts(r, f32(ssm_A_re)[0], f32(ssm_A_im)[0], f32(ssm_log_dt)[0], f32(ssm_B_re)[0], f32(ssm_B_im)[0],
                       f32(ssm_C_re)[0], f32(ssm_C_im)[0], f32(ssm_D)[0]) for r in range(2)]
    iota = np.ascontiguousarray(np.broadcast_to(np.arange(2048, dtype=np.float32), (128, 2048)))
    halfpi = np.full((128, 1), np.pi / 2, np.float32)
    in3 = []
    for c in cores:
        p, r = (c // 2) * 2, c % 2
        uT_all = np.ascontiguousarray(np.stack([np.asarray(r2[p + rr]['uT'])[2048 * r:2048 * (r + 1)] for rr in range(2)]))
        in3.append(dict(uT_all=uT_all, iota=iota, halfpi=halfpi, zcol=zcol, **lay[r]))
    r3 = run_bass_kernel_spmd(build_launch3(), in3, core_ids=cores).results
    del in3
    glu_w = f32(ssm_glu_w)[0]
    sw_out = f32(ssm_w_out)[0]
    glub = np.ascontiguousarray(f32(ssm_glu_b)[0].reshape(32, 128).T)
    in4 = []
    for c in cores:
        p, r = (c // 2) * 2, c % 2
        yg_own = np.ascontiguousarray(np.concatenate([np.asarray(r3[p + rr]['ygT'])[r] for rr in range(2)], axis=0))
        in4.append(dict(ygT_own=yg_own, sg1T=np.asarray(r2[c]['sg1T']), glu_w=glu_w, ssm_w_out=sw_out,
                        x1=np.asarray(r2[c]['x1']), glub=glub))
    r4 = run_bass_kernel_spmd(build_launch4(), in4, core_ids=cores).results
    out = np.zeros((4, 2048, D), np.float32)
    for c in cores:
        b, r = c // 2, c % 2
        out[b].reshape(8, 2, 128, D)[:, r] = np.asarray(r4[c]['out'], np.float32).reshape(8, 128, D)
    return out


def phase_E(nc, k, io, cst, pairs=range(64)):
    iota, halfpi, zc = cst['iota'], cst['halfpi'], cst['zcol']
    pairs = list(pairs)
    cts = sorted(set(q // 4 for q in pairs))
    with ExitStack() as es:
        T = lambda name, shape, dt=F32: sb(nc, es, "E_" + name, shape, dt)
        Are = T("Are", [128, 64]); Aim = T("Aim", [128, 64]); dtt = T("dt", [128, 64])
        xx = T("xx", [128, 64]); tt = T("tt", [128, 64]); rho = T("rho", [128, 64]); rm1 = T("rm1", [128, 64])
        th = T("th", [128, 64]); thf = T("thf", [128, 64]); fa1 = T("fa1", [128, 64])
        s1 = T("s1", [128, 64]); c1 = T("c1", [128, 64]); sh = T("sh", [128, 64])
        am1 = T("am1", [128, 64]); abi = T("abi", [128, 64]); den = T("den", [128, 64])
        fre = T("fre", [128, 64]); fim = T("fim", [128, 64]); t2 = T("t2", [128, 64])
        k.dma('sp', [], ['E_Are'], out=Are[:], in_=io['A_re_T'], key='E_Are')
        k.dma('sp', [], ['E_Aim'], out=Aim[:], in_=io['A_im_T'], key='E_Aim')
        k.dma('sp', [], ['E_dt'], out=dtt[:], in_=io['logdt_T'], key='E_dt')
        nm_ = lambda r: r if r.startswith('c_') else "E_" + r
        V = lambda reads, writes, fn: k.op('dve', [nm_(r) for r in reads], [nm_(w) for w in writes], fn)
        A = lambda reads, writes, fn: k.op('act', [nm_(r) for r in reads], [nm_(w) for w in writes], fn)
        Pl = lambda reads, writes, fn: k.op('pool', [nm_(r) for r in reads], [nm_(w) for w in writes], fn)
        A(['dt'], ['dt'], lambda e: e.activation(out=dtt[:], in_=dtt[:], func=AF.Exp))
        V(['Are', 'dt'], ['xx'], lambda e: e.tensor_tensor(out=xx[:], in0=Are[:], in1=dtt[:], op=ALU.mult))
        V(['xx'], ['tt'], lambda e: e.tensor_scalar(out=tt[:], in0=xx[:], scalar1=1.0 / 24, scalar2=1.0 / 6, op0=ALU.mult, op1=ALU.add))
        V(['tt', 'xx'], ['tt'], lambda e: e.tensor_tensor(out=tt[:], in0=tt[:], in1=xx[:], op=ALU.mult))
        V(['tt'], ['tt'], lambda e: e.tensor_scalar(out=tt[:], in0=tt[:], scalar1=0.5, scalar2=None, op0=ALU.add))
        V(['tt', 'xx'], ['tt'], lambda e: e.tensor_tensor(out=tt[:], in0=tt[:], in1=xx[:], op=ALU.mult))
        V(['tt'], ['tt'], lambda e: e.tensor_scalar(out=tt[:], in0=tt[:], scalar1=1.0, scalar2=None, op0=ALU.add))
        V(['tt', 'xx'], ['rm1'], lambda e: e.tensor_tensor(out=rm1[:], in0=tt[:], in1=xx[:], op=ALU.mult))
        V(['rm1'], ['rho'], lambda e: e.tensor_scalar(out=rho[:], in0=rm1[:], scalar1=1.0, scalar2=None, op0=ALU.add))
        V(['Aim', 'dt'], ['th'], lambda e: e.tensor_tensor(out=th[:], in0=Aim[:], in1=dtt[:], op=ALU.mult))
        V(['th'], ['t2'], lambda e: e.tensor_scalar(out=t2[:], in0=th[:], scalar1=1.0 / TWO_PI, scalar2=MAGIC, op0=ALU.mult, op1=ALU.add))
        V(['t2'], ['t2'], lambda e: e.tensor_scalar(out=t2[:], in0=t2[:], scalar1=MAGIC, scalar2=None, op0=ALU.subtract))
        V(['th', 't2'], ['thf'], lambda e: e.scalar_tensor_tensor(out=thf[:], in0=th[:], scalar=1.0 / TWO_PI, in1=t2[:], op0=ALU.mult, op1=ALU.subtract))
        V(['thf'], ['fa1'], lambda e: e.scalar_tensor_tensor(out=fa1[:], in0=thf[:], scalar=-1.0, in1=thf[:], op0=ALU.mult, op1=ALU.max))
        A(['thf'], ['s1'], lambda e: e.activation(out=s1[:], in_=thf[:], func=AF.Sin, scale=TWO_PI))
        A(['fa1', 'c_halfpi'], ['c1'], lambda e: e.activation(out=c1[:], in_=fa1[:], func=AF.Sin, scale=-TWO_PI, bias=halfpi[:, 0:1]))
        A(['thf'], ['sh'], lambda e: e.activation(out=sh[:], in_=thf[:], func=AF.Sin, scale=TWO_PI / 2))
        V(['sh'], ['sh'], lambda e: e.tensor_tensor(out=sh[:], in0=sh[:], in1=sh[:], op=ALU.mult))
        V(['rm1', 'c1'], ['am1'], lambda e: e.tensor_tensor(out=am1[:], in0=rm1[:], in1=c1[:], op=ALU.mult))
        V(['sh', 'am1'], ['am1'], lambda e: e.scalar_tensor_tensor(out=am1[:], in0=sh[:], scalar=-2.0, in1=am1[:], op0=ALU.mult, op1=ALU.add))
        V(['rho', 's1'], ['abi'], lambda e: e.tensor_tensor(out=abi[:], in0=rho[:], in1=s1[:], op=ALU.mult))
        V(['Are'], ['den'], lambda e: e.tensor_tensor(out=den[:], in0=Are[:], in1=Are[:], op=ALU.mult))
        V(['Aim'], ['t2'], lambda e: e.tensor_tensor(out=t2[:], in0=Aim[:], in1=Aim[:], op=ALU.mult))
        V(['den', 't2'], ['den'], lambda e: e.tensor_tensor(out=den[:], in0=den[:], in1=t2[:], op=ALU.add))
        V(['den'], ['den'], lambda e: e.reciprocal(out=den[:], in_=den[:]))
        V(['am1', 'Are'], ['fre'], lambda e: e.tensor_tensor(out=fre[:], in0=am1[:], in1=Are[:], op=ALU.mult))
        V(['abi', 'Aim'], ['t2'], lambda e: e.tensor_tensor(out=t2[:], in0=abi[:], in1=Aim[:], op=ALU.mult))
        V(['fre', 't2'], ['fre'], lambda e: e.tensor_tensor(out=fre[:], in0=fre[:], in1=t2[:], op=ALU.add))
        V(['fre', 'den'], ['fre'], lambda e: e.tensor_tensor(out=fre[:], in0=fre[:], in1=den[:], op=ALU.mult))
        V(['abi', 'Are'], ['fim'], lambda e: e.tensor_tensor(out=fim[:], in0=abi[:], in1=Are[:], op=ALU.mult))
        V(['am1', 'Aim'], ['t2'], lambda e: e.tensor_tensor(out=t2[:], in0=am1[:], in1=Aim[:], op=ALU.mult))
        V(['fim', 't2'], ['fim'], lambda e: e.tensor_tensor(out=fim[:], in0=fim[:], in1=t2[:], op=ALU.subtract))
        V(['fim', 'den'], ['fim'], lambda e: e.tensor_tensor(out=fim[:], in0=fim[:], in1=den[:], op=ALU.mult))

        ut = [T(f"ut{i}", [128, 8, 2, 128], BF16) for i in range(2)]
        if 'u_cands' in io:
            ucand = [[T(f"uc{h}", [128, 8, 2, 128], BF16)] * 2 for h in range(2)]
        wstg = [T("wstg", [128, 4, 4, 128])] * 2
        dstg = [T("dstg", [128, 128])] * 2
        TA = T("ta", [128, 2048]); TB = T("tb", [128, 2048])
        bw = [T(f"bw{i}", [128, 2, 4, 128], BF16) for i in range(2)]
        cw = [T(f"cw{i}", [128, 2, 4, 128], BF16) for i in range(2)]
        dw = [T(f"dw{i}", [128, 128], BF16) for i in range(2)]
        ctmp = T("ctmp", [128, 2, 4, 128])
        cosT = [T(f"cos{i}", [128, 2048]) for i in range(2)]
        sinT = [T(f"sin{i}", [128, 2048]) for i in range(2)]
        fa = T("fa", [128, 2048]); fr = T("fr", [128, 2048])
        G = [T("Gre", [128, 2048]), T("Gim", [128, 2048])]
        Rb = [[T(f"Rre{i}", [128, 2048]), T(f"Rim{i}", [128, 2048])] for i in range(2)]
        mg = T("mg", [128, 2])
        V([], ['mg'], lambda e: e.memset(mg[:, 0:1], MAGIC))
        V(['mg'], ['mg'], lambda e: e.memset(mg[:, 1:2], -MAGIC))
        H = [T(f"H{i}", [128, 4, 2048], BF16) for i in range(2)]
        yg = [T(f"yg{i}", [128, 8, 2, 128], BF16) for i in range(2)]
        psb = [ps(nc, es, f"E_psb{i}", [128, 1024]) for i in range(2)]
        yps = ps(nc, es, "E_yps", [128, 2048])

        def load_ct(ci):
            ct = cts[ci]
            j = ci % 2
            if 'u_cands' in io:
                for h in range(2):
                    for r in range(2):
                        k.dma('sp', ['uT_all'], [f"E_uc{h}"], out=ucand[h][j][:, :, r, :],
                              in_=io['u_cands'](h, r, ct).rearrange("c (m i) -> c m i", i=128), key=f"E_uc{h}")
                k.op('act', ["E_uc0", 'c_sel'], [f"E_ut{j}"],
                     lambda e: e.activation(out=ut[j][:], in_=ucand[0][j][:], func=AF.Copy, scale=cst['sel'][:, 0:1]))
                k.op('act', ["E_uc1", 'c_sel'], ["E_uc1"],
                     lambda e: e.activation(out=ucand[1][j][:], in_=ucand[1][j][:], func=AF.Copy, scale=cst['sel'][:, 1:2]))
                k.op('dve', ["E_uc1", f"E_ut{j}"], [f"E_ut{j}"],
                     lambda e: e.tensor_tensor(out=ut[j][:], in0=ut[j][:], in1=ucand[1][j][:], op=ALU.add))
            else:
                for r in range(2):
                    k.dma('sp', ['uT_all'], [f"E_ut{j}"], out=ut[j][:, :, r, :],
                          in_=io['uT_all'][r, ct * 128:(ct + 1) * 128, :].rearrange("c (m i) -> c m i", i=128), key=f"E_ut{j}")
            for w, nm in enumerate(('Bp_re', 'Bp_im', 'Cp_re', 'Cp_im')):
                k.dma('sp', [], ["E_wstg"], out=wstg[j][:, w, :, :], in_=io[nm][:, ct * 4:(ct + 1) * 4, :], key="E_wstg")
            k.dma('sp', [], ["E_dstg"], out=dstg[j][:], in_=io['Ddiag'][:, ct, :], key="E_dstg")
            prep_ct(ci)

        def prep_ct(ci):
            ct = cts[ci]
            j = ci % 2
            k.op('act', ["E_wstg"], [f"E_bw{j}"], lambda e: e.activation(out=bw[j][:], in_=wstg[j][:, 0:2, :, :], func=AF.Copy))
            k.op('act', ["E_dstg"], [f"E_dw{j}"], lambda e: e.activation(out=dw[j][:], in_=dstg[j][:], func=AF.Copy))
            Cre, Cim = wstg[j][:, 2, :, :], wstg[j][:, 3, :, :]
            for b in range(4):
                q = ct * 4 + b
                k.op('act', ["E_wstg", 'E_fre'], ['E_ctmp'], lambda e: e.activation(out=ctmp[:, 0, b, :], in_=Cre[:, b, :], func=AF.Copy, scale=fre[:, q:q + 1]))
                k.op('act', ["E_wstg", 'E_fim'], ['E_ctmp'], lambda e: e.activation(out=ctmp[:, 1, b, :], in_=Cim[:, b, :], func=AF.Copy, scale=fim[:, q:q + 1]))
            k.op('dve', ['E_ctmp'], [f"E_cw{j}"], lambda e: e.tensor_tensor(out=cw[j][:, 0, :, :], in0=ctmp[:, 0, :, :], in1=ctmp[:, 1, :, :], op=ALU.subtract))
            for b in range(4):
                q = ct * 4 + b
                k.op('act', ["E_wstg", 'E_fim'], ['E_ctmp'], lambda e: e.activation(out=ctmp[:, 0, b, :], in_=Cre[:, b, :], func=AF.Copy, scale=fim[:, q:q + 1]))
                k.op('act', ["E_wstg", 'E_fre'], ['E_ctmp'], lambda e: e.activation(out=ctmp[:, 1, b, :], in_=Cim[:, b, :], func=AF.Copy, scale=fre[:, q:q + 1]))
            k.op('dve', ['E_ctmp'], [f"E_cw{j}"],
                 lambda e: e.scalar_tensor_tensor(out=cw[j][:, 1, :, :], in0=ctmp[:, 0, :, :], scalar=-1.0, in1=ctmp[:, 1, :, :],
                                                  op0=ALU.mult, op1=ALU.subtract))

        ci_of = {ct: ci for ci, ct in enumerate(cts)}

        def tables_a(qi):
            q = pairs[qi]
            thq = thf[:, q:q + 1]
            A(['thf', 'c_iota', 'mg'], ['fa'], lambda e: e.activation(out=fa[:], in_=iota[:], func=AF.Identity, scale=thq, bias=mg[:, 0:1]))
            A(['fa', 'mg'], ['fa'], lambda e: e.activation(out=fa[:], in_=fa[:], func=AF.Identity, bias=mg[:, 1:2]))

        def tables_b(qi):
            q = pairs[qi]
            tb = qi % 2
            thq = thf[:, q:q + 1]
            V(['fa', 'thf', 'c_iota'], ['fr'], lambda e: e.scalar_tensor_tensor(out=fr[:], in0=iota[:], scalar=thq, in1=fa[:], op0=ALU.mult, op1=ALU.subtract))
            A(['fr'], [f"sin{tb}"], lambda e: e.activation(out=sinT[tb][:], in_=fr[:], func=AF.Sin, scale=TWO_PI))
            A(['fr'], ['fa'], lambda e: e.activation(out=fa[:], in_=fr[:], func=AF.Abs))
            A(['fa', 'c_halfpi'], [f"cos{tb}"], lambda e: e.activation(out=cosT[tb][:], in_=fa[:], func=AF.Sin, scale=-TWO_PI, bias=halfpi[:, 0:1]))

        def open_ct(ct):
            ci = ci_of[ct]
            j = ci % 2
            uflat = ut[j][:].rearrange("p m r i -> p (m r i)")
            k.pe([f"E_dw{j}", f"E_ut{j}"], ['E_yps'],
                 [lambda e, sg=sg: e.matmul(yps[:, sg * 512:(sg + 1) * 512], lhsT=dw[j][:], rhs=uflat[:, sg * 512:(sg + 1) * 512],
                                            start=True, stop=False) for sg in range(4)])

        def stage_in(qi):
            q = pairs[qi]
            ct, b, tb = q // 4, q % 4, qi % 2
            j = ci_of[ct] % 2
            uflat = ut[j][:].rearrange("p m r i -> p (m r i)")
            cT, sT = cosT[tb], sinT[tb]
            for sg in range(4):
                sl = slice(sg * 512, (sg + 1) * 512)
                s_ = sg % 2
                k.pe([f"E_bw{j}", f"E_ut{j}"], [f"E_psb{s_}"],
                     [lambda e: e.matmul(psb[s_][:, 0:512], lhsT=bw[j][:, 0, b, :], rhs=uflat[:, sl], start=True, stop=True),
                      lambda e: e.matmul(psb[s_][:, 512:1024], lhsT=bw[j][:, 1, b, :], rhs=uflat[:, sl], start=True, stop=True)])
                A([f"psb{s_}"], ['Gre'], lambda e: e.activation(out=G[0][:, sl], in_=psb[s_][:, 0:512], func=AF.Copy))
                A([f"psb{s_}"], ['Gim'], lambda e: e.activation(out=G[1][:, sl], in_=psb[s_][:, 512:1024], func=AF.Copy))

        def stage_rot(qi):
            q = pairs[qi]
            tb = qi % 2
            cT, sT = cosT[tb], sinT[tb]
            V([f"sin{tb}", 'Gim'], ['ta'], lambda e: e.tensor_tensor(out=TA[:], in0=G[1][:], in1=sT[:], op=ALU.mult))
            V([f"sin{tb}", 'Gre'], ['tb'], lambda e: e.tensor_tensor(out=TB[:], in0=G[0][:], in1=sT[:], op=ALU.mult))
            V([f"cos{tb}", 'Gre'], ['Gre'], lambda e: e.tensor_tensor(out=G[0][:], in0=G[0][:], in1=cT[:], op=ALU.mult))
            V([f"cos{tb}", 'Gim'], ['Gim'], lambda e: e.tensor_tensor(out=G[1][:], in0=G[1][:], in1=cT[:], op=ALU.mult))
            V(['Gre', 'ta'], ['Gre'], lambda e: e.tensor_tensor(out=G[0][:], in0=G[0][:], in1=TA[:], op=ALU.add))
            V(['Gim', 'tb'], ['Gim'], lambda e: e.tensor_tensor(out=G[1][:], in0=G[1][:], in1=TB[:], op=ALU.subtract))

        def stage_scan(qi):
            q = pairs[qi]
            rq = rho[:, q:q + 1].to_broadcast([128, 2048])
            R = Rb[qi % 2]
            for c in range(2):
                V(['rho', ('Gre', 'Gim')[c]], [(f"Rre{qi % 2}", f"Rim{qi % 2}")[c]],
                  lambda e: e.tensor_tensor_scan(out=R[c][:], data0=rq, data1=G[c][:], initial=0.0, op0=ALU.mult, op1=ALU.add))

        def stage_in_b(qi):
            tb, hb = qi % 2, qi % 2
            cT, sT = cosT[tb], sinT[tb]
            R = Rb[qi % 2]
            rre, rim = f"Rre{qi % 2}", f"Rim{qi % 2}"
            V([f"cos{tb}", rre], [f"H{hb}"], lambda e: e.tensor_tensor(out=H[hb][:, 0, :], in0=R[0][:], in1=cT[:], op=ALU.mult))
            V([f"sin{tb}", rim], [f"H{hb}"], lambda e: e.scalar_tensor_tensor(out=H[hb][:, 1, :], in0=R[1][:], scalar=-1.0, in1=sT[:], op0=ALU.mult, op1=ALU.mult))
            V([f"cos{tb}", rim], [f"H{hb}"], lambda e: e.tensor_tensor(out=H[hb][:, 2, :], in0=R[1][:], in1=cT[:], op=ALU.mult))
            V([f"sin{tb}", rre], [f"H{hb}"], lambda e: e.tensor_tensor(out=H[hb][:, 3, :], in0=R[0][:], in1=sT[:], op=ALU.mult))

        def stage_out(qi):
            q = pairs[qi]
            ct, b, hb = q // 4, q % 4, qi % 2
            ci = ci_of[ct]
            j = ci % 2
            myp = [qq for qq in pairs if qq // 4 == ct]
            if q == myp[0]:
                open_ct(ct)
            last = (q == myp[-1])
            fns = []
            for sg in range(4):
                sl = slice(sg * 512, (sg + 1) * 512)
                fns.append(lambda e, sl=sl: e.matmul(yps[:, sl], lhsT=cw[j][:, 0, b, :], rhs=H[hb][:, 0, sl], start=False, stop=False))
                fns.append(lambda e, sl=sl: e.matmul(yps[:, sl], lhsT=cw[j][:, 0, b, :], rhs=H[hb][:, 1, sl], start=False, stop=False))
                fns.append(lambda e, sl=sl: e.matmul(yps[:, sl], lhsT=cw[j][:, 1, b, :], rhs=H[hb][:, 2, sl], start=False, stop=False))
                fns.append(lambda e, sl=sl: e.matmul(yps[:, sl], lhsT=cw[j][:, 1, b, :], rhs=H[hb][:, 3, sl], start=False, stop=last))
            k.pe([f"E_cw{j}", f"E_H{hb}"], ['E_yps'], fns)
            if last:
                ygf = yg[j][:].rearrange("p m r i -> p (m r i)")
                for sg in range(4):
                    sl = slice(sg * 512, (sg + 1) * 512)
                    A(['yps'], ['ta'], lambda e: e.activation(out=TA[:, sl], in_=yps[:, sl], func=AF.Copy))
                    A(['yps'], ['tb'], lambda e: e.activation(out=TB[:, sl], in_=yps[:, sl], func=AF.Square))
                V(['tb'], ['tb'], lambda e: e.tensor_scalar(out=TB[:], in0=TB[:], scalar1=0.044715, scalar2=1.0, op0=ALU.mult, op1=ALU.add))
                V(['ta', 'tb'], ['tb'], lambda e: e.tensor_tensor(out=TB[:], in0=TB[:], in1=TA[:], op=ALU.mult))
                A(['tb'], ['tb'], lambda e: e.activation(out=TB[:], in_=TB[:], func=AF.Sigmoid, scale=1.5957691216057308))
                V(['ta', 'tb'], [f"yg{j}"], lambda e: e.tensor_tensor(out=ygf, in0=TA[:], in1=TB[:], op=ALU.mult))
                for r in range(2):
                    k.dma('act', [f"E_yg{j}"], ['ygT'], out=io['ygT'][r, ct * 128:(ct + 1) * 128, :].rearrange("c (m i) -> c m i", i=128),
                          in_=yg[j][:, :, r, :], key=f"E_yg{j}")

        load_ct(0)
        if len(cts) > 1:
            load_ct(1)
        tables_a(0)
        tables_b(0)
        npairs = len(pairs)
        for qi in range(npairs):
            q = pairs[qi]
            stage_in(qi)
            if qi + 1 < npairs:
                tables_a(qi + 1)
            stage_rot(qi)
            if qi + 1 < npairs:
                tables_b(qi + 1)
            stage_scan(qi)
            stage_in_b(qi)
            if qi >= 1:
                stage_out(qi - 1)
                qp = pairs[qi - 1]
                if qp // 4 != q // 4 and ci_of[qp // 4] + 2 < len(cts):
                    load_ct(ci_of[qp // 4] + 2)
        stage_out(npairs - 1)


def ssm_layouts(r, A_re, A_im, log_dt, B_re, B_im, C_re, C_im, Dv):
    g0 = 128 * r
    Bp = [np.zeros((128, 64, 128), np.float32) for _ in range(2)]
    Cp = [np.zeros((128, 64, 128), np.float32) for _ in range(2)]
    for q in range(64):
        b = q % 4
        for gg in range(2):
            g = g0 + 2 * q + gg
            rows = slice(32 * b + 16 * gg, 32 * b + 16 * gg + 16)
            cols = slice(64 * gg, 64 * gg + 64)
            Bp[0][rows, q, cols] = B_re[g].T
            Bp[1][rows, q, cols] = B_im[g].T
            Cp[0][cols, q, rows] = C_re[g].T
            Cp[1][cols, q, rows] = C_im[g].T
    Dd = np.zeros((128, 16, 128), np.float32)
    for ct in range(16):
        Dd[np.arange(128), ct, np.arange(128)] = Dv[2048 * r + 128 * ct: 2048 * r + 128 * (ct + 1)]
    tr = lambda a: np.ascontiguousarray(a[g0:g0 + 128].reshape(64, 2, 64).transpose(1, 2, 0).reshape(128, 64))
    ldt = np.ascontiguousarray(np.broadcast_to(log_dt[g0:g0 + 128].reshape(64, 2, 1), (64, 2, 64)).transpose(1, 2, 0).reshape(128, 64))
    return dict(Bp_re=Bp[0], Bp_im=Bp[1], Cp_re=Cp[0], Cp_im=Cp[1], Ddiag=Dd, A_re_T=tr(A_re), A_im_T=tr(A_im),
                logdt_T=ldt.astype(np.float32))


def kernel(x, norm_g, attn_w_in, attn_q_g, attn_k_g, attn_w_out, ssm_w_in, ssm_A_re, ssm_A_im, ssm_log_dt,
           ssm_B_re, ssm_B_im, ssm_C_re, ssm_C_im, ssm_D, ssm_glu_w, ssm_glu_b, ssm_w_out):
    f32 = lambda a: np.ascontiguousarray(np.asarray(a, np.float32))
    x = f32(x)
    cores = list(range(8))
    bf = lambda a: np.asarray(a, np.float32).astype(NPBF)
    norm_g = f32(norm_g)
    lay = [ssm_layouts(r, f32(ssm_A_re)[0], f32(ssm_A_im)[0], f32(ssm_log_dt)[0], f32(ssm_B_re)[0], f32(ssm_B_im)[0],
                       f32(ssm_C_re)[0], f32(ssm_C_im)[0], f32(ssm_D)[0]) for r in range(2)]
    shared = dict(
        w_in=f32(attn_w_in)[0], w_out=f32(attn_w_out)[0], ssm_w_in=f32(ssm_w_in)[0], glu_w=f32(ssm_glu_w)[0],
        ssm_w_out=f32(ssm_w_out)[0],
        gk0=np.ascontiguousarray(norm_g[0].reshape(32, 128).T), gk1=np.ascontiguousarray(norm_g[1].reshape(32, 128).T),
        qg=f32(attn_q_g)[0].reshape(128, 1), kg=f32(attn_k_g)[0].reshape(128, 1),
        ident=bf(np.eye(128)), ones=bf(np.ones((128, 128))), eps=np.full((128, 1), EPS, np.float32),
        zcol=np.zeros((128, 1), np.float32), halfpi=np.full((128, 1), np.pi / 2, np.float32),
        glub=np.ascontiguousarray(f32(ssm_glu_b)[0].reshape(32, 128).T),
        iota=np.ascontiguousarray(np.broadcast_to(np.arange(2048, dtype=np.float32), (128, 2048))))
    in_maps = []
    for c in cores:
        b, r = c // 2, c % 2
        sel = np.zeros((128, 2), np.float32)
        sel[:, r] = 1.0
        m = dict(shared)
        m.update(x=np.ascontiguousarray(x[b].reshape(8, 2, 128, D)[:, r].reshape(NTOK, D)), negm=_negmask(r), sel=sel, **lay[r])
        in_maps.append(m)
    res = run_bass_kernel_spmd(build_fused(), in_maps, core_ids=cores).results
    out = np.zeros((4, 2048, D), np.float32)
    for c in cores:
        b, r = c // 2, c % 2
        out[b].reshape(8, 2, 128, D)[:, r] = np.asarray(res[c]['out'], np.float32).reshape(8, 128, D)
    return out
```
